# Optimizing a Trainium2 kernel written in Bass

```python
import jax, jax.numpy as jnp
from jax import lax
import numpy as np

D_MODEL = 1024
BATCH = 4
SEQ = 4096
DEPTH = 2

CTX_LEN = 256
GRID_W = 64
N_Q_HEADS = 8
N_KV_HEADS = 2
HEAD_DIM = 64
Q_PER_KV = N_Q_HEADS // N_KV_HEADS
ATTN_WIDTH = N_Q_HEADS * HEAD_DIM
KV_WIDTH = N_KV_HEADS * HEAD_DIM
WINDOW = 128
BLOCK = 128
ROPE_BASE = 10000.0
POOL_GROUPS = 4
POOL_GROUP_DIM = 64
POOL_WIDTH = POOL_GROUPS * POOL_GROUP_DIM
POOL_WINDOWS = (2, 4, 8, 16)
FOURIER_GROUPS = 4
FOURIER_GROUP_DIM = 64
FOURIER_WIDTH = FOURIER_GROUPS * FOURIER_GROUP_DIM
N_BRANCHES = 3
IN_SPLITS = (KV_WIDTH, 2 * KV_WIDTH, 2 * KV_WIDTH + ATTN_WIDTH,
             2 * KV_WIDTH + ATTN_WIDTH + POOL_WIDTH,
             2 * KV_WIDTH + ATTN_WIDTH + POOL_WIDTH + FOURIER_WIDTH)
IN_WIDTH = 2 * KV_WIDTH + ATTN_WIDTH + POOL_WIDTH + FOURIER_WIDTH + N_BRANCHES * D_MODEL
D_FF = 2816
CONV_WIDTH = 3
N_MOD = 6
EPS = 1e-6
NEG_INF = -1e30

kernel_name = "hybrid_gated_parallel_dit_block"


def rms_norm(x, g):
    xf = x.astype(jnp.float32)
    y = xf * lax.rsqrt(jnp.mean(xf * xf, axis=-1, keepdims=True) + EPS)
    return (y * g.astype(jnp.float32)).astype(x.dtype)


def adaln_params(cond, w_mod, b_mod):
    m = jax.nn.silu(cond) @ w_mod + b_mod
    return jnp.split(m[..., None, :], N_MOD, axis=-1)


def modulate(h, shift, scale):
    return h * (1 + scale) + shift


def axial_rope_tables(n_tokens):
    rows = n_tokens // GRID_W
    row = jnp.repeat(jnp.arange(rows, dtype=jnp.int32), GRID_W).astype(jnp.float32)
    col = jnp.tile(jnp.arange(GRID_W, dtype=jnp.int32), rows).astype(jnp.float32)
    n_freq = HEAD_DIM // 4
    inv_freq = ROPE_BASE ** (-jnp.arange(n_freq, dtype=jnp.float32) / n_freq)
    ang = jnp.stack([row[:, None] * inv_freq[None, :], col[:, None] * inv_freq[None, :]], axis=1)
    return jnp.cos(ang), jnp.sin(ang)


def apply_rope(x, cos, sin):
    B, T, H, _ = x.shape
    xr = x.astype(jnp.float32).reshape(B, T, H, 2, 2, HEAD_DIM // 4)
    c = cos[None, :, None, :, :]
    s = sin[None, :, None, :, :]
    x1, x2 = xr[..., 0, :], xr[..., 1, :]
    out = jnp.stack([x1 * c - x2 * s, x2 * c + x1 * s], axis=-2)
    return out.reshape(x.shape).astype(x.dtype)


def sink_column(sink, lead_shape):
    s = sink.astype(jnp.float32).reshape((1, N_KV_HEADS, Q_PER_KV) + (1,) * (len(lead_shape) - 2))
    return jnp.broadcast_to(s, tuple(lead_shape) + (1,))


def latent_attention(q, k, v, kc, vc, sink):
    B, T = q.shape[:2]
    nb = T // BLOCK
    ctx_len = kc.shape[1]
    scale = HEAD_DIM ** -0.5
    qb = q.reshape(B, nb, BLOCK, N_KV_HEADS, Q_PER_KV, HEAD_DIM)

    def band(a):
        a = a.reshape(B, nb, BLOCK, N_KV_HEADS, HEAD_DIM)
        pad = jnp.zeros_like(a[:, :1])
        ap = jnp.concatenate([pad, a, pad], axis=1)
        return jnp.concatenate([ap[:, :-2], ap[:, 1:-1], ap[:, 2:]], axis=2)

    kb, vb = band(k), band(v)
    s_lat = jnp.einsum('bnqhgd,bnkhd->bhgnqk', qb, kb).astype(jnp.float32) * scale
    blk = jnp.arange(nb, dtype=jnp.int32)[:, None]
    qpos = blk * BLOCK + jnp.arange(BLOCK, dtype=jnp.int32)[None, :]
    kpos = (blk - 1) * BLOCK + jnp.arange(3 * BLOCK, dtype=jnp.int32)[None, :]
    rel = kpos[:, None, :] - qpos[:, :, None]
    valid = (jnp.abs(rel) <= WINDOW) & (kpos[:, None, :] >= 0) & (kpos[:, None, :] < T)
    s_lat = jnp.where(valid, s_lat, NEG_INF)
    s_ctx = jnp.einsum('bnqhgd,bchd->bhgnqc', qb, kc).astype(jnp.float32) * scale
    s_all = jnp.concatenate([s_lat, s_ctx, sink_column(sink, s_lat.shape[:-1])], axis=-1)
    p = jax.nn.softmax(s_all, axis=-1)
    p_lat = p[..., :3 * BLOCK].astype(v.dtype)
    p_ctx = p[..., 3 * BLOCK:3 * BLOCK + ctx_len].astype(v.dtype)
    o = (jnp.einsum('bhgnqk,bnkhd->bnqhgd', p_lat, vb)
         + jnp.einsum('bhgnqc,bchd->bnqhgd', p_ctx, vc))
    return o.reshape(B, T, ATTN_WIDTH)


def context_attention(qc, kc, vc, sink):
    B, L = qc.shape[:2]
    scale = HEAD_DIM ** -0.5
    qg = qc.reshape(B, L, N_KV_HEADS, Q_PER_KV, HEAD_DIM)
    s = jnp.einsum('bqhgd,bkhd->bhgqk', qg, kc).astype(jnp.float32) * scale
    s_all = jnp.concatenate([s, sink_column(sink, s.shape[:-1])], axis=-1)
    p = jax.nn.softmax(s_all, axis=-1)[..., :L].astype(vc.dtype)
    o = jnp.einsum('bhgqk,bkhd->bqhgd', p, vc)
    return o.reshape(B, L, ATTN_WIDTH)


def multiscale_pool(u, pool_w, pool_scale):
    B, T = u.shape[:2]
    ug = u.reshape(B, T, POOL_GROUPS, POOL_GROUP_DIM).astype(jnp.float32)
    cs = jnp.concatenate([jnp.zeros_like(ug[:, :1]), jnp.cumsum(ug, axis=1)], axis=1)
    w = jnp.array(POOL_WINDOWS, dtype=jnp.int32)[None, :]
    t = jnp.arange(T, dtype=jnp.int32)[:, None]
    lo = jnp.clip(t - w // 2, 0, T)
    hi = jnp.clip(t - w // 2 + w, 0, T)
    gidx = jnp.arange(POOL_GROUPS, dtype=jnp.int32)[None, :]
    win_sum = cs[:, hi, gidx] - cs[:, lo, gidx]
    cnt = (hi - lo).astype(jnp.float32)[None, :, :, None]
    pooled = (win_sum / cnt - ug).astype(u.dtype)
    y = jnp.einsum('btgc,gcd->btgd', pooled, pool_w).reshape(B, T, POOL_WIDTH)
    return y * pool_scale


def fourier_mix(f):
    B, T = f.shape[:2]
    fg = f.reshape(B, T, FOURIER_GROUPS, FOURIER_GROUP_DIM).astype(jnp.float32)
    y = jnp.fft.fft2(fg, axes=(1, 3), norm='ortho').real
    return y.astype(f.dtype).reshape(B, T, FOURIER_WIDTH)


def merge_branches(o_attn, pool_in, four_in, gate_in, pool_w, pool_scale,
                   w_br_attn, w_br_pool, w_br_four, w_out):
    g_a, g_p, g_f = jnp.split(jax.nn.sigmoid(gate_in), N_BRANCHES, axis=-1)
    y = (g_a * (o_attn @ w_br_attn)
         + g_p * (multiscale_pool(pool_in, pool_w, pool_scale) @ w_br_pool)
         + g_f * (fourier_mix(four_in) @ w_br_four))
    return y @ w_out


def dwconv_centred(u, w):
    T = u.shape[1]
    up = jnp.pad(u, ((0, 0), (1, 1), (0, 0)))
    return up[:, :T] * w[0] + up[:, 1:T + 1] * w[1] + up[:, 2:] * w[2]


def conv_ffn(h, w_up, conv_w, w_down):
    a = dwconv_centred(h @ w_up, conv_w)
    val, gate = jnp.split(a, 2, axis=-1)
    return (val * jax.nn.silu(gate)) @ w_down


def setup_inputs(seed: int = 0) -> dict:
    key = jax.random.key(seed)
    ks = jax.random.split(key, 24)
    f32 = jnp.float32
    n = lambda k, shape, s: jax.random.normal(k, shape, f32) * s
    return {
        'x': n(ks[0], (BATCH, SEQ, D_MODEL), 1.0),
        'c': n(ks[1], (BATCH, D_MODEL), 1.0),
        'ctx': n(ks[2], (BATCH, CTX_LEN, D_MODEL), 1.0),
        'c_ctx': n(ks[3], (D_MODEL,), 1.0),
        'w_mod': n(ks[4], (DEPTH, D_MODEL, N_MOD * D_MODEL), 0.5 * D_MODEL ** -0.5),
        'b_mod': n(ks[5], (DEPTH, N_MOD * D_MODEL), 0.02),
        'norm_mix': 1.0 + n(ks[6], (DEPTH, D_MODEL), 0.02),
        'norm_ffn': 1.0 + n(ks[7], (DEPTH, D_MODEL), 0.02),
        'w_in': n(ks[8], (DEPTH, D_MODEL, IN_WIDTH), D_MODEL ** -0.5),
        'attn_sink': n(ks[9], (DEPTH, N_Q_HEADS), 0.5),
        'pool_w': n(ks[10], (DEPTH, POOL_GROUPS, POOL_GROUP_DIM, POOL_GROUP_DIM), POOL_GROUP_DIM ** -0.5),
        'pool_scale': 1.0 + n(ks[11], (DEPTH, POOL_WIDTH), 0.02),
        'w_br_attn': n(ks[12], (DEPTH, ATTN_WIDTH, D_MODEL), ATTN_WIDTH ** -0.5),
        'w_br_pool': n(ks[13], (DEPTH, POOL_WIDTH, D_MODEL), POOL_WIDTH ** -0.5),
        'w_br_four': n(ks[14], (DEPTH, FOURIER_WIDTH, D_MODEL), FOURIER_WIDTH ** -0.5),
        'w_out': n(ks[15], (DEPTH, D_MODEL, D_MODEL), D_MODEL ** -0.5),
        'w_up': n(ks[16], (DEPTH, D_MODEL, 2 * D_FF), D_MODEL ** -0.5),
        'conv_w': n(ks[17], (DEPTH, CONV_WIDTH, 2 * D_FF), CONV_WIDTH ** -0.5),
        'w_down': n(ks[18], (DEPTH, D_FF, D_MODEL), D_FF ** -0.5),
        'norm_final': 1.0 + n(ks[19], (D_MODEL,), 0.02),
    }


def reference(x, c, ctx, c_ctx, w_mod, b_mod, norm_mix, norm_ffn, w_in, attn_sink, pool_w, pool_scale,
              w_br_attn, w_br_pool, w_br_four, w_out, w_up, conv_w, w_down, norm_final):
    B, T, _ = x.shape
    cos, sin = axial_rope_tables(T)
    xc = ctx
    for l in range(DEPTH):
        last = l == DEPTH - 1
        sh1, sc1, g1, sh2, sc2, g2 = adaln_params(c, w_mod[l], b_mod[l])
        csh1, csc1, cg1, csh2, csc2, cg2 = adaln_params(c_ctx, w_mod[l], b_mod[l])

        h = modulate(rms_norm(x, norm_mix[l]), sh1, sc1)
        hc = modulate(rms_norm(xc, norm_mix[l]), csh1, csc1)
        k, v, q, u, f, gt = jnp.split(h @ w_in[l], IN_SPLITS, axis=-1)
        if last:
            kc, vc = jnp.split(hc @ w_in[l][:, :2 * KV_WIDTH], 2, axis=-1)
        else:
            kc, vc, qc, uc, fc, gtc = jnp.split(hc @ w_in[l], IN_SPLITS, axis=-1)
        L = hc.shape[1]
        q = apply_rope(q.reshape(B, T, N_Q_HEADS, HEAD_DIM), cos, sin)
        k = apply_rope(k.reshape(B, T, N_KV_HEADS, HEAD_DIM), cos, sin)
        v = v.reshape(B, T, N_KV_HEADS, HEAD_DIM)
        kc = kc.reshape(B, L, N_KV_HEADS, HEAD_DIM)
        vc = vc.reshape(B, L, N_KV_HEADS, HEAD_DIM)
        o = latent_attention(q, k, v, kc, vc, attn_sink[l])
        x = x + g1 * merge_branches(o, u, f, gt, pool_w[l], pool_scale[l],
                                    w_br_attn[l], w_br_pool[l], w_br_four[l], w_out[l])
        if not last:
            oc = context_attention(qc.reshape(B, L, N_Q_HEADS, HEAD_DIM), kc, vc, attn_sink[l])
            xc = xc + cg1 * merge_branches(oc, uc, fc, gtc, pool_w[l], pool_scale[l],
                                           w_br_attn[l], w_br_pool[l], w_br_four[l], w_out[l])

        hf = modulate(rms_norm(x, norm_ffn[l]), sh2, sc2)
        x = x + g2 * conv_ffn(hf, w_up[l], conv_w[l], w_down[l])
        if not last:
            hcf = modulate(rms_norm(xc, norm_ffn[l]), csh2, csc2)
            xc = xc + cg2 * conv_ffn(hcf, w_up[l], conv_w[l], w_down[l])
    return rms_norm(x, norm_final)
```

```python
import numpy as np
from contextlib import ExitStack
import concourse.bass as bass
import concourse.mybir as mybir
from concourse.bass_utils import run_bass_kernel_spmd

F32 = mybir.dt.float32
BF16 = mybir.dt.bfloat16
AF = mybir.ActivationFunctionType
ALU = mybir.AluOpType

D = 1024
T = 4096
CT = 256
DEPTH = 2
DFF = 2816
NFF = 22
INW2 = 4992
TT = 512
NCORES = 4
EPS = 1e-6


class Buf:
    __slots__ = ("name", "w", "r")

    def __init__(self, name):
        self.name = name
        self.w = []
        self.r = []


class Prog:
    ENG = ("pe", "act", "dve", "pool", "sp")
    SEM_ROLL = 30000
    NDMASEM = 12

    def __init__(self, nc, es):
        self.nc = nc
        self.es = es
        self.planning = False
        self.q = {e: [] for e in self.ENG}
        self.esem = {}
        self.ecnt = {}
        self.waited = {e: {} for e in self.ENG}
        self.nsem = 0
        for e in self.ENG:
            self._new_esem(e)
        self.dsem = {}
        self.dpos = {}
        for qn in ("sp", "act", "pool"):
            self.dsem[qn] = [[self._sem(f"d_{qn}_{i}"), 0] for i in range(self.NDMASEM)]
            self.dpos[qn] = 0
        self.n_inst = 0
        self.n_wait = 0

    def _sem(self, name):
        self.nsem += 1
        return self.es.enter_context(self.nc.semaphore(f"{name}_{self.nsem}"))

    def _new_esem(self, e):
        self.esem[e] = self._sem(f"e_{e}")
        self.ecnt[e] = 0

    def _wait(self, eng, ev):
        if ev is None:
            return
        sem, val, src = ev
        if src == "pe" and eng == "pe":
            return
        k = id(sem)
        if self.waited[eng].get(k, 0) >= val:
            return
        self.waited[eng][k] = val
        self.q[eng].append(("w", sem, val))
        self.n_wait += 1

    def _deps(self, eng, reads, writes, acc=False):
        for b in reads:
            for ev in b.w:
                self._wait(eng, ev)
        for b in writes:
            if not acc:
                for ev in b.w:
                    self._wait(eng, ev)
            for ev in b.r:
                self._wait(eng, ev)

    def _commit(self, ev, reads, writes, acc=False):
        for b in reads:
            if len(b.r) > 48:
                d = {}
                for e2 in b.r:
                    k = id(e2[0])
                    if k not in d or d[k][1] < e2[1]:
                        d[k] = e2
                b.r = list(d.values())
            b.r.append(ev)
        for b in writes:
            if acc:
                b.w.append(ev)
            else:
                b.w = [ev]
            b.r = []

    def op(self, eng, fn, reads=(), writes=()):
        if self.planning:
            return None
        self._deps(eng, reads, writes)
        if self.ecnt[eng] >= self.SEM_ROLL:
            self._new_esem(eng)
        self.ecnt[eng] += 1
        sem = self.esem[eng]
        val = self.ecnt[eng]
        self.q[eng].append(("o", fn, sem))
        ev = (sem, val, eng)
        self._commit(ev, reads, writes)
        self.n_inst += 1
        return ev

    def dma(self, qn, out, in_, reads=(), writes=(), acc=False):
        if self.planning:
            return None
        self._deps(qn, reads, writes, acc)
        pos = self.dpos[qn]
        self.dpos[qn] = (pos + 1) % self.NDMASEM
        slot = self.dsem[qn][pos]
        sem, cnt = slot
        if cnt > 0:
            self._wait(qn, (sem, cnt, "dma"))
        if cnt >= self.SEM_ROLL:
            sem = self._sem(f"d_{qn}")
            cnt = 0
            slot[0] = sem
        cnt += 16
        slot[1] = cnt
        self.q[qn].append(("d", out, in_, sem))
        ev = (sem, cnt, "dma")
        self._commit(ev, reads, writes, acc)
        self.n_inst += 1
        return ev

    def final_wait(self, eng, ev):
        self.q[eng].append(("w", ev[0], ev[1]))

    def emit(self):
        nc = self.nc
        q = self.q

        def replay(e, lst):
            for it in lst:
                if it[0] == "w":
                    e.wait_ge(it[1], it[2])
                elif it[0] == "o":
                    it[1](e).then_inc(it[2], 1)
                else:
                    e.dma_start(out=it[1], in_=it[2]).then_inc(it[3], 16)

        with nc.Block() as block:
            @block.sync
            def _(e):
                replay(e, q["sp"])

            @block.tensor
            def _(e):
                replay(e, q["pe"])

            @block.scalar
            def _(e):
                replay(e, q["act"])

            @block.vector
            def _(e):
                replay(e, q["dve"])

            @block.gpsimd
            def _(e):
                replay(e, q["pool"])


class WPool:
    SLOT = 4096

    def __init__(self, P, nc, es, nslots=5, ahead=3, scratch=None):
        self.P = P
        self.n = nslots
        self.ahead = ahead
        self.scratch = scratch
        self.tiles = [es.enter_context(nc.sbuf_tensor(f"wslot{i}", [128, self.SLOT], BF16)) for i in range(nslots)]
        self.bufs = [Buf(f"wslot{i}") for i in range(nslots)]
        self.plan = []
        self.issued = 0
        self.cur = 0
        self.keys = {}

    def _view(self, i):
        src, npart, kc, ncol, qn, key = self.plan[i]
        t = self.tiles[i % self.n]
        return t[0:npart, 0:kc * ncol].rearrange("p (k n) -> p k n", k=kc)

    def _issue(self, i):
        src, npart, kc, ncol, qn, key = self.plan[i]
        sbuf_ = self.bufs[i % self.n]
        flat = self.tiles[i % self.n][0:npart, 0:kc * ncol]
        if key is None or self.scratch is None:
            self.P.dma(qn, self._view(i), src, writes=[sbuf_])
        elif key not in self.keys:
            k = len(self.keys)
            SB = Buf(f"scr{k}")
            self.keys[key] = (k, SB)
            self.P.dma(qn, self._view(i), src, writes=[sbuf_])
            self.P.dma("sp", self.scratch[k, 0:npart, 0:kc * ncol], flat, reads=[sbuf_], writes=[SB])
        else:
            k, SB = self.keys[key]
            self.P.dma("sp", flat, self.scratch[k, 0:npart, 0:kc * ncol], reads=[SB], writes=[sbuf_])

    def req(self, src, npart, kc, ncol, qn="pool", key=None):
        assert kc * ncol <= self.SLOT
        if self.P.planning:
            self.plan.append((src, npart, kc, ncol, qn, key))
            return None, None
        while self.issued < min(len(self.plan), self.cur + self.ahead + 1):
            self._issue(self.issued)
            self.issued += 1
        i = self.cur
        self.cur += 1
        assert self.plan[i][1:] == (npart, kc, ncol, qn, key), (i, self.plan[i][1:], (npart, kc, ncol, qn, key))
        return self._view(i), self.bufs[i % self.n]


def build_program(depth_run=DEPTH, dbg=False):
    nc = bass.Bass("TRN2", target_bir_lowering=False)
    es = ExitStack()
    with es:
        P = Prog(nc, es)

        def dram(name, shape, dt, kind):
            return nc.dram_tensor(name, shape, dt, kind=kind).ap()

        def sb(name, shape, dt=F32):
            return es.enter_context(nc.sbuf_tensor(name, shape, dt))

        xT = dram("xT", [D, T], F32, "ExternalInput")
        ctxT = dram("ctxT", [D, CT], F32, "ExternalInput")
        cvec_d = dram("cvec", [128, 16], F32, "ExternalInput")
        w_mod_d = dram("w_mod", [DEPTH, D, 6 * D], F32, "ExternalInput")
        bmod_d = dram("bmod", [128, DEPTH * 96], F32, "ExternalInput")
        nmix_d = dram("nmix", [128, DEPTH * 16], F32, "ExternalInput")
        nffn_d = dram("nffn", [128, DEPTH * 16], F32, "ExternalInput")
        nfin_d = dram("nfin", [128, 8], F32, "ExternalInput")
        w_in_d = dram("w_in2", [DEPTH, D, INW2], F32, "ExternalInput")
        sink_d = dram("sinkx", [DEPTH * 2, 512], F32, "ExternalInput")
        sel_d = dram("sel", [4, 256], F32, "ExternalInput")
        pwbd_d = dram("pwbd", [DEPTH * 2, 128, 128], F32, "ExternalInput")
        pscale_d = dram("pscale", [128, DEPTH * 2], F32, "ExternalInput")
        wba_d = dram("w_br_attn", [DEPTH, 512, D], F32, "ExternalInput")
        wbp_d = dram("w_br_pool", [DEPTH, 256, D], F32, "ExternalInput")
        wbf_d = dram("w_br_four", [DEPTH, 256, D], F32, "ExternalInput")
        wout_d = dram("w_out", [DEPTH, D, D], F32, "ExternalInput")
        wup_d = dram("w_up", [DEPTH, D, 2 * DFF], F32, "ExternalInput")
        convw_d = dram("convw", [128, DEPTH * 3 * 44], F32, "ExternalInput")
        wdn_d = dram("w_down", [DEPTH, DFF, D], F32, "ExternalInput")
        ropeC_d = dram("ropeC", [128, T], F32, "ExternalInput")
        ropeS_d = dram("ropeS", [128, T], F32, "ExternalInput")
        mask_d = dram("masks", [128, 1024], F32, "ExternalInput")
        ctab_d = dram("ctab", [T // TT, 4, 128, 4096], BF16, "ExternalInput")
        stab_d = dram("stab", [T // TT, 4, 128, 4096], BF16, "ExternalInput")
        ctabc_d = dram("ctabc", [CT, CT], BF16, "ExternalInput")
        stabc_d = dram("stabc", [CT, CT], BF16, "ExternalInput")
        cs64_d = dram("cs64", [128, 256], F32, "ExternalInput")
        rcnt_d = dram("rcnt", [128, 32], F32, "ExternalInput")
        invw_d = dram("invw", [128, 2], F32, "ExternalInput")
        outT = dram("outT", [D, T], F32, "ExternalOutput")
        XM = dram("xmid", [D, T], F32, "Internal")
        X1 = dram("x1s", [D, T], F32, "Internal")
        UL = dram("u_lat", [256, T + 16], F32, "Internal")
        UC = dram("u_ctx", [256, CT + 16], F32, "Internal")

        def fm(ap):
            return ap.rearrange("(k p) t -> p k t", p=128)

        NTL = T // TT
        XB = {"x0": [Buf(f"x0_{i}") for i in range(NTL)], "xm": [Buf(f"xm_{i}") for i in range(NTL)],
              "x1": [Buf(f"x1_{i}") for i in range(NTL)], "out": [Buf(f"o_{i}") for i in range(NTL)]}
        ULB = Buf("UL")
        UCB = Buf("UC")

        WSCR = dram("wscr", [96, 128, 4096], BF16, "Internal")
        W = WPool(P, nc, es, nslots=4, ahead=2, scratch=WSCR)
        ones = sb("ones", [128, 128], BF16)
        ONES = Buf("ones")
        kT = sb("kT", [128, T], BF16)
        KT = Buf("kT")
        V = sb("V", [128, T // 128, 128], BF16)
        VB = Buf("V")
        Ftok = sb("Ftok", [128, T // 128, 256], BF16)
        FB = Buf("Ftok")
        kcT = sb("kcT", [128, CT], BF16)
        KCT = Buf("kcT")
        Vc = sb("Vc", [128, 2, 128], BF16)
        VCB = Buf("Vc")
        Fc = sb("Fc", [128, 2, 256], BF16)
        FCB = Buf("Fc")
        xc = sb("xc", [128, 8, CT + 2], F32)
        XC = Buf("xc")
        cvec = sb("cvec_s", [128, 16], F32)
        csil = sb("csil", [128, 16], BF16)
        CV = Buf("cvec")
        MODS = [Buf("mod0"), Buf("mod1")]
        LCUR = [0]
        bmod = sb("bmod_s", [128, DEPTH * 96], F32)
        nmix = sb("nmix_s", [128, DEPTH * 16], F32)
        nffn = sb("nffn_s", [128, DEPTH * 16], F32)
        nfin = sb("nfin_s", [128, 8], F32)
        CONSTB = Buf("consts")
        esrow = sb("esrow", [4, 512], BF16)
        sel = sb("sel_s", [4, 256], BF16)
        pwbd = sb("pwbd_s", [128, DEPTH * 2, 128], BF16)
        pscale = sb("pscale_s", [128, DEPTH * 2], F32)
        convw = sb("convw_s", [128, DEPTH * 3 * 44], F32)
        masks = sb("masks_s", [128, 1024], BF16)
        cs64 = sb("cs64_s", [128, 256], BF16)
        rcnt = sb("rcnt_s", [128, 32], F32)
        invw = sb("invw_s", [128, 2], F32)
        class _Cur:
            pass
        cur = _Cur()
        _xts = [sb(f"xt{i}", [128, 8, TT + 2], F32) for i in range(2)]
        _XTs = [Buf(f"xt{i}") for i in range(2)]
        _hs = [sb(f"h{i}", [128, 8, TT + 2], BF16) for i in range(2)]
        _HBs = [Buf(f"h{i}") for i in range(2)]
        _flipc = [0]

        def flip():
            i = _flipc[0] % 2
            _flipc[0] += 1
            cur.xt, cur.XT, cur.h, cur.HB = _xts[i], _XTs[i], _hs[i], _HBs[i]
        def setcur(k):
            i = k % 2
            cur.xt, cur.XT, cur.h, cur.HB = _xts[i], _XTs[i], _hs[i], _HBs[i]
        flip()
        act = sb("act", [128, NFF, TT], BF16)
        ACTB = Buf("act")
        sq = [sb(f"sq{i}", [128, 512], BF16) for i in range(2)]
        SQ = [Buf(f"sq{i}") for i in range(2)]
        rstd = sb("rstd", [128, TT + 16], F32)
        RSTD = Buf("rstd")
        tmp = [sb(f"tmp{i}", [128, 512], F32) for i in range(3)]
        TMP = [Buf(f"tmp{i}") for i in range(3)]
        esrow_f = tmp[0][0:4, :]
        rope = sb("rope", [128, 2, 512], F32)
        ROPE = Buf("rope")
        qrot = act[:, 8:12, :]
        QR = Buf("qrot")
        oT = act[0:64, 12:20, :]
        OT = Buf("oT")
        PT = sb("PT", [128, 5, 512], BF16)
        PTB = [Buf(f"PT{i}") for i in range(5)]
        PTB2 = [Buf(f"PT2_{i}") for i in range(5)]
        ptc = [0]
        rden = tmp[2][0:64, :]
        RDEN = TMP[2]
        uloc = sb("uloc", [128, 2, TT + 16], F32)
        ULOC = Buf("uloc")
        s1 = sb("s1", [128, 2, TT + 16], F32)
        S1 = Buf("s1")
        s2 = sb("s2", [128, TT + 16], F32)
        S2 = Buf("s2")
        s4 = rstd
        S4 = RSTD
        win = sb("win", [128, 2, TT], F32)
        WIN = Buf("win")
        pooled = act[:, 20:22, :]
        PLD = Buf("pooled")
        ypool = sb("ypool", [128, 2, TT], BF16)
        YP = Buf("ypool")
        gT = sb("gT", [128, 4, TT], BF16)
        GT = Buf("gT")
        yfour = sb("yfour", [128, 2, TT], BF16)
        YF = Buf("yfour")
        sg = [sb(f"sg{i}", [128, 512], F32) for i in range(3)]
        SG = [Buf(f"sg{i}") for i in range(3)]
        y = act[:, 0:8, :]
        YB = Buf("y")
        sgB = [sb(f"sgB{i}", [128, 512], F32) for i in range(3)]
        SGB = [Buf(f"sgB{i}") for i in range(3)]
        cvsets = [(sg, SG), (sgB, SGB)]
        tvs = [[sb(f"tv{a}{b}", [128, 8], F32) for b in range(3)] for a in range(2)]
        TVS = [[Buf(f"tv{a}{b}") for b in range(3)] for a in range(2)]
        tvsets = [(tvs[0], TVS[0]), (tvs[1], TVS[1])]

        def fence():
            P.op("pool", lambda e: e.memset(cvec[:, 0:1], 0.0), [CV], [ACTB, QR, OT, YB, PLD])
        ps = [es.enter_context(nc.psum_tensor(f"ps{i}", [128, 512], F32)) for i in range(8)]
        PS = [Buf(f"ps{i}") for i in range(8)]
        pspos = [0]

        def next_ps():
            i = pspos[0]
            pspos[0] = (i + 1) % 8
            return ps[i], PS[i]

        dumps = {}

        def dump(name, src, shape, reads):
            if not dbg or P.planning or name in dumps:
                return
            dumps[name] = dram("dbg_" + name, list(shape), F32, "ExternalOutput")
            P.dma("pool", dumps[name], src, reads=reads, writes=[Buf("dbg_" + name)])

        def mm(out, lhsT, rhs, start, stop, reads, writes):
            P.op("pe", lambda e: e.matmul(out, lhsT=lhsT, rhs=rhs, start=start, stop=stop), reads, writes)

        def actf(out, in_, func, reads, writes, bias=None, scale=None, eng="act"):
            kw = {}
            if bias is not None:
                kw["bias"] = bias
            if scale is not None:
                kw["scale"] = scale
            P.op("act", lambda e: e.activation(out=out, in_=in_, func=func, **kw), reads, writes)

        def tt(eng, out, in0, in1, op, reads, writes):
            P.op(eng, lambda e: e.tensor_tensor(out=out, in0=in0, in1=in1, op=op), reads, writes)

        def stt(eng, out, in0, scalar, in1, op0, op1, reads, writes):
            P.op(eng, lambda e: e.scalar_tensor_tensor(out=out, in0=in0, scalar=scalar, in1=in1, op0=op0, op1=op1), reads, writes)

        def ts(eng, out, in0, s1_, s2_, op0, op1, reads, writes):
            if op1 is None:
                P.op(eng, lambda e: e.tensor_scalar(out=out, in0=in0, scalar1=s1_, scalar2=None, op0=op0), reads, writes)
            else:
                P.op(eng, lambda e: e.tensor_scalar(out=out, in0=in0, scalar1=s1_, scalar2=s2_, op0=op0, op1=op1), reads, writes)

        def cp(eng, out, in_, reads, writes):
            P.op(eng, lambda e: e.tensor_copy(out=out, in_=in_), reads, writes)

        def mset(eng, ap, val, writes):
            P.op(eng, lambda e: e.memset(ap, val), (), writes)

        def setup():
            mset("pool", ones[:, :], 1.0, [ONES])
            P.dma("sp", cvec[:, :], cvec_d[:, :], writes=[CV])
            P.dma("sp", bmod[:, :], bmod_d[:, :], writes=[CONSTB])
            P.dma("sp", nmix[:, :], nmix_d[:, :], writes=[CONSTB])
            P.dma("sp", nffn[:, :], nffn_d[:, :], writes=[CONSTB])
            P.dma("sp", nfin[:, :], nfin_d[:, :], writes=[CONSTB])
            P.dma("sp", pscale[:, :], pscale_d[:, :], writes=[CONSTB])
            P.dma("sp", convw[:, :], convw_d[:, :], writes=[CONSTB])
            P.dma("sp", rcnt[:, :], rcnt_d[:, :], writes=[CONSTB])
            P.dma("sp", invw[:, :], invw_d[:, :], writes=[CONSTB])
            P.dma("sp", esrow_f[:, :], sink_d[:, :], writes=[TMP[0]])
            P.dma("pool", sel[:, :], sel_d[:, :], writes=[CONSTB])
            P.dma("pool", pwbd[:, :, :], pwbd_d.rearrange("a p n -> p a n"), writes=[CONSTB])
            P.dma("pool", masks[:, :], mask_d[:, :], writes=[CONSTB])
            P.dma("pool", cs64[:, :], cs64_d[:, :], writes=[CONSTB])
            actf(esrow[:, :], esrow_f[:, :], AF.Exp, [CONSTB, TMP[0]], [CONSTB])
            actf(csil[:, :], cvec[:, :], AF.Silu, [CV], [CV])
            mset("pool", xc[:, :, :], 0.0, [XC])
            P.dma("sp", xc[:, :, 1:CT + 1], fm(ctxT), writes=[XC])
            mset("pool", s1[:, :, :], 0.0, [S1])
            for (U, UB, n) in ((UL, ULB, T), (UC, UCB, CT)):
                Uv = U.rearrange("(c p) t -> p c t", p=128)
                P.dma("sp", Uv[:, :, 0:8], s1[:, :, 0:8], reads=[S1], writes=[UB])
                P.dma("sp", Uv[:, :, n + 8:n + 16], s1[:, :, 0:8], reads=[S1], writes=[UB])

        def mod_phase(l):
            pm, PM = next_ps()
            for blk in range(12):
                wv, WB = W.req(w_mod_d[l].rearrange("(k p) n -> p k n", p=128)[:, :, blk * 512:(blk + 1) * 512], 128, 8, 512)
                for j in range(4):
                    oc = blk * 4 + j
                    for kc in range(8):
                        mm(pm[:, oc * 2:oc * 2 + 2], wv[:, kc, j * 128:(j + 1) * 128] if wv is not None else None,
                           csil[:, kc * 2:kc * 2 + 2], kc == 0, kc == 7, [WB, CV], [PM])
            MOD = MODS[l]
            mT = bmod[:, l * 96:(l + 1) * 96]
            tt("dve", mT, pm[:, 0:96], mT, ALU.add, [PM, CONSTB], [MOD])
            stt("dve", nmix[:, l * 16:(l + 1) * 16], mT[:, 16:32], 1.0, nmix[:, l * 16:(l + 1) * 16], ALU.add, ALU.mult, [MOD, CONSTB], [MOD])
            stt("dve", nffn[:, l * 16:(l + 1) * 16], mT[:, 64:80], 1.0, nffn[:, l * 16:(l + 1) * 16], ALU.add, ALU.mult, [MOD, CONSTB], [MOD])
            if l == 0:
                dump("modT", mT, [128, 96], [MOD])

        def modv(idx, kc, s):
            c = LCUR[0] * 96 + (idx * 8 + kc) * 2 + s
            return bmod[:, c:c + 1]

        def gmA():
            return nmix[:, LCUR[0] * 16:(LCUR[0] + 1) * 16]

        def gmB():
            return nffn[:, LCUR[0] * 16:(LCUR[0] + 1) * 16]

        def norm(xv, XBUF, ranges, gmt, shidx, s, hv, HBUF, zero_cols=()):
            for (c0, n) in ranges:
                pn, PN = next_ps()
                for kc in range(8):
                    k2 = kc % 2
                    tt("pool", sq[k2][:, 0:n], xv(kc, c0, n), xv(kc, c0, n), ALU.mult, [XBUF], [SQ[k2]])
                    mm(pn[:, 0:n], ones[:, :], sq[k2][:, 0:n], kc == 0, kc == 7, [ONES, SQ[k2]], [PN])
                actf(rstd[:, 0:n], pn[:, 0:n], AF.Sqrt, [PN], [RSTD], bias=EPS, scale=1.0 / D)
                P.op("dve", lambda e, n=n: e.reciprocal(out=rstd[:, 0:n], in_=rstd[:, 0:n]), [RSTD], [RSTD])
                for kc in range(8):
                    k3 = kc % 3
                    tt("dve", tmp[k3][:, 0:n], xv(kc, c0, n), rstd[:, 0:n], ALU.mult, [XBUF, RSTD], [TMP[k3]])
                    if shidx is None:
                        actf(hv(kc, c0, n), tmp[k3][:, 0:n], AF.Identity, [TMP[k3], CONSTB], [HBUF], scale=gmt[:, kc:kc + 1])
                    else:
                        actf(hv(kc, c0, n), tmp[k3][:, 0:n], AF.Identity, [TMP[k3], MODS[LCUR[0]]], [HBUF],
                             bias=modv(shidx, kc, s), scale=gmt[:, kc * 2 + s:kc * 2 + s + 1])
            for c in zero_cols:
                mset("pool", hv(0, c, 1).rearrange("p n -> p n") if False else cur.h[:, :, c:c + 1], 0.0, [HBUF])

        def p1_tile(l, S, n0, n, xv, XBUF, hoff, do_norm=True, hook=None):
            lat = S == "L"
            s = 0 if lat else 1
            hv = lambda kc, c0, m: cur.h[:, kc, hoff + c0:hoff + c0 + m]
            if do_norm:
                norm(xv, XBUF, [(0, n)], gmA(), 0, s, hv, cur.HB)
            if hook is not None:
                hook()
            if l == 0 and lat and n0 == 0:
                dump("h", cur.h[:, :, 0:n], [128, 8, n], [cur.HB])
            win_ = w_in_d[l].rearrange("(k p) n -> p k n", p=128)
            wa, WA = W.req(win_[:, :, 0:384], 128, 8, 384, key=("p1a", l))
            wb, WBb = W.req(win_[:, :, 384:896], 128, 8, 512, key=("p1b", l))
            hh = lambda kc: cur.h[:, kc, hoff:hoff + n]
            pk, PK = next_ps()
            for kc in range(8):
                mm(pk[:, 0:n], wa[:, kc, 0:128] if wa is not None else None, hh(kc), kc == 0, kc == 7, [WA, cur.HB], [PK])
            if lat:
                pkp, PKP = next_ps()
                for kc in range(8):
                    mm(pkp[:, 0:n], wa[:, kc, 128:256] if wa is not None else None, hh(kc), kc == 0, kc == 7, [WA, cur.HB], [PKP])
                P.dma("pool", rope[:, 0, 0:n], ropeC_d[:, n0:n0 + n], writes=[ROPE])
                P.dma("pool", rope[:, 1, 0:n], ropeS_d[:, n0:n0 + n], writes=[ROPE])
                tt("dve", tmp[0][:, 0:n], pk[:, 0:n], rope[:, 0, 0:n], ALU.mult, [PK, ROPE], [TMP[0]])
                tt("dve", tmp[1][:, 0:n], pkp[:, 0:n], rope[:, 1, 0:n], ALU.mult, [PKP, ROPE], [TMP[1]])
                tt("pool", kT[:, n0:n0 + n], tmp[0][:, 0:n], tmp[1][:, 0:n], ALU.add, [TMP[0], TMP[1]], [KT])
            else:
                actf(kcT[:, 0:n], pk[:, 0:n], AF.Copy, [PK], [KCT])
            pv, PV = next_ps()
            nb = n // 128
            for b in range(nb):
                for kc in range(8):
                    mm(pv[:, b * 128:(b + 1) * 128], cur.h[:, kc, hoff + b * 128:hoff + (b + 1) * 128],
                       wa[:, kc, 256:384] if wa is not None else None, kc == 0, kc == 7, [WA, cur.HB], [PV])
            if lat:
                actf(V[:, n0 // 128:n0 // 128 + nb, :], pv[:, 0:n].rearrange("p (b c) -> p b c", b=nb), AF.Copy, [PV], [VB])
            else:
                actf(Vc[:, 0:nb, :], pv[:, 0:n].rearrange("p (b c) -> p b c", b=nb), AF.Copy, [PV], [VCB])
            if l == DEPTH - 1 and not lat:
                return
            for c in range(2):
                pu, PU = next_ps()
                for kc in range(8):
                    mm(pu[:, 0:n], wb[:, kc, c * 128:(c + 1) * 128] if wb is not None else None, hh(kc), kc == 0, kc == 7, [WBb, cur.HB], [PU])
                actf(uloc[:, c, 0:n], pu[:, 0:n], AF.Copy, [PU], [ULOC])
            U, UB = (UL, ULB) if lat else (UC, UCB)
            P.dma("pool", U.rearrange("(c p) t -> p c t", p=128)[:, :, 8 + n0:8 + n0 + n], uloc[:, :, 0:n], reads=[ULOC], writes=[UB])
            for b0 in range(0, nb, 2):
                pf, PF = next_ps()
                nbb = min(2, nb - b0)
                for b in range(nbb):
                    for kc in range(8):
                        mm(pf[:, b * 256:(b + 1) * 256], cur.h[:, kc, hoff + (b0 + b) * 128:hoff + (b0 + b + 1) * 128],
                           wb[:, kc, 256:512] if wb is not None else None, kc == 0, kc == 7, [WBb, cur.HB], [PF])
                src = pf[:, 0:nbb * 256].rearrange("p (b c) -> p b c", b=nbb)
                if lat:
                    cp("dve", Ftok[:, n0 // 128 + b0:n0 // 128 + b0 + nbb, :], src, [PF], [FB])
                else:
                    cp("dve", Fc[:, b0:b0 + nbb, :], src, [PF], [FCB])

        def p2_tile(l, S, t0, n, xv, XBUF, hoff, do_norm=True, hook=None):
            lat = S == "L"
            s = 0 if lat else 1
            Tn = T if lat else CT
            dd = l == 0 and lat and t0 == 0
            hv = lambda kc, c0, m: cur.h[:, kc, hoff + c0:hoff + c0 + m]
            hh = lambda kc: cur.h[:, kc, hoff:hoff + n]
            if do_norm:
                norm(xv, XBUF, [(0, n)], gmA(), 0, s, hv, cur.HB)
            win_ = w_in_d[l].rearrange("(k p) n -> p k n", p=128)
            wq, WQ = W.req(win_[:, :, 896:1408], 128, 8, 512, key=("q", l))
            if lat:
                wqp, WQP = W.req(win_[:, :, 1408:1920], 128, 8, 512, key=("qp", l))
                P.dma("pool", rope[:, 0, 0:n], ropeC_d[:, t0:t0 + n], writes=[ROPE])
                P.dma("pool", rope[:, 1, 0:n], ropeS_d[:, t0:t0 + n], writes=[ROPE])
            for j in range(4):
                pq, PQ = next_ps()
                for kc in range(8):
                    mm(pq[:, 0:n], wq[:, kc, j * 128:(j + 1) * 128] if wq is not None else None, hh(kc), kc == 0, kc == 7, [WQ, cur.HB], [PQ])
                if lat:
                    pqp, PQP = next_ps()
                    for kc in range(8):
                        mm(pqp[:, 0:n], wqp[:, kc, j * 128:(j + 1) * 128] if wqp is not None else None, hh(kc), kc == 0, kc == 7, [WQP, cur.HB], [PQP])
                    if j % 2 == 0:
                        ta, TA, tb, TBb = tmp[0], TMP[0], tmp[1], TMP[1]
                    else:
                        ta, TA, tb, TBb = sgB[0], SGB[0], sgB[1], SGB[1]
                    tt("dve", ta[:, 0:n], pq[:, 0:n], rope[:, 0, 0:n], ALU.mult, [PQ, ROPE], [TA])
                    tt("dve", tb[:, 0:n], pqp[:, 0:n], rope[:, 1, 0:n], ALU.mult, [PQP, ROPE], [TBb])
                    tt("pool", qrot[:, j, 0:n], ta[:, 0:n], tb[:, 0:n], ALU.add, [TA, TBb], [QR])
                else:
                    actf(qrot[:, j, 0:n], pq[:, 0:n], AF.Copy, [PQ], [QR])
            U, UB = (UL, ULB) if lat else (UC, UCB)
            L_ = n + 16
            P.dma("pool", uloc[:, :, 0:L_], U.rearrange("(c p) t -> p c t", p=128)[:, :, t0:t0 + L_], reads=[UB], writes=[ULOC])
            tt("pool", s1[:, :, 1:L_], uloc[:, :, 1:L_], uloc[:, :, 0:L_ - 1], ALU.add, [ULOC], [S1])
            tt("pool", s2[:, 3:L_], s1[:, 1, 3:L_], s1[:, 1, 1:L_ - 2], ALU.add, [S1], [S2])
            tt("pool", s4[64:128, 7:L_], s2[64:128, 7:L_], s2[64:128, 3:L_ - 4], ALU.add, [S2], [S4])
            cp("pool", win[0:64, 0, 0:n], s1[0:64, 0, 8:8 + n], [S1], [WIN])
            tt("pool", win[64:128, 0, 0:n], s1[64:128, 0, 7:7 + n], s1[64:128, 0, 9:9 + n], ALU.add, [S1], [WIN])
            tt("pool", win[0:64, 1, 0:n], s2[0:64, 7:7 + n], s2[0:64, 11:11 + n], ALU.add, [S2], [WIN])
            tt("pool", win[64:128, 1, 0:n], s4[64:128, 7:7 + n], s4[64:128, 15:15 + n], ALU.add, [S4], [WIN])
            for c in range(2):
                stt("dve", pooled[:, c, 0:n], win[:, c, 0:n], invw[:, c:c + 1], uloc[:, c, 8:8 + n], ALU.mult, ALU.subtract,
                    [WIN, CONSTB, ULOC], [PLD])
                if t0 == 0:
                    tt("pool", s2[:, 0:8], win[:, c, 0:8], rcnt[:, c * 16:c * 16 + 8], ALU.mult, [WIN, CONSTB, S2], [S2])
                    tt("pool", pooled[:, c, 0:8], s2[:, 0:8], uloc[:, c, 8:16], ALU.subtract, [S2, ULOC], [PLD])
                if t0 + n == Tn:
                    tt("pool", s2[:, 0:8], win[:, c, n - 8:n], rcnt[:, c * 16 + 8:c * 16 + 16], ALU.mult, [WIN, CONSTB, S2], [S2])
                    tt("pool", pooled[:, c, n - 8:n], s2[:, 0:8], uloc[:, c, n:n + 8], ALU.subtract, [S2, ULOC], [PLD])

            def pool_mm():
                for c in range(2):
                    pp, PP = next_ps()
                    mm(pp[:, 0:n], pwbd[:, l * 2 + c, :], pooled[:, c, 0:n], True, True, [CONSTB, PLD], [PP])
                    actf(ypool[:, c, 0:n], pp[:, 0:n], AF.Identity, [PP, CONSTB], [YP], scale=pscale[:, l * 2 + c:l * 2 + c + 1])
            nqb = n // 128

            def att_front(qb, hk):
                gq = (t0 // 128) + qb
                pr = slice(hk * 64, (hk + 1) * 64)
                chunks = []
                if lat:
                    for d_, mi in ((-1, 0), (0, None), (1, 1)):
                        kb = gq + d_
                        if 0 <= kb < T // 128:
                            chunks.append((kT[pr, kb * 128:(kb + 1) * 128], V[:, kb, hk * 64:(hk + 1) * 64], mi, [KT, VB]))
                for cb in range(2):
                    chunks.append((kcT[pr, cb * 128:(cb + 1) * 128], Vc[:, cb, hk * 64:(hk + 1) * 64], None, [KCT, VCB]))
                rhs_q = qrot[pr, :, qb * 128:(qb + 1) * 128]
                if ptc[0] % 2 == 0:
                    PTv = [PT[:, ci_, :] for ci_ in range(5)]
                    PTb = PTB
                else:
                    PTv = [gT[:, 0, :], gT[:, 1, :], gT[:, 2, :], gT[:, 3, :], yfour[:, 0, :]]
                    PTb = PTB2
                ptc[0] += 1
                for ci, (kap, vap, mi, bb) in enumerate(chunks):
                    pS, PSb = next_ps()
                    mm(pS[:, :].rearrange("p (g q) -> p g q", g=4), kap, rhs_q, True, True, [bb[0], QR], [PSb])
                    actf(PTv[ci], pS[:, :], AF.Exp, [PSb], [PTb[ci]], scale=0.125)
                    if mi is not None:
                        tt("pool", PTv[ci], PTv[ci], masks[:, mi * 512:(mi + 1) * 512], ALU.mult, [PTb[ci], CONSTB], [PTb[ci]])
                return chunks, PTv, PTb

            def att_back(qb, hk, st):
                chunks, PTv, PTb = st
                po, PO = next_ps()
                pd, PD = next_ps()
                nc_ = len(chunks)
                for ci, (kap, vap, mi, bb) in enumerate(chunks):
                    mm(po[0:64, :], vap, PTv[ci], ci == 0, ci == nc_ - 1, [bb[1], PTb[ci]], [PO])
                for ci in range(nc_):
                    mm(pd[0:64, :], ones[:, 0:64], PTv[ci], ci == 0, False, [ONES, PTb[ci]], [PD])
                er = l * 2 + hk
                mm(pd[0:64, :], sel[0:4, er * 64:(er + 1) * 64], esrow[0:4, :], False, True, [CONSTB], [PD])
                P.op("dve", lambda e, pd=pd: e.reciprocal(out=rden[:, :], in_=pd[0:64, :]), [PD], [RDEN])
                tt("dve", oT[:, hk * 4:(hk + 1) * 4, qb * 128:(qb + 1) * 128], po[0:64, :].rearrange("p (g q) -> p g q", g=4),
                   rden[:, :].rearrange("p (g q) -> p g q", g=4), ALU.mult, [PO, RDEN], [OT])

            prev_att = None
            for qb in range(nqb):
                for hk in range(2):
                    st = att_front(qb, hk)
                    if prev_att is not None:
                        att_back(*prev_att)
                    prev_att = (qb, hk, st)
            att_back(*prev_att)
            pool_mm()
            if dd:
                dump("kT", kT[:, :], [128, T], [KT])
                dump("V", V[:, :, :], [128, T // 128, 128], [VB])
                dump("Ftok", Ftok[:, :, :], [128, T // 128, 256], [FB])
                dump("qrot", qrot[:, :, :], [128, 4, TT], [QR])
                dump("oT", oT[:, :, :], [64, 8, TT], [OT])
            if dd:
                dump("uloc", uloc[:, :, :], [128, 2, TT + 16], [ULOC])
                dump("pooled", pooled[:, :, :], [128, 2, TT], [PLD])
                dump("ypool", ypool[:, :, :], [128, 2, TT], [YP])
            ntt = Tn // 128
            ct_, st_ = (ctab_d, stab_d) if lat else (ctabc_d, stabc_d)
            Fsrc, FBUF = (Ftok, FB) if lat else (Fc, FCB)
            pg = [next_ps() for _ in range(4)]
            nblk = max(1, ntt // 8)
            per = ntt // nblk
            for bi in range(nblk):
                for ti, tab in enumerate((ct_, st_)):
                    if lat:
                        tsrc = tab[t0 // TT, bi].rearrange("p (k n) -> p k n", k=per)
                    else:
                        tsrc = tab.rearrange("(k p) t -> p k t", p=128)[:, bi * per:(bi + 1) * per, t0:t0 + n]
                    tv, TB = W.req(tsrc, 128, per, n, "sp")
                    for fc in range(2):
                        pgt, PGT = pg[ti * 2 + fc]
                        for k in range(per):
                            tti = bi * per + k
                            mm(pgt[:, 0:n], Fsrc[:, tti, fc * 128:(fc + 1) * 128], tv[:, k, :] if tv is not None else None,
                               tti == 0, tti == ntt - 1, [FBUF, TB], [PGT])
            for i in range(4):
                pgt, PGT = pg[i]
                if i % 2 == 0:
                    actf(gT[:, i, 0:n], pgt[:, 0:n], AF.Copy, [PGT], [GT])
                else:
                    cp("dve", gT[:, i, 0:n], pgt[:, 0:n], [PGT], [GT])
            for fc in range(2):
                pf, PF = next_ps()
                mm(pf[:, 0:n], cs64[:, 0:128], gT[:, fc, 0:n], True, False, [CONSTB, GT], [PF])
                mm(pf[:, 0:n], cs64[:, 128:256], gT[:, 2 + fc, 0:n], False, True, [CONSTB, GT], [PF])
                actf(yfour[:, fc, 0:n], pf[:, 0:n], AF.Copy, [PF], [YF])
            if dd:
                dump("yfour", yfour[:, :, :], [128, 2, TT], [YF])
            if hook is not None:
                hook()
            for oc in range(8):
                wc, WC = W.req(("br", l, oc), 128, 1, 1536, key=("br", l, oc))
                wg, WG = W.req(win_[:, :, 1920 + oc * 384:1920 + (oc + 1) * 384], 128, 8, 384, key=("g", l, oc))
                if wc is not None:
                    wba = wc[0:64, 0, 0:1024].rearrange("p (h n) -> p h n", h=8)
                    wpf = wc[:, 0, 1024:1536].rearrange("p (c n) -> p c n", c=4)
                else:
                    wba = wpf = None
                pa, PA = next_ps()
                for hd in range(8):
                    mm(pa[:, 0:n], wba[:, hd, :] if wba is not None else None, oT[:, hd, 0:n], hd == 0, hd == 7, [WC, OT], [PA])
                pb, PB = next_ps()
                for c in range(2):
                    mm(pb[:, 0:n], wpf[:, c, :] if wpf is not None else None, ypool[:, c, 0:n], c == 0, c == 1, [WC, YP], [PB])
                pc, PC = next_ps()
                for c in range(2):
                    mm(pc[:, 0:n], wpf[:, 2 + c, :] if wpf is not None else None, yfour[:, c, 0:n], c == 0, c == 1, [WC, YF], [PC])
                brs = [(pa, PA), (pb, PB), (pc, PC)]
                for b in range(3):
                    pgt, PGT = next_ps()
                    for kc in range(8):
                        mm(pgt[:, 0:n], wg[:, kc, b * 128:(b + 1) * 128] if wg is not None else None, hh(kc), kc == 0, kc == 7, [WG, cur.HB], [PGT])
                    actf(sg[b][:, 0:n], pgt[:, 0:n], AF.Sigmoid, [PGT], [SG[b]])
                    tt("dve", sg[b][:, 0:n], sg[b][:, 0:n], brs[b][0][:, 0:n], ALU.mult, [SG[b], brs[b][1]], [SG[b]])
                tt("pool", sg[0][:, 0:n], sg[0][:, 0:n], sg[1][:, 0:n], ALU.add, [SG[0], SG[1]], [SG[0]])
                tt("pool", y[:, oc, 0:n], sg[0][:, 0:n], sg[2][:, 0:n], ALU.add, [SG[0], SG[2]], [YB])
            if dd:
                dump("y", y[:, :, :], [128, 8, TT], [YB])
            for half in range(2):
                wo, WO = W.req(wout_d[l].rearrange("(k p) n -> p k n", p=128)[:, :, half * 512:(half + 1) * 512], 128, 8, 512, key=("wo", l, half))
                for j in range(4):
                    oc = half * 4 + j
                    pz, PZ = next_ps()
                    for kc in range(8):
                        mm(pz[:, 0:n], wo[:, kc, j * 128:(j + 1) * 128] if wo is not None else None, y[:, kc, 0:n], kc == 0, kc == 7, [WO, YB], [PZ])
                    stt("dve", xv(oc, 0, n), pz[:, 0:n], modv(2, oc, s), xv(oc, 0, n), ALU.mult, ALU.add, [PZ, MODS[LCUR[0]], XBUF], [XBUF])

        def p3_tile(l, S, t0, n, xv, XBUF, zero_cols, do_norm=True, hook=None):
            lat = S == "L"
            s = 0 if lat else 1
            hv = lambda kc, c0, m: cur.h[:, kc, c0:c0 + m]
            if n == 512:
                ranges = [(0, 512), (510, 4)]
            else:
                ranges = [(0, n + 2)]
            if do_norm:
                norm(xv, XBUF, [(0, min(512, n + 2))] + ([(512, 2)] if n == 512 else []), gmB(), 3, s, hv, cur.HB, zero_cols=zero_cols)
            wup_ = wup_d[l].rearrange("(k p) n -> p k n", p=128)
            pend = [None]
            for jg in range(0, NFF, 4):
                nj = min(4, NFF - jg)
                wv_, WVb = W.req(wup_[:, :, jg * 128:(jg + nj) * 128], 128, 8, nj * 128, key=("wv", l, jg))
                wg_, WGb = W.req(wup_[:, :, DFF + jg * 128:DFF + (jg + nj) * 128], 128, 8, nj * 128, key=("wg", l, jg))
                for jj in range(nj):
                    j = jg + jj
                    for (c0, m) in ranges:
                        pvv, PVV = next_ps()
                        pgg, PGG = next_ps()
                        for kc in range(8):
                            mm(pvv[:, 0:m], wv_[:, kc, jj * 128:(jj + 1) * 128] if wv_ is not None else None, cur.h[:, kc, c0:c0 + m], kc == 0, kc == 7, [WVb, cur.HB], [PVV])
                        for kc in range(8):
                            mm(pgg[:, 0:m], wg_[:, kc, jj * 128:(jj + 1) * 128] if wg_ is not None else None, cur.h[:, kc, c0:c0 + m], kc == 0, kc == 7, [WGb, cur.HB], [PGG])
                        mo = m - 2
                        (cva, cvg, cvs), (CVA, CVG, CVS) = (cvsets if m > 8 else tvsets)[j % 2]
                        for (pp_, PPb, ch, dst, DST) in ((pvv, PVV, j, cva, CVA), (pgg, PGG, NFF + j, cvg, CVG)):
                            cw = lambda k, ch=ch: convw[:, (l * 3 + k) * 44 + ch:(l * 3 + k) * 44 + ch + 1]
                            actf(dst[:, 0:mo], pp_[:, 1:1 + mo], AF.Identity, [PPb, CONSTB], [DST], scale=cw(1))
                            stt("dve", dst[:, 0:mo], pp_[:, 0:mo], cw(0), dst[:, 0:mo], ALU.mult, ALU.add, [PPb, CONSTB, DST], [DST])
                            stt("dve", dst[:, 0:mo], pp_[:, 2:2 + mo], cw(2), dst[:, 0:mo], ALU.mult, ALU.add, [PPb, CONSTB, DST], [DST])
                        def fin_(cva=cva, cvg=cvg, cvs=cvs, CVA=CVA, CVG=CVG, CVS=CVS, j=j, c0=c0, mo=mo):
                            actf(cvs[:, 0:mo], cvg[:, 0:mo], AF.Silu, [CVG], [CVS])
                            tt("pool", act[:, j, c0:c0 + mo], cva[:, 0:mo], cvs[:, 0:mo], ALU.mult, [CVA, CVS], [ACTB])
                        if pend[0] is not None:
                            pend[0]()
                        pend[0] = fin_
            if pend[0] is not None:
                pend[0]()
                pend[0] = None
            if hook is not None:
                hook()
            wdn_ = wdn_d[l].rearrange("(k p) n -> p k n", p=128)
            for oc in range(8):
                wd, WD = W.req(wdn_[:, :, oc * 128:(oc + 1) * 128], 128, NFF, 128, key=("wd", l, oc))
                pz, PZ = next_ps()
                for j in range(NFF):
                    mm(pz[:, 0:n], wd[:, j, :] if wd is not None else None, act[:, j, 0:n], j == 0, j == NFF - 1, [WD, ACTB], [PZ])
                stt("dve", xv(oc, 1, n), pz[:, 0:n], modv(5, oc, s), xv(oc, 1, n), ALU.mult, ALU.add, [PZ, MODS[LCUR[0]], XBUF], [XBUF])

        _orig_req = W.req

        def req2(src, npart, kc, ncol, qn="pool", key=None):
            return _orig_req(src, npart, kc, ncol, qn, key)

        _orig_dma = P.dma

        def dma2(qn, out, in_, reads=(), writes=(), acc=False):
            if isinstance(in_, tuple):
                _, l_, oc_ = in_
                cs = slice(oc_ * 128, (oc_ + 1) * 128)
                _orig_dma(qn, out[0:64, 0, 0:1024].rearrange("p (h n) -> p h n", h=8),
                          wba_d[l_].rearrange("(h p) n -> p h n", p=64)[:, :, cs], reads, writes)
                _orig_dma(qn, out[:, 0, 1024:1280].rearrange("p (c n) -> p c n", c=2),
                          wbp_d[l_].rearrange("(c p) n -> p c n", p=128)[:, :, cs], reads, writes, acc=True)
                return _orig_dma(qn, out[:, 0, 1280:1536].rearrange("p (c n) -> p c n", c=2),
                                 wbf_d[l_].rearrange("(c p) n -> p c n", p=128)[:, :, cs], reads, writes, acc=True)
            return _orig_dma(qn, out, in_, reads, writes, acc)

        W.req = req2
        P.dma = dma2

        xcv = lambda kc, c0, m: xc[:, kc, 1 + c0:1 + c0 + m]
        xcv3 = lambda kc, c0, m: xc[:, kc, c0:c0 + m]
        xtv = lambda kc, c0, m: cur.xt[:, kc, c0:c0 + m]

        def program():
            last_out = []
            for l in range(depth_run):
                last = l == DEPTH - 1
                Xin, XIN = (fm(xT), XB["x0"]) if l == 0 else (fm(X1), XB["x1"])
                LCUR[0] = l
                if l == 0:
                    for l2 in range(depth_run):
                        mod_phase(l2)
                hv0 = lambda kc, c0, m: cur.h[:, kc, c0:c0 + m]

                def p12_prep(i):
                    setcur(i)
                    P.dma("pool", cur.xt[:, :, 0:TT], Xin[:, :, i * TT:(i + 1) * TT], reads=[XIN[i]], writes=[cur.XT])
                    norm(xtv, cur.XT, [(0, TT)], gmA(), 0, 0, hv0, cur.HB)

                def mk_hook(prep, i):
                    def hook():
                        if i + 1 < NTL:
                            prep(i + 1)
                            setcur(i)
                    return hook

                fuse_p1 = (not last) and (l + 1 < depth_run)
                if l == 0:
                    setcur(1)
                    p1_tile(l, "C", 0, CT, xcv, XC, 0)
                    p12_prep(0)
                    for i in range(NTL):
                        setcur(i)
                        p1_tile(l, "L", i * TT, TT, xtv, cur.XT, 0, do_norm=False, hook=mk_hook(p12_prep, i))
                fence()
                p12_prep(0)
                for i in range(NTL):
                    setcur(i)
                    p2_tile(l, "L", i * TT, TT, xtv, cur.XT, 0, do_norm=False, hook=mk_hook(p12_prep, i))
                    P.dma("pool", fm(XM)[:, :, i * TT:(i + 1) * TT], cur.xt[:, :, 0:TT], reads=[cur.XT], writes=[XB["xm"][i]])
                if not last:
                    setcur(NTL)
                    p2_tile(l, "C", 0, CT, xcv, XC, 0)
                    if l == 0:
                        dump("xcmid", xc[:, :, :], [128, 8, CT + 2], [XC])
                fence()

                def p3_prep(i):
                    setcur(i)
                    lo = i * TT - 1
                    zc = []
                    if i == 0:
                        P.dma("pool", cur.xt[:, :, 1:TT + 2], fm(XM)[:, :, 0:TT + 1], reads=[XB["xm"][0], XB["xm"][1]], writes=[cur.XT])
                        mset("pool", cur.xt[:, :, 0:1], 0.0, [cur.XT])
                        zc = [0]
                    elif i == NTL - 1:
                        P.dma("pool", cur.xt[:, :, 0:TT + 1], fm(XM)[:, :, lo:lo + TT + 1], reads=[XB["xm"][i - 1], XB["xm"][i]], writes=[cur.XT])
                        mset("pool", cur.xt[:, :, TT + 1:TT + 2], 0.0, [cur.XT])
                        zc = [TT + 1]
                    else:
                        P.dma("pool", cur.xt[:, :, 0:TT + 2], fm(XM)[:, :, lo:lo + TT + 2],
                              reads=[XB["xm"][i - 1], XB["xm"][i], XB["xm"][i + 1]], writes=[cur.XT])
                    norm(xtv, cur.XT, [(0, 512), (512, 2)], gmB(), 3, 0, hv0, cur.HB, zero_cols=zc)

                if not last:
                    setcur(NTL)
                    p3_tile(l, "C", 0, CT, xcv3, XC, [0, CT + 1])
                    if fuse_p1:
                        LCUR[0] = l + 1
                        setcur(NTL + 1)
                        p1_tile(l + 1, "C", 0, CT, xcv, XC, 0)
                        LCUR[0] = l
                p3_prep(0)
                for i in range(NTL):
                    setcur(i)
                    p3_tile(l, "L", i * TT, TT, xtv, cur.XT, [], do_norm=False, hook=mk_hook(p3_prep, i))
                    if last or depth_run == 1 and l == depth_run - 1:
                        if last:
                            xo = lambda kc, c0, m: cur.xt[:, kc, 1 + c0:1 + c0 + m]
                            norm(xo, cur.XT, [(0, TT)], nfin, None, 0, xo, cur.XT)
                            ev = P.dma("pool", fm(outT)[:, :, i * TT:(i + 1) * TT], cur.xt[:, :, 1:TT + 1], reads=[cur.XT], writes=[XB["out"][i]])
                        else:
                            ev = P.dma("pool", fm(outT)[:, :, i * TT:(i + 1) * TT], cur.xt[:, :, 1:TT + 1], reads=[cur.XT], writes=[XB["out"][i]])
                        last_out.append(ev)
                    else:
                        P.dma("pool", fm(X1)[:, :, i * TT:(i + 1) * TT], cur.xt[:, :, 1:TT + 1], reads=[cur.XT], writes=[XB["x1"][i]])
                        if fuse_p1:
                            LCUR[0] = l + 1
                            xo1 = lambda kc, c0, m: cur.xt[:, kc, 1 + c0:1 + c0 + m]
                            p1_tile(l + 1, "L", i * TT, TT, xo1, cur.XT, 0)
                            LCUR[0] = l
                if l == 0:
                    dump("xmid", fm(XM), [128, 8, T], XB["xm"])
                    dump("xc", xc[:, :, :], [128, 8, CT + 2], [XC])
            return last_out

        P.planning = True
        program()
        P.planning = False
        pspos[0] = 0
        setup()
        outs = program()
        for ev in outs:
            if ev is not None:
                P.final_wait("sp", ev)
        P.emit()
        print(f"[build] insts={P.n_inst} waits={P.n_wait} sems={P.nsem} wblocks={len(W.plan)}")
    if dbg:
        return nc, list(dumps.keys())
    return nc


def _host_consts():
    import ml_dtypes
    bf = ml_dtypes.bfloat16
    c = {}
    t = np.arange(T)
    row = (t // 64).astype(np.float32)
    col = (t % 64).astype(np.float32)
    inv_freq = (np.float32(10000.0) ** (-np.arange(16, dtype=np.float32) / np.float32(16))).astype(np.float32)
    C = np.zeros((128, T), np.float32)
    S = np.zeros((128, T), np.float32)
    for p in range(128):
        d = p % 64
        a, r, f = d // 32, (d % 32) // 16, d % 16
        pos = row if a == 0 else col
        ang = (pos * inv_freq[f]).astype(np.float32)
        C[p] = np.cos(ang)
        S[p] = np.sin(ang) * (-1.0 if r == 0 else 1.0)
    c["ropeC"], c["ropeS"] = C, S
    j = np.arange(128)[:, None]
    i = np.arange(128)[None, :]
    mp = (j >= i).astype(np.float32)
    mn = (j <= i).astype(np.float32)
    c["masks"] = np.concatenate([np.tile(mp, (1, 4)), np.tile(mn, (1, 4))], axis=1).astype(np.float32)
    for nm, N in (("", T), ("c", CT)):
        tt_ = np.arange(N, dtype=np.int64)
        k = (tt_[:, None] * tt_[None, :]) % N
        ang = 2.0 * np.pi * k.astype(np.float64) / N
        sc = 1.0 / np.sqrt(N * 64.0)
        ct = (np.cos(ang) * sc).astype(np.float32).astype(bf)
        st = (-np.sin(ang) * sc).astype(np.float32).astype(bf)
        if N == T:
            lay = lambda a: np.ascontiguousarray(a.reshape(4, 8, 128, T // TT, TT).transpose(3, 0, 2, 1, 4)).reshape(T // TT, 4, 128, 4096)
            ct, st = lay(ct), lay(st)
        c["ctab" + nm] = ct
        c["stab" + nm] = st
    cc = np.arange(64, dtype=np.int64)
    k = (cc[:, None] * cc[None, :]) % 64
    ang = 2.0 * np.pi * k.astype(np.float64) / 64
    C64, S64 = np.cos(ang), np.sin(ang)
    cs = np.zeros((128, 256), np.float32)
    for g in range(2):
        cs[g * 64:(g + 1) * 64, g * 64:(g + 1) * 64] = C64
        cs[g * 64:(g + 1) * 64, 128 + g * 64:128 + (g + 1) * 64] = S64
    c["cs64"] = cs
    wins = (2, 4, 8, 16)
    invw = np.zeros((128, 2), np.float32)
    rc = np.ones((128, 32), np.float32)
    for g, w in enumerate(wins):
        ch, half = g // 2, g % 2
        pr = slice(half * 64, (half + 1) * 64)
        invw[pr, ch] = 1.0 / w
        for e in range(8):
            cnt_first = min(e - w // 2 + w, 10 ** 9) - max(e - w // 2, 0)
            rc[pr, ch * 16 + e] = 1.0 / cnt_first
            cnt_last = min(8 - e + w // 2, w)
            rc[pr, ch * 16 + 8 + e] = 1.0 / cnt_last
    c["invw"], c["rcnt"] = invw, rc
    return c


_CONSTS = None
_NC = None
_DBG_HOOK = None


def _fmv(v):
    return np.ascontiguousarray(v.reshape(-1, 128).T)


def kernel(x, c, ctx, c_ctx, w_mod, b_mod, norm_mix, norm_ffn, w_in, attn_sink, pool_w, pool_scale,
           w_br_attn, w_br_pool, w_br_four, w_out, w_up, conv_w, w_down, norm_final):
    global _CONSTS, _NC
    f32 = np.float32
    A = lambda a: np.ascontiguousarray(np.asarray(a, dtype=f32))
    x, c, ctx, c_ctx = A(x), A(c), A(ctx), A(c_ctx)
    w_mod, b_mod, norm_mix, norm_ffn, w_in = A(w_mod), A(b_mod), A(norm_mix), A(norm_ffn), A(w_in)
    attn_sink, pool_w, pool_scale = A(attn_sink), A(pool_w), A(pool_scale)
    w_br_attn, w_br_pool, w_br_four, w_out = A(w_br_attn), A(w_br_pool), A(w_br_four), A(w_out)
    w_up, conv_w, w_down, norm_final = A(w_up), A(conv_w), A(w_down), A(norm_final)
    if _CONSTS is None:
        _CONSTS = _host_consts()
    if _NC is None:
        _NC = build_program()
    K = _CONSTS
    d = np.arange(64)
    swap = (d // 32) * 32 + (1 - (d % 32) // 16) * 16 + d % 16
    k_cols = np.arange(0, 128)
    kp_cols = np.concatenate([hh * 64 + swap for hh in range(2)])
    v_cols = np.arange(128, 256)
    u_cols = np.arange(768, 1024)
    f_cols = np.arange(1024, 1280)
    q_cols = np.concatenate([np.concatenate([256 + j * 64 + d, 256 + (4 + j) * 64 + d]) for j in range(4)])
    qp_cols = np.concatenate([np.concatenate([256 + j * 64 + swap, 256 + (4 + j) * 64 + swap]) for j in range(4)])
    g_cols = np.concatenate([np.concatenate([1280 + b * 1024 + oc * 128 + np.arange(128) for b in range(3)]) for oc in range(8)])
    cols = np.concatenate([k_cols, kp_cols, v_cols, u_cols, f_cols, q_cols, qp_cols, g_cols])
    assert cols.shape[0] == INW2
    w_in2 = np.ascontiguousarray(w_in[:, :, cols])
    bmod = np.concatenate([np.repeat(_fmv(b_mod[l])[:, :, None], 2, axis=2).reshape(128, 96) for l in range(DEPTH)], axis=1)
    nmix = np.concatenate([np.repeat(_fmv(norm_mix[l])[:, :, None], 2, axis=2).reshape(128, 16) for l in range(DEPTH)], axis=1)
    nffn = np.concatenate([np.repeat(_fmv(norm_ffn[l])[:, :, None], 2, axis=2).reshape(128, 16) for l in range(DEPTH)], axis=1)
    nfin = _fmv(norm_final)
    pscale = np.concatenate([_fmv(pool_scale[l]) for l in range(DEPTH)], axis=1)
    convw = np.concatenate([_fmv(conv_w[l, k]) for l in range(DEPTH) for k in range(3)], axis=1)
    sinkx = np.zeros((DEPTH * 2, 512), f32)
    for l in range(DEPTH):
        for hk in range(2):
            sinkx[l * 2 + hk] = np.repeat(attn_sink[l, hk * 4:(hk + 1) * 4], 128)
    pwbd = np.zeros((DEPTH * 2, 128, 128), f32)
    for l in range(DEPTH):
        for g in range(4):
            ch, half = g // 2, g % 2
            pwbd[l * 2 + ch, half * 64:(half + 1) * 64, half * 64:(half + 1) * 64] = pool_w[l, g]
    selm = np.zeros((4, 256), f32)
    for r in range(4):
        selm[r, r * 64:(r + 1) * 64] = 1.0
    shared = {
        "sel": selm,
        "w_mod": w_mod, "bmod": np.ascontiguousarray(bmod), "nmix": np.ascontiguousarray(nmix),
        "nffn": np.ascontiguousarray(nffn), "nfin": nfin, "w_in2": w_in2, "sinkx": sinkx, "pwbd": pwbd,
        "pscale": np.ascontiguousarray(pscale), "w_br_attn": w_br_attn, "w_br_pool": w_br_pool,
        "w_br_four": w_br_four, "w_out": w_out, "w_up": w_up, "convw": np.ascontiguousarray(convw),
        "w_down": w_down, "ropeC": K["ropeC"], "ropeS": K["ropeS"], "masks": K["masks"],
        "ctab": K["ctab"], "stab": K["stab"], "ctabc": K["ctabc"], "stabc": K["stabc"], "cs64": K["cs64"],
        "rcnt": K["rcnt"], "invw": K["invw"],
    }
    in_maps = []
    for b in range(NCORES):
        cv = np.stack([_fmv(c[b]), _fmv(c_ctx)], axis=2).reshape(128, 16)
        m = dict(shared)
        m["xT"] = np.ascontiguousarray(x[b].T)
        m["ctxT"] = np.ascontiguousarray(ctx[b].T)
        m["cvec"] = np.ascontiguousarray(cv)
        in_maps.append(m)
    if _DBG_HOOK is not None:
        return _DBG_HOOK(in_maps)
    res = run_bass_kernel_spmd(_NC, in_maps, core_ids=list(range(NCORES)))
    out = np.stack([np.ascontiguousarray(res.results[b]["outT"].T) for b in range(NCORES)], axis=0)
    return out.astype(np.float32)
```

```python
import numpy as np
from contextlib import ExitStack
import concourse.bass as bass
import concourse.mybir as mybir
from concourse.bass_utils import run_bass_kernel_spmd

F32 = mybir.dt.float32
BF16 = mybir.dt.bfloat16
AF = mybir.ActivationFunctionType
ALU = mybir.AluOpType

D = 1024
T = 4096
CT = 256
DEPTH = 2
DFF = 2816
NFF = 22
INW2 = 4992
TT = 512
NCORES = 4
EPS = 1e-6


class Buf:
    __slots__ = ("name", "w", "r")

    def __init__(self, name):
        self.name = name
        self.w = []
        self.r = []


class Prog:
    ENG = ("pe", "act", "dve", "pool", "sp")
    SEM_ROLL = 30000
    NDMASEM = 12

    def __init__(self, nc, es):
        self.nc = nc
        self.es = es
        self.planning = False
        self.q = {e: [] for e in self.ENG}
        self.esem = {}
        self.ecnt = {}
        self.waited = {e: {} for e in self.ENG}
        self.nsem = 0
        for e in self.ENG:
            self._new_esem(e)
        self.dsem = {}
        self.dpos = {}
        for qn in ("sp", "act", "pool"):
            self.dsem[qn] = [[self._sem(f"d_{qn}_{i}"), 0] for i in range(self.NDMASEM)]
            self.dpos[qn] = 0
        self.n_inst = 0
        self.n_wait = 0

    def _sem(self, name):
        self.nsem += 1
        return self.es.enter_context(self.nc.semaphore(f"{name}_{self.nsem}"))

    def _new_esem(self, e):
        self.esem[e] = self._sem(f"e_{e}")
        self.ecnt[e] = 0

    def _wait(self, eng, ev):
        if ev is None:
            return
        sem, val, src = ev
        if src == "pe" and eng == "pe":
            return
        k = id(sem)
        if self.waited[eng].get(k, 0) >= val:
            return
        self.waited[eng][k] = val
        self.q[eng].append(("w", sem, val))
        self.n_wait += 1

    def _deps(self, eng, reads, writes, acc=False):
        for b in reads:
            for ev in b.w:
                self._wait(eng, ev)
        for b in writes:
            if not acc:
                for ev in b.w:
                    self._wait(eng, ev)
            for ev in b.r:
                self._wait(eng, ev)

    def _commit(self, ev, reads, writes, acc=False):
        for b in reads:
            if len(b.r) > 48:
                d = {}
                for e2 in b.r:
                    k = id(e2[0])
                    if k not in d or d[k][1] < e2[1]:
                        d[k] = e2
                b.r = list(d.values())
            b.r.append(ev)
        for b in writes:
            if acc:
                b.w.append(ev)
            else:
                b.w = [ev]
            b.r = []

    def op(self, eng, fn, reads=(), writes=()):
        if self.planning:
            return None
        self._deps(eng, reads, writes)
        if self.ecnt[eng] >= self.SEM_ROLL:
            self._new_esem(eng)
        self.ecnt[eng] += 1
        sem = self.esem[eng]
        val = self.ecnt[eng]
        self.q[eng].append(("o", fn, sem))
        ev = (sem, val, eng)
        self._commit(ev, reads, writes)
        self.n_inst += 1
        return ev

    def dma(self, qn, out, in_, reads=(), writes=(), acc=False):
        if self.planning:
            return None
        self._deps(qn, reads, writes, acc)
        pos = self.dpos[qn]
        self.dpos[qn] = (pos + 1) % self.NDMASEM
        slot = self.dsem[qn][pos]
        sem, cnt = slot
        if cnt > 0:
            self._wait(qn, (sem, cnt, "dma"))
        if cnt >= self.SEM_ROLL:
            sem = self._sem(f"d_{qn}")
            cnt = 0
            slot[0] = sem
        cnt += 16
        slot[1] = cnt
        self.q[qn].append(("d", out, in_, sem))
        ev = (sem, cnt, "dma")
        self._commit(ev, reads, writes, acc)
        self.n_inst += 1
        return ev

    def final_wait(self, eng, ev):
        self.q[eng].append(("w", ev[0], ev[1]))

    def emit(self):
        nc = self.nc
        q = self.q

        def replay(e, lst):
            for it in lst:
                if it[0] == "w":
                    e.wait_ge(it[1], it[2])
                elif it[0] == "o":
                    it[1](e).then_inc(it[2], 1)
                else:
                    e.dma_start(out=it[1], in_=it[2]).then_inc(it[3], 16)

        with nc.Block() as block:
            @block.sync
            def _(e):
                replay(e, q["sp"])

            @block.tensor
            def _(e):
                replay(e, q["pe"])

            @block.scalar
            def _(e):
                replay(e, q["act"])

            @block.vector
            def _(e):
                replay(e, q["dve"])

            @block.gpsimd
            def _(e):
                replay(e, q["pool"])


class WPool:
    SLOT = 4096

    def __init__(self, P, nc, es, nslots=5, ahead=3, scratch=None):
        self.P = P
        self.n = nslots
        self.ahead = ahead
        self.scratch = scratch
        self.tiles = [es.enter_context(nc.sbuf_tensor(f"wslot{i}", [128, self.SLOT], BF16)) for i in range(nslots)]
        self.bufs = [Buf(f"wslot{i}") for i in range(nslots)]
        self.plan = []
        self.issued = 0
        self.cur = 0
        self.keys = {}

    def _view(self, i):
        src, npart, kc, ncol, qn, key = self.plan[i]
        t = self.tiles[i % self.n]
        return t[0:npart, 0:kc * ncol].rearrange("p (k n) -> p k n", k=kc)

    def _issue(self, i):
        src, npart, kc, ncol, qn, key = self.plan[i]
        sbuf_ = self.bufs[i % self.n]
        flat = self.tiles[i % self.n][0:npart, 0:kc * ncol]
        if key is None or self.scratch is None:
            self.P.dma(qn, self._view(i), src, writes=[sbuf_])
        elif key not in self.keys:
            k = len(self.keys)
            SB = Buf(f"scr{k}")
            self.keys[key] = (k, SB)
            self.P.dma(qn, self._view(i), src, writes=[sbuf_])
            self.P.dma("sp", self.scratch[k, 0:npart, 0:kc * ncol], flat, reads=[sbuf_], writes=[SB])
        else:
            k, SB = self.keys[key]
            self.P.dma("sp", flat, self.scratch[k, 0:npart, 0:kc * ncol], reads=[SB], writes=[sbuf_])

    def req(self, src, npart, kc, ncol, qn="pool", key=None):
        assert kc * ncol <= self.SLOT
        if self.P.planning:
            self.plan.append((src, npart, kc, ncol, qn, key))
            return None, None
        while self.issued < min(len(self.plan), self.cur + self.ahead + 1):
            self._issue(self.issued)
            self.issued += 1
        i = self.cur
        self.cur += 1
        assert self.plan[i][1:] == (npart, kc, ncol, qn, key), (i, self.plan[i][1:], (npart, kc, ncol, qn, key))
        return self._view(i), self.bufs[i % self.n]


def build_program(depth_run=DEPTH, dbg=False):
    nc = bass.Bass("TRN2", target_bir_lowering=False)
    es = ExitStack()
    with es:
        P = Prog(nc, es)

        def dram(name, shape, dt, kind):
            return nc.dram_tensor(name, shape, dt, kind=kind).ap()

        def sb(name, shape, dt=F32):
            return es.enter_context(nc.sbuf_tensor(name, shape, dt))

        xT = dram("xT", [D, T], F32, "ExternalInput")
        ctxT = dram("ctxT", [D, CT], F32, "ExternalInput")
        cvec_d = dram("cvec", [128, 16], F32, "ExternalInput")
        w_mod_d = dram("w_mod", [DEPTH, D, 6 * D], F32, "ExternalInput")
        bmod_d = dram("bmod", [128, DEPTH * 96], F32, "ExternalInput")
        nmix_d = dram("nmix", [128, DEPTH * 16], F32, "ExternalInput")
        nffn_d = dram("nffn", [128, DEPTH * 16], F32, "ExternalInput")
        nfin_d = dram("nfin", [128, 8], F32, "ExternalInput")
        w_in_d = dram("w_in2", [DEPTH, D, INW2], F32, "ExternalInput")
        sink_d = dram("sinkx", [DEPTH * 2, 512], F32, "ExternalInput")
        sel_d = dram("sel", [4, 256], F32, "ExternalInput")
        pwbd_d = dram("pwbd", [DEPTH * 2, 128, 128], F32, "ExternalInput")
        pscale_d = dram("pscale", [128, DEPTH * 2], F32, "ExternalInput")
        wba_d = dram("w_br_attn", [DEPTH, 512, D], F32, "ExternalInput")
        wbp_d = dram("w_br_pool", [DEPTH, 256, D], F32, "ExternalInput")
        wbf_d = dram("w_br_four", [DEPTH, 256, D], F32, "ExternalInput")
        wout_d = dram("w_out", [DEPTH, D, D], F32, "ExternalInput")
        wup_d = dram("w_up", [DEPTH, D, 2 * DFF], F32, "ExternalInput")
        convw_d = dram("convw", [128, DEPTH * 3 * 44], F32, "ExternalInput")
        wdn_d = dram("w_down", [DEPTH, DFF, D], F32, "ExternalInput")
        ropeC_d = dram("ropeC", [128, T], F32, "ExternalInput")
        ropeS_d = dram("ropeS", [128, T], F32, "ExternalInput")
        mask_d = dram("masks", [128, 1024], F32, "ExternalInput")
        ctab_d = dram("ctab", [T // TT, 4, 128, 4096], BF16, "ExternalInput")
        stab_d = dram("stab", [T // TT, 4, 128, 4096], BF16, "ExternalInput")
        ctabc_d = dram("ctabc", [CT, CT], BF16, "ExternalInput")
        stabc_d = dram("stabc", [CT, CT], BF16, "ExternalInput")
        cs64_d = dram("cs64", [128, 256], F32, "ExternalInput")
        rcnt_d = dram("rcnt", [128, 32], F32, "ExternalInput")
        invw_d = dram("invw", [128, 2], F32, "ExternalInput")
        outT = dram("outT", [D, T], F32, "ExternalOutput")
        XM = dram("xmid", [D, T], F32, "Internal")
        X1 = dram("x1s", [D, T], F32, "Internal")
        UL = dram("u_lat", [256, T + 16], F32, "Internal")
        UC = dram("u_ctx", [256, CT + 16], F32, "Internal")

        def fm(ap):
            return ap.rearrange("(k p) t -> p k t", p=128)

        NTL = T // TT
        XB = {"x0": [Buf(f"x0_{i}") for i in range(NTL)], "xm": [Buf(f"xm_{i}") for i in range(NTL)],
              "x1": [Buf(f"x1_{i}") for i in range(NTL)], "out": [Buf(f"o_{i}") for i in range(NTL)]}
        ULB = Buf("UL")
        UCB = Buf("UC")

        WSCR = dram("wscr", [96, 128, 4096], BF16, "Internal")
        W = WPool(P, nc, es, nslots=4, ahead=2, scratch=WSCR)
        ones = sb("ones", [128, 128], BF16)
        ONES = Buf("ones")
        kT = sb("kT", [128, T], BF16)
        KT = Buf("kT")
        V = sb("V", [128, T // 128, 128], BF16)
        VB = Buf("V")
        Ftok = sb("Ftok", [128, T // 128, 256], BF16)
        FB = Buf("Ftok")
        kcT = sb("kcT", [128, CT], BF16)
        KCT = Buf("kcT")
        Vc = sb("Vc", [128, 2, 128], BF16)
        VCB = Buf("Vc")
        Fc = sb("Fc", [128, 2, 256], BF16)
        FCB = Buf("Fc")
        xc = sb("xc", [128, 8, CT + 2], F32)
        XC = Buf("xc")
        cvec = sb("cvec_s", [128, 16], F32)
        csil = sb("csil", [128, 16], BF16)
        CV = Buf("cvec")
        MODS = [Buf("mod0"), Buf("mod1")]
        LCUR = [0]
        bmod = sb("bmod_s", [128, DEPTH * 96], F32)
        nmix = sb("nmix_s", [128, DEPTH * 16], F32)
        nffn = sb("nffn_s", [128, DEPTH * 16], F32)
        nfin = sb("nfin_s", [128, 8], F32)
        CONSTB = Buf("consts")
        esrow = sb("esrow", [4, 512], BF16)
        sel = sb("sel_s", [4, 256], BF16)
        pwbd = sb("pwbd_s", [128, DEPTH * 2, 128], BF16)
        pscale = sb("pscale_s", [128, DEPTH * 2], F32)
        convw = sb("convw_s", [128, DEPTH * 3 * 44], F32)
        masks = sb("masks_s", [128, 1024], BF16)
        cs64 = sb("cs64_s", [128, 256], BF16)
        rcnt = sb("rcnt_s", [128, 32], F32)
        invw = sb("invw_s", [128, 2], F32)
        class _Cur:
            pass
        cur = _Cur()
        _xts = [sb(f"xt{i}", [128, 8, TT + 2], F32) for i in range(2)]
        _XTs = [Buf(f"xt{i}") for i in range(2)]
        _hs = [sb(f"h{i}", [128, 8, TT + 2], BF16) for i in range(2)]
        _HBs = [Buf(f"h{i}") for i in range(2)]
        _flipc = [0]

        def flip():
            i = _flipc[0] % 2
            _flipc[0] += 1
            cur.xt, cur.XT, cur.h, cur.HB = _xts[i], _XTs[i], _hs[i], _HBs[i]
        def setcur(k):
            i = k % 2
            cur.xt, cur.XT, cur.h, cur.HB = _xts[i], _XTs[i], _hs[i], _HBs[i]
        flip()
        act = sb("act", [128, NFF, TT], BF16)
        ACTB = Buf("act")
        sq = [sb(f"sq{i}", [128, 512], BF16) for i in range(2)]
        SQ = [Buf(f"sq{i}") for i in range(2)]
        rstd = sb("rstd", [128, TT + 16], F32)
        RSTD = Buf("rstd")
        tmp = [sb(f"tmp{i}", [128, 512], F32) for i in range(3)]
        TMP = [Buf(f"tmp{i}") for i in range(3)]
        esrow_f = tmp[0][0:4, :]
        rope = sb("rope", [128, 2, 512], F32)
        ROPE = Buf("rope")
        qrot = act[:, 8:12, :]
        QR = Buf("qrot")
        oT = act[0:64, 12:20, :]
        OT = Buf("oT")
        PT = sb("PT", [128, 5, 512], BF16)
        PTB = [Buf(f"PT{i}") for i in range(5)]
        PTB2 = [Buf(f"PT2_{i}") for i in range(5)]
        ptc = [0]
        rden = tmp[2][0:64, :]
        RDEN = TMP[2]
        uloc = sb("uloc", [128, 2, TT + 16], F32)
        ULOC = Buf("uloc")
        s1 = sb("s1", [128, 2, TT + 16], F32)
        S1 = Buf("s1")
        s2 = sb("s2", [128, TT + 16], F32)
        S2 = Buf("s2")
        s4 = rstd
        S4 = RSTD
        win = sb("win", [128, 2, TT], F32)
        WIN = Buf("win")
        pooled = act[:, 20:22, :]
        PLD = Buf("pooled")
        ypool = sb("ypool", [128, 2, TT], BF16)
        YP = Buf("ypool")
        gT = sb("gT", [128, 4, TT], BF16)
        GT = Buf("gT")
        yfour = sb("yfour", [128, 2, TT], BF16)
        YF = Buf("yfour")
        sg = [sb(f"sg{i}", [128, 512], F32) for i in range(3)]
        SG = [Buf(f"sg{i}") for i in range(3)]
        y = act[:, 0:8, :]
        YB = Buf("y")
        sgB = [sb(f"sgB{i}", [128, 512], F32) for i in range(3)]
        SGB = [Buf(f"sgB{i}") for i in range(3)]
        cvsets = [(sg, SG), (sgB, SGB)]
        tvs = [[sb(f"tv{a}{b}", [128, 8], F32) for b in range(3)] for a in range(2)]
        TVS = [[Buf(f"tv{a}{b}") for b in range(3)] for a in range(2)]
        tvsets = [(tvs[0], TVS[0]), (tvs[1], TVS[1])]

        def fence():
            P.op("pool", lambda e: e.memset(cvec[:, 0:1], 0.0), [CV], [ACTB, QR, OT, YB, PLD])
        ps = [es.enter_context(nc.psum_tensor(f"ps{i}", [128, 512], F32)) for i in range(8)]
        PS = [Buf(f"ps{i}") for i in range(8)]
        pspos = [0]

        def next_ps():
            i = pspos[0]
            pspos[0] = (i + 1) % 8
            return ps[i], PS[i]

        dumps = {}

        def dump(name, src, shape, reads):
            if not dbg or P.planning or name in dumps:
                return
            dumps[name] = dram("dbg_" + name, list(shape), F32, "ExternalOutput")
            P.dma("pool", dumps[name], src, reads=reads, writes=[Buf("dbg_" + name)])

        def mm(out, lhsT, rhs, start, stop, reads, writes):
            P.op("pe", lambda e: e.matmul(out, lhsT=lhsT, rhs=rhs, start=start, stop=stop), reads, writes)

        def actf(out, in_, func, reads, writes, bias=None, scale=None, eng="act"):
            kw = {}
            if bias is not None:
                kw["bias"] = bias
            if scale is not None:
                kw["scale"] = scale
            P.op("act", lambda e: e.activation(out=out, in_=in_, func=func, **kw), reads, writes)

        def tt(eng, out, in0, in1, op, reads, writes):
            P.op(eng, lambda e: e.tensor_tensor(out=out, in0=in0, in1=in1, op=op), reads, writes)

        def stt(eng, out, in0, scalar, in1, op0, op1, reads, writes):
            P.op(eng, lambda e: e.scalar_tensor_tensor(out=out, in0=in0, scalar=scalar, in1=in1, op0=op0, op1=op1), reads, writes)

        def ts(eng, out, in0, s1_, s2_, op0, op1, reads, writes):
            if op1 is None:
                P.op(eng, lambda e: e.tensor_scalar(out=out, in0=in0, scalar1=s1_, scalar2=None, op0=op0), reads, writes)
            else:
                P.op(eng, lambda e: e.tensor_scalar(out=out, in0=in0, scalar1=s1_, scalar2=s2_, op0=op0, op1=op1), reads, writes)

        def cp(eng, out, in_, reads, writes):
            P.op(eng, lambda e: e.tensor_copy(out=out, in_=in_), reads, writes)

        def mset(eng, ap, val, writes):
            P.op(eng, lambda e: e.memset(ap, val), (), writes)

        def setup():
            mset("pool", ones[:, :], 1.0, [ONES])
            P.dma("sp", cvec[:, :], cvec_d[:, :], writes=[CV])
            P.dma("sp", bmod[:, :], bmod_d[:, :], writes=[CONSTB])
            P.dma("sp", nmix[:, :], nmix_d[:, :], writes=[CONSTB])
            P.dma("sp", nffn[:, :], nffn_d[:, :], writes=[CONSTB])
            P.dma("sp", nfin[:, :], nfin_d[:, :], writes=[CONSTB])
            P.dma("sp", pscale[:, :], pscale_d[:, :], writes=[CONSTB])
            P.dma("sp", convw[:, :], convw_d[:, :], writes=[CONSTB])
            P.dma("sp", rcnt[:, :], rcnt_d[:, :], writes=[CONSTB])
            P.dma("sp", invw[:, :], invw_d[:, :], writes=[CONSTB])
            P.dma("sp", esrow_f[:, :], sink_d[:, :], writes=[TMP[0]])
            P.dma("pool", sel[:, :], sel_d[:, :], writes=[CONSTB])
            P.dma("pool", pwbd[:, :, :], pwbd_d.rearrange("a p n -> p a n"), writes=[CONSTB])
            P.dma("pool", masks[:, :], mask_d[:, :], writes=[CONSTB])
            P.dma("pool", cs64[:, :], cs64_d[:, :], writes=[CONSTB])
            actf(esrow[:, :], esrow_f[:, :], AF.Exp, [CONSTB, TMP[0]], [CONSTB])
            actf(csil[:, :], cvec[:, :], AF.Silu, [CV], [CV])
            mset("pool", xc[:, :, :], 0.0, [XC])
            P.dma("sp", xc[:, :, 1:CT + 1], fm(ctxT), writes=[XC])
            mset("pool", s1[:, :, :], 0.0, [S1])
            for (U, UB, n) in ((UL, ULB, T), (UC, UCB, CT)):
                Uv = U.rearrange("(c p) t -> p c t", p=128)
                P.dma("sp", Uv[:, :, 0:8], s1[:, :, 0:8], reads=[S1], writes=[UB])
                P.dma("sp", Uv[:, :, n + 8:n + 16], s1[:, :, 0:8], reads=[S1], writes=[UB])

        def mod_phase(l):
            pm, PM = next_ps()
            for blk in range(12):
                wv, WB = W.req(w_mod_d[l].rearrange("(k p) n -> p k n", p=128)[:, :, blk * 512:(blk + 1) * 512], 128, 8, 512)
                for j in range(4):
                    oc = blk * 4 + j
                    for kc in range(8):
                        mm(pm[:, oc * 2:oc * 2 + 2], wv[:, kc, j * 128:(j + 1) * 128] if wv is not None else None,
                           csil[:, kc * 2:kc * 2 + 2], kc == 0, kc == 7, [WB, CV], [PM])
            MOD = MODS[l]
            mT = bmod[:, l * 96:(l + 1) * 96]
            tt("dve", mT, pm[:, 0:96], mT, ALU.add, [PM, CONSTB], [MOD])
            stt("dve", nmix[:, l * 16:(l + 1) * 16], mT[:, 16:32], 1.0, nmix[:, l * 16:(l + 1) * 16], ALU.add, ALU.mult, [MOD, CONSTB], [MOD])
            stt("dve", nffn[:, l * 16:(l + 1) * 16], mT[:, 64:80], 1.0, nffn[:, l * 16:(l + 1) * 16], ALU.add, ALU.mult, [MOD, CONSTB], [MOD])
            if l == 0:
                dump("modT", mT, [128, 96], [MOD])

        def modv(idx, kc, s):
            c = LCUR[0] * 96 + (idx * 8 + kc) * 2 + s
            return bmod[:, c:c + 1]

        def gmA():
            return nmix[:, LCUR[0] * 16:(LCUR[0] + 1) * 16]

        def gmB():
            return nffn[:, LCUR[0] * 16:(LCUR[0] + 1) * 16]

        def norm(xv, XBUF, ranges, gmt, shidx, s, hv, HBUF, zero_cols=()):
            for (c0, n) in ranges:
                pn, PN = next_ps()
                for kc in range(8):
                    k2 = kc % 2
                    tt("pool", sq[k2][:, 0:n], xv(kc, c0, n), xv(kc, c0, n), ALU.mult, [XBUF], [SQ[k2]])
                    mm(pn[:, 0:n], ones[:, :], sq[k2][:, 0:n], kc == 0, kc == 7, [ONES, SQ[k2]], [PN])
                actf(rstd[:, 0:n], pn[:, 0:n], AF.Sqrt, [PN], [RSTD], bias=EPS, scale=1.0 / D)
                P.op("dve", lambda e, n=n: e.reciprocal(out=rstd[:, 0:n], in_=rstd[:, 0:n]), [RSTD], [RSTD])
                for kc in range(8):
                    k3 = kc % 3
                    tt("dve", tmp[k3][:, 0:n], xv(kc, c0, n), rstd[:, 0:n], ALU.mult, [XBUF, RSTD], [TMP[k3]])
                    if shidx is None:
                        actf(hv(kc, c0, n), tmp[k3][:, 0:n], AF.Identity, [TMP[k3], CONSTB], [HBUF], scale=gmt[:, kc:kc + 1])
                    else:
                        actf(hv(kc, c0, n), tmp[k3][:, 0:n], AF.Identity, [TMP[k3], MODS[LCUR[0]]], [HBUF],
                             bias=modv(shidx, kc, s), scale=gmt[:, kc * 2 + s:kc * 2 + s + 1])
            for c in zero_cols:
                mset("pool", hv(0, c, 1).rearrange("p n -> p n") if False else cur.h[:, :, c:c + 1], 0.0, [HBUF])

        def p1_tile(l, S, n0, n, xv, XBUF, hoff, do_norm=True, hook=None):
            lat = S == "L"
            s = 0 if lat else 1
            hv = lambda kc, c0, m: cur.h[:, kc, hoff + c0:hoff + c0 + m]
            if do_norm:
                norm(xv, XBUF, [(0, n)], gmA(), 0, s, hv, cur.HB)
            if hook is not None:
                hook()
            if l == 0 and lat and n0 == 0:
                dump("h", cur.h[:, :, 0:n], [128, 8, n], [cur.HB])
            win_ = w_in_d[l].rearrange("(k p) n -> p k n", p=128)
            wa, WA = W.req(win_[:, :, 0:384], 128, 8, 384, key=("p1a", l))
            wb, WBb = W.req(win_[:, :, 384:896], 128, 8, 512, key=("p1b", l))
            hh = lambda kc: cur.h[:, kc, hoff:hoff + n]
            pk, PK = next_ps()
            for kc in range(8):
                mm(pk[:, 0:n], wa[:, kc, 0:128] if wa is not None else None, hh(kc), kc == 0, kc == 7, [WA, cur.HB], [PK])
            if lat:
                pkp, PKP = next_ps()
                for kc in range(8):
                    mm(pkp[:, 0:n], wa[:, kc, 128:256] if wa is not None else None, hh(kc), kc == 0, kc == 7, [WA, cur.HB], [PKP])
                P.dma("pool", rope[:, 0, 0:n], ropeC_d[:, n0:n0 + n], writes=[ROPE])
                P.dma("pool", rope[:, 1, 0:n], ropeS_d[:, n0:n0 + n], writes=[ROPE])
                tt("dve", tmp[0][:, 0:n], pk[:, 0:n], rope[:, 0, 0:n], ALU.mult, [PK, ROPE], [TMP[0]])
                tt("dve", tmp[1][:, 0:n], pkp[:, 0:n], rope[:, 1, 0:n], ALU.mult, [PKP, ROPE], [TMP[1]])
                tt("pool", kT[:, n0:n0 + n], tmp[0][:, 0:n], tmp[1][:, 0:n], ALU.add, [TMP[0], TMP[1]], [KT])
            else:
                actf(kcT[:, 0:n], pk[:, 0:n], AF.Copy, [PK], [KCT])
            pv, PV = next_ps()
            nb = n // 128
            for b in range(nb):
                for kc in range(8):
                    mm(pv[:, b * 128:(b + 1) * 128], cur.h[:, kc, hoff + b * 128:hoff + (b + 1) * 128],
                       wa[:, kc, 256:384] if wa is not None else None, kc == 0, kc == 7, [WA, cur.HB], [PV])
            if lat:
                actf(V[:, n0 // 128:n0 // 128 + nb, :], pv[:, 0:n].rearrange("p (b c) -> p b c", b=nb), AF.Copy, [PV], [VB])
            else:
                actf(Vc[:, 0:nb, :], pv[:, 0:n].rearrange("p (b c) -> p b c", b=nb), AF.Copy, [PV], [VCB])
            if l == DEPTH - 1 and not lat:
                return
            for c in range(2):
                pu, PU = next_ps()
                for kc in range(8):
                    mm(pu[:, 0:n], wb[:, kc, c * 128:(c + 1) * 128] if wb is not None else None, hh(kc), kc == 0, kc == 7, [WBb, cur.HB], [PU])
                actf(uloc[:, c, 0:n], pu[:, 0:n], AF.Copy, [PU], [ULOC])
            U, UB = (UL, ULB) if lat else (UC, UCB)
            P.dma("pool", U.rearrange("(c p) t -> p c t", p=128)[:, :, 8 + n0:8 + n0 + n], uloc[:, :, 0:n], reads=[ULOC], writes=[UB])
            for b0 in range(0, nb, 2):
                pf, PF = next_ps()
                nbb = min(2, nb - b0)
                for b in range(nbb):
                    for kc in range(8):
                        mm(pf[:, b * 256:(b + 1) * 256], cur.h[:, kc, hoff + (b0 + b) * 128:hoff + (b0 + b + 1) * 128],
                           wb[:, kc, 256:512] if wb is not None else None, kc == 0, kc == 7, [WBb, cur.HB], [PF])
                src = pf[:, 0:nbb * 256].rearrange("p (b c) -> p b c", b=nbb)
                if lat:
                    cp("dve", Ftok[:, n0 // 128 + b0:n0 // 128 + b0 + nbb, :], src, [PF], [FB])
                else:
                    cp("dve", Fc[:, b0:b0 + nbb, :], src, [PF], [FCB])

        def p2_tile(l, S, t0, n, xv, XBUF, hoff, do_norm=True, hook=None):
            lat = S == "L"
            s = 0 if lat else 1
            Tn = T if lat else CT
            dd = l == 0 and lat and t0 == 0
            hv = lambda kc, c0, m: cur.h[:, kc, hoff + c0:hoff + c0 + m]
            hh = lambda kc: cur.h[:, kc, hoff:hoff + n]
            if do_norm:
                norm(xv, XBUF, [(0, n)], gmA(), 0, s, hv, cur.HB)
            win_ = w_in_d[l].rearrange("(k p) n -> p k n", p=128)
            wq, WQ = W.req(win_[:, :, 896:1408], 128, 8, 512, key=("q", l))
            if lat:
                wqp, WQP = W.req(win_[:, :, 1408:1920], 128, 8, 512, key=("qp", l))
                P.dma("pool", rope[:, 0, 0:n], ropeC_d[:, t0:t0 + n], writes=[ROPE])
                P.dma("pool", rope[:, 1, 0:n], ropeS_d[:, t0:t0 + n], writes=[ROPE])
            for j in range(4):
                pq, PQ = next_ps()
                for kc in range(8):
                    mm(pq[:, 0:n], wq[:, kc, j * 128:(j + 1) * 128] if wq is not None else None, hh(kc), kc == 0, kc == 7, [WQ, cur.HB], [PQ])
                if lat:
                    pqp, PQP = next_ps()
                    for kc in range(8):
                        mm(pqp[:, 0:n], wqp[:, kc, j * 128:(j + 1) * 128] if wqp is not None else None, hh(kc), kc == 0, kc == 7, [WQP, cur.HB], [PQP])
                    if j % 2 == 0:
                        ta, TA, tb, TBb = tmp[0], TMP[0], tmp[1], TMP[1]
                    else:
                        ta, TA, tb, TBb = sgB[0], SGB[0], sgB[1], SGB[1]
                    tt("dve", ta[:, 0:n], pq[:, 0:n], rope[:, 0, 0:n], ALU.mult, [PQ, ROPE], [TA])
                    tt("dve", tb[:, 0:n], pqp[:, 0:n], rope[:, 1, 0:n], ALU.mult, [PQP, ROPE], [TBb])
                    tt("pool", qrot[:, j, 0:n], ta[:, 0:n], tb[:, 0:n], ALU.add, [TA, TBb], [QR])
                else:
                    actf(qrot[:, j, 0:n], pq[:, 0:n], AF.Copy, [PQ], [QR])
            U, UB = (UL, ULB) if lat else (UC, UCB)
            L_ = n + 16
            P.dma("pool", uloc[:, :, 0:L_], U.rearrange("(c p) t -> p c t", p=128)[:, :, t0:t0 + L_], reads=[UB], writes=[ULOC])
            tt("pool", s1[:, :, 1:L_], uloc[:, :, 1:L_], uloc[:, :, 0:L_ - 1], ALU.add, [ULOC], [S1])
            tt("pool", s2[:, 3:L_], s1[:, 1, 3:L_], s1[:, 1, 1:L_ - 2], ALU.add, [S1], [S2])
            tt("pool", s4[64:128, 7:L_], s2[64:128, 7:L_], s2[64:128, 3:L_ - 4], ALU.add, [S2], [S4])
            cp("pool", win[0:64, 0, 0:n], s1[0:64, 0, 8:8 + n], [S1], [WIN])
            tt("pool", win[64:128, 0, 0:n], s1[64:128, 0, 7:7 + n], s1[64:128, 0, 9:9 + n], ALU.add, [S1], [WIN])
            tt("pool", win[0:64, 1, 0:n], s2[0:64, 7:7 + n], s2[0:64, 11:11 + n], ALU.add, [S2], [WIN])
            tt("pool", win[64:128, 1, 0:n], s4[64:128, 7:7 + n], s4[64:128, 15:15 + n], ALU.add, [S4], [WIN])
            for c in range(2):
                stt("dve", pooled[:, c, 0:n], win[:, c, 0:n], invw[:, c:c + 1], uloc[:, c, 8:8 + n], ALU.mult, ALU.subtract,
                    [WIN, CONSTB, ULOC], [PLD])
                if t0 == 0:
                    tt("pool", s2[:, 0:8], win[:, c, 0:8], rcnt[:, c * 16:c * 16 + 8], ALU.mult, [WIN, CONSTB, S2], [S2])
                    tt("pool", pooled[:, c, 0:8], s2[:, 0:8], uloc[:, c, 8:16], ALU.subtract, [S2, ULOC], [PLD])
                if t0 + n == Tn:
                    tt("pool", s2[:, 0:8], win[:, c, n - 8:n], rcnt[:, c * 16 + 8:c * 16 + 16], ALU.mult, [WIN, CONSTB, S2], [S2])
                    tt("pool", pooled[:, c, n - 8:n], s2[:, 0:8], uloc[:, c, n:n + 8], ALU.subtract, [S2, ULOC], [PLD])

            def pool_mm():
                for c in range(2):
                    pp, PP = next_ps()
                    mm(pp[:, 0:n], pwbd[:, l * 2 + c, :], pooled[:, c, 0:n], True, True, [CONSTB, PLD], [PP])
                    actf(ypool[:, c, 0:n], pp[:, 0:n], AF.Identity, [PP, CONSTB], [YP], scale=pscale[:, l * 2 + c:l * 2 + c + 1])
            nqb = n // 128

            def att_front(qb, hk):
                gq = (t0 // 128) + qb
                pr = slice(hk * 64, (hk + 1) * 64)
                chunks = []
                if lat:
                    for d_, mi in ((-1, 0), (0, None), (1, 1)):
                        kb = gq + d_
                        if 0 <= kb < T // 128:
                            chunks.append((kT[pr, kb * 128:(kb + 1) * 128], V[:, kb, hk * 64:(hk + 1) * 64], mi, [KT, VB]))
                for cb in range(2):
                    chunks.append((kcT[pr, cb * 128:(cb + 1) * 128], Vc[:, cb, hk * 64:(hk + 1) * 64], None, [KCT, VCB]))
                rhs_q = qrot[pr, :, qb * 128:(qb + 1) * 128]
                if ptc[0] % 2 == 0:
                    PTv = [PT[:, ci_, :] for ci_ in range(5)]
                    PTb = PTB
                else:
                    PTv = [gT[:, 0, :], gT[:, 1, :], gT[:, 2, :], gT[:, 3, :], yfour[:, 0, :]]
                    PTb = PTB2
                ptc[0] += 1
                for ci, (kap, vap, mi, bb) in enumerate(chunks):
                    pS, PSb = next_ps()
                    mm(pS[:, :].rearrange("p (g q) -> p g q", g=4), kap, rhs_q, True, True, [bb[0], QR], [PSb])
                    actf(PTv[ci], pS[:, :], AF.Exp, [PSb], [PTb[ci]], scale=0.125)
                    if mi is not None:
                        tt("pool", PTv[ci], PTv[ci], masks[:, mi * 512:(mi + 1) * 512], ALU.mult, [PTb[ci], CONSTB], [PTb[ci]])
                return chunks, PTv, PTb

            def att_back(qb, hk, st):
                chunks, PTv, PTb = st
                po, PO = next_ps()
                pd, PD = next_ps()
                nc_ = len(chunks)
                for ci, (kap, vap, mi, bb) in enumerate(chunks):
                    mm(po[0:64, :], vap, PTv[ci], ci == 0, ci == nc_ - 1, [bb[1], PTb[ci]], [PO])
                for ci in range(nc_):
                    mm(pd[0:64, :], ones[:, 0:64], PTv[ci], ci == 0, False, [ONES, PTb[ci]], [PD])
                er = l * 2 + hk
                mm(pd[0:64, :], sel[0:4, er * 64:(er + 1) * 64], esrow[0:4, :], False, True, [CONSTB], [PD])
                P.op("dve", lambda e, pd=pd: e.reciprocal(out=rden[:, :], in_=pd[0:64, :]), [PD], [RDEN])
                tt("dve", oT[:, hk * 4:(hk + 1) * 4, qb * 128:(qb + 1) * 128], po[0:64, :].rearrange("p (g q) -> p g q", g=4),
                   rden[:, :].rearrange("p (g q) -> p g q", g=4), ALU.mult, [PO, RDEN], [OT])

            prev_att = None
            for qb in range(nqb):
                for hk in range(2):
                    st = att_front(qb, hk)
                    if prev_att is not None:
                        att_back(*prev_att)
                    prev_att = (qb, hk, st)
            att_back(*prev_att)
            pool_mm()
            if dd:
                dump("kT", kT[:, :], [128, T], [KT])
                dump("V", V[:, :, :], [128, T // 128, 128], [VB])
                dump("Ftok", Ftok[:, :, :], [128, T // 128, 256], [FB])
                dump("qrot", qrot[:, :, :], [128, 4, TT], [QR])
                dump("oT", oT[:, :, :], [64, 8, TT], [OT])
            if dd:
                dump("uloc", uloc[:, :, :], [128, 2, TT + 16], [ULOC])
                dump("pooled", pooled[:, :, :], [128, 2, TT], [PLD])
                dump("ypool", ypool[:, :, :], [128, 2, TT], [YP])
            ntt = Tn // 128
            ct_, st_ = (ctab_d, stab_d) if lat else (ctabc_d, stabc_d)
            Fsrc, FBUF = (Ftok, FB) if lat else (Fc, FCB)
            pg = [next_ps() for _ in range(4)]
            nblk = max(1, ntt // 8)
            per = ntt // nblk
            for bi in range(nblk):
                for ti, tab in enumerate((ct_, st_)):
                    if lat:
                        tsrc = tab[t0 // TT, bi].rearrange("p (k n) -> p k n", k=per)
                    else:
                        tsrc = tab.rearrange("(k p) t -> p k t", p=128)[:, bi * per:(bi + 1) * per, t0:t0 + n]
                    tv, TB = W.req(tsrc, 128, per, n, "sp")
                    for fc in range(2):
                        pgt, PGT = pg[ti * 2 + fc]
                        for k in range(per):
                            tti = bi * per + k
                            mm(pgt[:, 0:n], Fsrc[:, tti, fc * 128:(fc + 1) * 128], tv[:, k, :] if tv is not None else None,
                               tti == 0, tti == ntt - 1, [FBUF, TB], [PGT])
            for i in range(4):
                pgt, PGT = pg[i]
                if i % 2 == 0:
                    actf(gT[:, i, 0:n], pgt[:, 0:n], AF.Copy, [PGT], [GT])
                else:
                    cp("dve", gT[:, i, 0:n], pgt[:, 0:n], [PGT], [GT])
            for fc in range(2):
                pf, PF = next_ps()
                mm(pf[:, 0:n], cs64[:, 0:128], gT[:, fc, 0:n], True, False, [CONSTB, GT], [PF])
                mm(pf[:, 0:n], cs64[:, 128:256], gT[:, 2 + fc, 0:n], False, True, [CONSTB, GT], [PF])
                actf(yfour[:, fc, 0:n], pf[:, 0:n], AF.Copy, [PF], [YF])
            if dd:
                dump("yfour", yfour[:, :, :], [128, 2, TT], [YF])
            if hook is not None:
                hook()
            for oc in range(8):
                wc, WC = W.req(("br", l, oc), 128, 1, 1536, key=("br", l, oc))
                wg, WG = W.req(win_[:, :, 1920 + oc * 384:1920 + (oc + 1) * 384], 128, 8, 384, key=("g", l, oc))
                if wc is not None:
                    wba = wc[0:64, 0, 0:1024].rearrange("p (h n) -> p h n", h=8)
                    wpf = wc[:, 0, 1024:1536].rearrange("p (c n) -> p c n", c=4)
                else:
                    wba = wpf = None
                pa, PA = next_ps()
                for hd in range(8):
                    mm(pa[:, 0:n], wba[:, hd, :] if wba is not None else None, oT[:, hd, 0:n], hd == 0, hd == 7, [WC, OT], [PA])
                pb, PB = next_ps()
                for c in range(2):
                    mm(pb[:, 0:n], wpf[:, c, :] if wpf is not None else None, ypool[:, c, 0:n], c == 0, c == 1, [WC, YP], [PB])
                pc, PC = next_ps()
                for c in range(2):
                    mm(pc[:, 0:n], wpf[:, 2 + c, :] if wpf is not None else None, yfour[:, c, 0:n], c == 0, c == 1, [WC, YF], [PC])
                brs = [(pa, PA), (pb, PB), (pc, PC)]
                for b in range(3):
                    pgt, PGT = next_ps()
                    for kc in range(8):
                        mm(pgt[:, 0:n], wg[:, kc, b * 128:(b + 1) * 128] if wg is not None else None, hh(kc), kc == 0, kc == 7, [WG, cur.HB], [PGT])
                    actf(sg[b][:, 0:n], pgt[:, 0:n], AF.Sigmoid, [PGT], [SG[b]])
                    tt("dve", sg[b][:, 0:n], sg[b][:, 0:n], brs[b][0][:, 0:n], ALU.mult, [SG[b], brs[b][1]], [SG[b]])
                tt("pool", sg[0][:, 0:n], sg[0][:, 0:n], sg[1][:, 0:n], ALU.add, [SG[0], SG[1]], [SG[0]])
                tt("pool", y[:, oc, 0:n], sg[0][:, 0:n], sg[2][:, 0:n], ALU.add, [SG[0], SG[2]], [YB])
            if dd:
                dump("y", y[:, :, :], [128, 8, TT], [YB])
            for half in range(2):
                wo, WO = W.req(wout_d[l].rearrange("(k p) n -> p k n", p=128)[:, :, half * 512:(half + 1) * 512], 128, 8, 512, key=("wo", l, half))
                for j in range(4):
                    oc = half * 4 + j
                    pz, PZ = next_ps()
                    for kc in range(8):
                        mm(pz[:, 0:n], wo[:, kc, j * 128:(j + 1) * 128] if wo is not None else None, y[:, kc, 0:n], kc == 0, kc == 7, [WO, YB], [PZ])
                    stt("dve", xv(oc, 0, n), pz[:, 0:n], modv(2, oc, s), xv(oc, 0, n), ALU.mult, ALU.add, [PZ, MODS[LCUR[0]], XBUF], [XBUF])

        def p3_tile(l, S, t0, n, xv, XBUF, zero_cols, do_norm=True, hook=None):
            lat = S == "L"
            s = 0 if lat else 1
            hv = lambda kc, c0, m: cur.h[:, kc, c0:c0 + m]
            if n == 512:
                ranges = [(0, 512), (510, 4)]
            else:
                ranges = [(0, n + 2)]
            if do_norm:
                norm(xv, XBUF, [(0, min(512, n + 2))] + ([(512, 2)] if n == 512 else []), gmB(), 3, s, hv, cur.HB, zero_cols=zero_cols)
            wup_ = wup_d[l].rearrange("(k p) n -> p k n", p=128)
            pend = [None]
            for jg in range(0, NFF, 4):
                nj = min(4, NFF - jg)
                wv_, WVb = W.req(wup_[:, :, jg * 128:(jg + nj) * 128], 128, 8, nj * 128, key=("wv", l, jg))
                wg_, WGb = W.req(wup_[:, :, DFF + jg * 128:DFF + (jg + nj) * 128], 128, 8, nj * 128, key=("wg", l, jg))
                for jj in range(nj):
                    j = jg + jj
                    for (c0, m) in ranges:
                        pvv, PVV = next_ps()
                        pgg, PGG = next_ps()
                        for kc in range(8):
                            mm(pvv[:, 0:m], wv_[:, kc, jj * 128:(jj + 1) * 128] if wv_ is not None else None, cur.h[:, kc, c0:c0 + m], kc == 0, kc == 7, [WVb, cur.HB], [PVV])
                        for kc in range(8):
                            mm(pgg[:, 0:m], wg_[:, kc, jj * 128:(jj + 1) * 128] if wg_ is not None else None, cur.h[:, kc, c0:c0 + m], kc == 0, kc == 7, [WGb, cur.HB], [PGG])
                        mo = m - 2
                        (cva, cvg, cvs), (CVA, CVG, CVS) = (cvsets if m > 8 else tvsets)[j % 2]
                        for (pp_, PPb, ch, dst, DST) in ((pvv, PVV, j, cva, CVA), (pgg, PGG, NFF + j, cvg, CVG)):
                            cw = lambda k, ch=ch: convw[:, (l * 3 + k) * 44 + ch:(l * 3 + k) * 44 + ch + 1]
                            actf(dst[:, 0:mo], pp_[:, 1:1 + mo], AF.Identity, [PPb, CONSTB], [DST], scale=cw(1))
                            stt("dve", dst[:, 0:mo], pp_[:, 0:mo], cw(0), dst[:, 0:mo], ALU.mult, ALU.add, [PPb, CONSTB, DST], [DST])
                            stt("dve", dst[:, 0:mo], pp_[:, 2:2 + mo], cw(2), dst[:, 0:mo], ALU.mult, ALU.add, [PPb, CONSTB, DST], [DST])
                        def fin_(cva=cva, cvg=cvg, cvs=cvs, CVA=CVA, CVG=CVG, CVS=CVS, j=j, c0=c0, mo=mo):
                            actf(cvs[:, 0:mo], cvg[:, 0:mo], AF.Silu, [CVG], [CVS])
                            tt("pool", act[:, j, c0:c0 + mo], cva[:, 0:mo], cvs[:, 0:mo], ALU.mult, [CVA, CVS], [ACTB])
                        if pend[0] is not None:
                            pend[0]()
                        pend[0] = fin_
            if pend[0] is not None:
                pend[0]()
                pend[0] = None
            if hook is not None:
                hook()
            wdn_ = wdn_d[l].rearrange("(k p) n -> p k n", p=128)
            for oc in range(8):
                wd, WD = W.req(wdn_[:, :, oc * 128:(oc + 1) * 128], 128, NFF, 128, key=("wd", l, oc))
                pz, PZ = next_ps()
                for j in range(NFF):
                    mm(pz[:, 0:n], wd[:, j, :] if wd is not None else None, act[:, j, 0:n], j == 0, j == NFF - 1, [WD, ACTB], [PZ])
                stt("dve", xv(oc, 1, n), pz[:, 0:n], modv(5, oc, s), xv(oc, 1, n), ALU.mult, ALU.add, [PZ, MODS[LCUR[0]], XBUF], [XBUF])

        _orig_req = W.req

        def req2(src, npart, kc, ncol, qn="pool", key=None):
            return _orig_req(src, npart, kc, ncol, qn, key)

        _orig_dma = P.dma

        def dma2(qn, out, in_, reads=(), writes=(), acc=False):
            if isinstance(in_, tuple):
                _, l_, oc_ = in_
                cs = slice(oc_ * 128, (oc_ + 1) * 128)
                _orig_dma(qn, out[0:64, 0, 0:1024].rearrange("p (h n) -> p h n", h=8),
                          wba_d[l_].rearrange("(h p) n -> p h n", p=64)[:, :, cs], reads, writes)
                _orig_dma(qn, out[:, 0, 1024:1280].rearrange("p (c n) -> p c n", c=2),
                          wbp_d[l_].rearrange("(c p) n -> p c n", p=128)[:, :, cs], reads, writes, acc=True)
                return _orig_dma(qn, out[:, 0, 1280:1536].rearrange("p (c n) -> p c n", c=2),
                                 wbf_d[l_].rearrange("(c p) n -> p c n", p=128)[:, :, cs], reads, writes, acc=True)
            return _orig_dma(qn, out, in_, reads, writes, acc)

        W.req = req2
        P.dma = dma2

        xcv = lambda kc, c0, m: xc[:, kc, 1 + c0:1 + c0 + m]
        xcv3 = lambda kc, c0, m: xc[:, kc, c0:c0 + m]
        xtv = lambda kc, c0, m: cur.xt[:, kc, c0:c0 + m]

        def program():
            last_out = []
            for l in range(depth_run):
                last = l == DEPTH - 1
                Xin, XIN = (fm(xT), XB["x0"]) if l == 0 else (fm(X1), XB["x1"])
                LCUR[0] = l
                if l == 0:
                    mod_phase(0)
                hv0 = lambda kc, c0, m: cur.h[:, kc, c0:c0 + m]

                def p12_prep(i):
                    setcur(i)
                    P.dma("pool", cur.xt[:, :, 0:TT], Xin[:, :, i * TT:(i + 1) * TT], reads=[XIN[i]], writes=[cur.XT])
                    norm(xtv, cur.XT, [(0, TT)], gmA(), 0, 0, hv0, cur.HB)

                def mk_hook(prep, i):
                    def hook():
                        if i + 1 < NTL:
                            prep(i + 1)
                            setcur(i)
                    return hook

                fuse_p1 = (not last) and (l + 1 < depth_run)
                if l == 0:
                    setcur(1)
                    p1_tile(l, "C", 0, CT, xcv, XC, 0)
                    p12_prep(0)
                    for i in range(NTL):
                        setcur(i)
                        p1_tile(l, "L", i * TT, TT, xtv, cur.XT, 0, do_norm=False, hook=mk_hook(p12_prep, i))
                fence()
                p12_prep(0)
                for i in range(NTL):
                    setcur(i)
                    p2_tile(l, "L", i * TT, TT, xtv, cur.XT, 0, do_norm=False, hook=mk_hook(p12_prep, i))
                    P.dma("pool", fm(XM)[:, :, i * TT:(i + 1) * TT], cur.xt[:, :, 0:TT], reads=[cur.XT], writes=[XB["xm"][i]])
                if not last:
                    setcur(NTL)
                    p2_tile(l, "C", 0, CT, xcv, XC, 0)
                    if l == 0:
                        dump("xcmid", xc[:, :, :], [128, 8, CT + 2], [XC])
                if l + 1 < depth_run:
                    mod_phase(l + 1)
                fence()

                def p3_prep(i):
                    setcur(i)
                    lo = i * TT - 1
                    zc = []
                    if i == 0:
                        P.dma("pool", cur.xt[:, :, 1:TT + 2], fm(XM)[:, :, 0:TT + 1], reads=[XB["xm"][0], XB["xm"][1]], writes=[cur.XT])
                        mset("pool", cur.xt[:, :, 0:1], 0.0, [cur.XT])
                        zc = [0]
                    elif i == NTL - 1:
                        P.dma("pool", cur.xt[:, :, 0:TT + 1], fm(XM)[:, :, lo:lo + TT + 1], reads=[XB["xm"][i - 1], XB["xm"][i]], writes=[cur.XT])
                        mset("pool", cur.xt[:, :, TT + 1:TT + 2], 0.0, [cur.XT])
                        zc = [TT + 1]
                    else:
                        P.dma("pool", cur.xt[:, :, 0:TT + 2], fm(XM)[:, :, lo:lo + TT + 2],
                              reads=[XB["xm"][i - 1], XB["xm"][i], XB["xm"][i + 1]], writes=[cur.XT])
                    norm(xtv, cur.XT, [(0, 512), (512, 2)], gmB(), 3, 0, hv0, cur.HB, zero_cols=zc)

                if not last:
                    setcur(NTL)
                    p3_tile(l, "C", 0, CT, xcv3, XC, [0, CT + 1])
                    if fuse_p1:
                        LCUR[0] = l + 1
                        setcur(NTL + 1)
                        p1_tile(l + 1, "C", 0, CT, xcv, XC, 0)
                        LCUR[0] = l
                p3_prep(0)
                for i in range(NTL):
                    setcur(i)
                    p3_tile(l, "L", i * TT, TT, xtv, cur.XT, [], do_norm=False, hook=mk_hook(p3_prep, i))
                    if last or depth_run == 1 and l == depth_run - 1:
                        if last:
                            xo = lambda kc, c0, m: cur.xt[:, kc, 1 + c0:1 + c0 + m]
                            norm(xo, cur.XT, [(0, TT)], nfin, None, 0, xo, cur.XT)
                            ev = P.dma("pool", fm(outT)[:, :, i * TT:(i + 1) * TT], cur.xt[:, :, 1:TT + 1], reads=[cur.XT], writes=[XB["out"][i]])
                        else:
                            ev = P.dma("pool", fm(outT)[:, :, i * TT:(i + 1) * TT], cur.xt[:, :, 1:TT + 1], reads=[cur.XT], writes=[XB["out"][i]])
                        last_out.append(ev)
                    else:
                        P.dma("pool", fm(X1)[:, :, i * TT:(i + 1) * TT], cur.xt[:, :, 1:TT + 1], reads=[cur.XT], writes=[XB["x1"][i]])
                        if fuse_p1:
                            LCUR[0] = l + 1
                            xo1 = lambda kc, c0, m: cur.xt[:, kc, 1 + c0:1 + c0 + m]
                            p1_tile(l + 1, "L", i * TT, TT, xo1, cur.XT, 0)
                            LCUR[0] = l
                if l == 0:
                    dump("xmid", fm(XM), [128, 8, T], XB["xm"])
                    dump("xc", xc[:, :, :], [128, 8, CT + 2], [XC])
            return last_out

        P.planning = True
        program()
        P.planning = False
        pspos[0] = 0
        setup()
        outs = program()
        for ev in outs:
            if ev is not None:
                P.final_wait("sp", ev)
        P.emit()
        print(f"[build] insts={P.n_inst} waits={P.n_wait} sems={P.nsem} wblocks={len(W.plan)}")
    if dbg:
        return nc, list(dumps.keys())
    return nc


def _host_consts():
    import ml_dtypes
    bf = ml_dtypes.bfloat16
    c = {}
    t = np.arange(T)
    row = (t // 64).astype(np.float32)
    col = (t % 64).astype(np.float32)
    inv_freq = (np.float32(10000.0) ** (-np.arange(16, dtype=np.float32) / np.float32(16))).astype(np.float32)
    C = np.zeros((128, T), np.float32)
    S = np.zeros((128, T), np.float32)
    for p in range(128):
        d = p % 64
        a, r, f = d // 32, (d % 32) // 16, d % 16
        pos = row if a == 0 else col
        ang = (pos * inv_freq[f]).astype(np.float32)
        C[p] = np.cos(ang)
        S[p] = np.sin(ang) * (-1.0 if r == 0 else 1.0)
    c["ropeC"], c["ropeS"] = C, S
    j = np.arange(128)[:, None]
    i = np.arange(128)[None, :]
    mp = (j >= i).astype(np.float32)
    mn = (j <= i).astype(np.float32)
    c["masks"] = np.concatenate([np.tile(mp, (1, 4)), np.tile(mn, (1, 4))], axis=1).astype(np.float32)
    for nm, N in (("", T), ("c", CT)):
        tt_ = np.arange(N, dtype=np.int64)
        k = (tt_[:, None] * tt_[None, :]) % N
        ang = 2.0 * np.pi * k.astype(np.float64) / N
        sc = 1.0 / np.sqrt(N * 64.0)
        ct = (np.cos(ang) * sc).astype(np.float32).astype(bf)
        st = (-np.sin(ang) * sc).astype(np.float32).astype(bf)
        if N == T:
            lay = lambda a: np.ascontiguousarray(a.reshape(4, 8, 128, T // TT, TT).transpose(3, 0, 2, 1, 4)).reshape(T // TT, 4, 128, 4096)
            ct, st = lay(ct), lay(st)
        c["ctab" + nm] = ct
        c["stab" + nm] = st
    cc = np.arange(64, dtype=np.int64)
    k = (cc[:, None] * cc[None, :]) % 64
    ang = 2.0 * np.pi * k.astype(np.float64) / 64
    C64, S64 = np.cos(ang), np.sin(ang)
    cs = np.zeros((128, 256), np.float32)
    for g in range(2):
        cs[g * 64:(g + 1) * 64, g * 64:(g + 1) * 64] = C64
        cs[g * 64:(g + 1) * 64, 128 + g * 64:128 + (g + 1) * 64] = S64
    c["cs64"] = cs
    wins = (2, 4, 8, 16)
    invw = np.zeros((128, 2), np.float32)
    rc = np.ones((128, 32), np.float32)
    for g, w in enumerate(wins):
        ch, half = g // 2, g % 2
        pr = slice(half * 64, (half + 1) * 64)
        invw[pr, ch] = 1.0 / w
        for e in range(8):
            cnt_first = min(e - w // 2 + w, 10 ** 9) - max(e - w // 2, 0)
            rc[pr, ch * 16 + e] = 1.0 / cnt_first
            cnt_last = min(8 - e + w // 2, w)
            rc[pr, ch * 16 + 8 + e] = 1.0 / cnt_last
    c["invw"], c["rcnt"] = invw, rc
    return c


_CONSTS = None
_NC = None
_DBG_HOOK = None


def _fmv(v):
    return np.ascontiguousarray(v.reshape(-1, 128).T)


def kernel(x, c, ctx, c_ctx, w_mod, b_mod, norm_mix, norm_ffn, w_in, attn_sink, pool_w, pool_scale,
           w_br_attn, w_br_pool, w_br_four, w_out, w_up, conv_w, w_down, norm_final):
    global _CONSTS, _NC
    f32 = np.float32
    A = lambda a: np.ascontiguousarray(np.asarray(a, dtype=f32))
    x, c, ctx, c_ctx = A(x), A(c), A(ctx), A(c_ctx)
    w_mod, b_mod, norm_mix, norm_ffn, w_in = A(w_mod), A(b_mod), A(norm_mix), A(norm_ffn), A(w_in)
    attn_sink, pool_w, pool_scale = A(attn_sink), A(pool_w), A(pool_scale)
    w_br_attn, w_br_pool, w_br_four, w_out = A(w_br_attn), A(w_br_pool), A(w_br_four), A(w_out)
    w_up, conv_w, w_down, norm_final = A(w_up), A(conv_w), A(w_down), A(norm_final)
    if _CONSTS is None:
        _CONSTS = _host_consts()
    if _NC is None:
        _NC = build_program()
    K = _CONSTS
    d = np.arange(64)
    swap = (d // 32) * 32 + (1 - (d % 32) // 16) * 16 + d % 16
    k_cols = np.arange(0, 128)
    kp_cols = np.concatenate([hh * 64 + swap for hh in range(2)])
    v_cols = np.arange(128, 256)
    u_cols = np.arange(768, 1024)
    f_cols = np.arange(1024, 1280)
    q_cols = np.concatenate([np.concatenate([256 + j * 64 + d, 256 + (4 + j) * 64 + d]) for j in range(4)])
    qp_cols = np.concatenate([np.concatenate([256 + j * 64 + swap, 256 + (4 + j) * 64 + swap]) for j in range(4)])
    g_cols = np.concatenate([np.concatenate([1280 + b * 1024 + oc * 128 + np.arange(128) for b in range(3)]) for oc in range(8)])
    cols = np.concatenate([k_cols, kp_cols, v_cols, u_cols, f_cols, q_cols, qp_cols, g_cols])
    assert cols.shape[0] == INW2
    w_in2 = np.ascontiguousarray(w_in[:, :, cols])
    bmod = np.concatenate([np.repeat(_fmv(b_mod[l])[:, :, None], 2, axis=2).reshape(128, 96) for l in range(DEPTH)], axis=1)
    nmix = np.concatenate([np.repeat(_fmv(norm_mix[l])[:, :, None], 2, axis=2).reshape(128, 16) for l in range(DEPTH)], axis=1)
    nffn = np.concatenate([np.repeat(_fmv(norm_ffn[l])[:, :, None], 2, axis=2).reshape(128, 16) for l in range(DEPTH)], axis=1)
    nfin = _fmv(norm_final)
    pscale = np.concatenate([_fmv(pool_scale[l]) for l in range(DEPTH)], axis=1)
    convw = np.concatenate([_fmv(conv_w[l, k]) for l in range(DEPTH) for k in range(3)], axis=1)
    sinkx = np.zeros((DEPTH * 2, 512), f32)
    for l in range(DEPTH):
        for hk in range(2):
            sinkx[l * 2 + hk] = np.repeat(attn_sink[l, hk * 4:(hk + 1) * 4], 128)
    pwbd = np.zeros((DEPTH * 2, 128, 128), f32)
    for l in range(DEPTH):
        for g in range(4):
            ch, half = g // 2, g % 2
            pwbd[l * 2 + ch, half * 64:(half + 1) * 64, half * 64:(half + 1) * 64] = pool_w[l, g]
    selm = np.zeros((4, 256), f32)
    for r in range(4):
        selm[r, r * 64:(r + 1) * 64] = 1.0
    shared = {
        "sel": selm,
        "w_mod": w_mod, "bmod": np.ascontiguousarray(bmod), "nmix": np.ascontiguousarray(nmix),
        "nffn": np.ascontiguousarray(nffn), "nfin": nfin, "w_in2": w_in2, "sinkx": sinkx, "pwbd": pwbd,
        "pscale": np.ascontiguousarray(pscale), "w_br_attn": w_br_attn, "w_br_pool": w_br_pool,
        "w_br_four": w_br_four, "w_out": w_out, "w_up": w_up, "convw": np.ascontiguousarray(convw),
        "w_down": w_down, "ropeC": K["ropeC"], "ropeS": K["ropeS"], "masks": K["masks"],
        "ctab": K["ctab"], "stab": K["stab"], "ctabc": K["ctabc"], "stabc": K["stabc"], "cs64": K["cs64"],
        "rcnt": K["rcnt"], "invw": K["invw"],
    }
    in_maps = []
    for b in range(NCORES):
        cv = np.stack([_fmv(c[b]), _fmv(c_ctx)], axis=2).reshape(128, 16)
        m = dict(shared)
        m["xT"] = np.ascontiguousarray(x[b].T)
        m["ctxT"] = np.ascontiguousarray(ctx[b].T)
        m["cvec"] = np.ascontiguousarray(cv)
        in_maps.append(m)
    if _DBG_HOOK is not None:
        return _DBG_HOOK(in_maps)
    res = run_bass_kernel_spmd(_NC, in_maps, core_ids=list(range(NCORES)))
    out = np.stack([np.ascontiguousarray(res.results[b]["outT"].T) for b in range(NCORES)], axis=0)
    return out.astype(np.float32)
```

```python
import numpy as np
from contextlib import ExitStack
import concourse.bass as bass
import concourse.mybir as mybir
from concourse.bass_utils import run_bass_kernel_spmd

F32 = mybir.dt.float32
BF16 = mybir.dt.bfloat16
AF = mybir.ActivationFunctionType
ALU = mybir.AluOpType

D = 1024
T = 4096
CT = 256
DEPTH = 2
DFF = 2816
NFF = 22
INW2 = 4992
TT = 512
NCORES = 4
EPS = 1e-6


class Buf:
    __slots__ = ("name", "w", "r")

    def __init__(self, name):
        self.name = name
        self.w = []
        self.r = []


class Prog:
    ENG = ("pe", "act", "dve", "pool", "sp")
    SEM_ROLL = 30000
    NDMASEM = 12

    def __init__(self, nc, es):
        self.nc = nc
        self.es = es
        self.planning = False
        self.q = {e: [] for e in self.ENG}
        self.esem = {}
        self.ecnt = {}
        self.waited = {e: {} for e in self.ENG}
        self.nsem = 0
        for e in self.ENG:
            self._new_esem(e)
        self.dsem = {}
        self.dpos = {}
        for qn in ("sp", "act", "pool"):
            self.dsem[qn] = [[self._sem(f"d_{qn}_{i}"), 0] for i in range(self.NDMASEM)]
            self.dpos[qn] = 0
        self.n_inst = 0
        self.n_wait = 0

    def _sem(self, name):
        self.nsem += 1
        return self.es.enter_context(self.nc.semaphore(f"{name}_{self.nsem}"))

    def _new_esem(self, e):
        self.esem[e] = self._sem(f"e_{e}")
        self.ecnt[e] = 0

    def _wait(self, eng, ev):
        if ev is None:
            return
        sem, val, src = ev
        if src == "pe" and eng == "pe":
            return
        k = id(sem)
        if self.waited[eng].get(k, 0) >= val:
            return
        self.waited[eng][k] = val
        self.q[eng].append(("w", sem, val))
        self.n_wait += 1

    def _deps(self, eng, reads, writes, acc=False):
        for b in reads:
            for ev in b.w:
                self._wait(eng, ev)
        for b in writes:
            if not acc:
                for ev in b.w:
                    self._wait(eng, ev)
            for ev in b.r:
                self._wait(eng, ev)

    def _commit(self, ev, reads, writes, acc=False):
        for b in reads:
            if len(b.r) > 48:
                d = {}
                for e2 in b.r:
                    k = id(e2[0])
                    if k not in d or d[k][1] < e2[1]:
                        d[k] = e2
                b.r = list(d.values())
            b.r.append(ev)
        for b in writes:
            if acc:
                b.w.append(ev)
            else:
                b.w = [ev]
            b.r = []

    def op(self, eng, fn, reads=(), writes=()):
        if self.planning:
            return None
        self._deps(eng, reads, writes)
        if self.ecnt[eng] >= self.SEM_ROLL:
            self._new_esem(eng)
        self.ecnt[eng] += 1
        sem = self.esem[eng]
        val = self.ecnt[eng]
        self.q[eng].append(("o", fn, sem))
        ev = (sem, val, eng)
        self._commit(ev, reads, writes)
        self.n_inst += 1
        return ev

    def dma(self, qn, out, in_, reads=(), writes=(), acc=False):
        if self.planning:
            return None
        self._deps(qn, reads, writes, acc)
        pos = self.dpos[qn]
        self.dpos[qn] = (pos + 1) % self.NDMASEM
        slot = self.dsem[qn][pos]
        sem, cnt = slot
        if cnt > 0:
            self._wait(qn, (sem, cnt, "dma"))
        if cnt >= self.SEM_ROLL:
            sem = self._sem(f"d_{qn}")
            cnt = 0
            slot[0] = sem
        cnt += 16
        slot[1] = cnt
        self.q[qn].append(("d", out, in_, sem))
        ev = (sem, cnt, "dma")
        self._commit(ev, reads, writes, acc)
        self.n_inst += 1
        return ev

    def final_wait(self, eng, ev):
        self.q[eng].append(("w", ev[0], ev[1]))

    def emit(self):
        nc = self.nc
        q = self.q

        def replay(e, lst):
            for it in lst:
                if it[0] == "w":
                    e.wait_ge(it[1], it[2])
                elif it[0] == "o":
                    it[1](e).then_inc(it[2], 1)
                else:
                    e.dma_start(out=it[1], in_=it[2]).then_inc(it[3], 16)

        with nc.Block() as block:
            @block.sync
            def _(e):
                replay(e, q["sp"])

            @block.tensor
            def _(e):
                replay(e, q["pe"])

            @block.scalar
            def _(e):
                replay(e, q["act"])

            @block.vector
            def _(e):
                replay(e, q["dve"])

            @block.gpsimd
            def _(e):
                replay(e, q["pool"])


class WPool:
    SLOT = 4096

    def __init__(self, P, nc, es, nslots=5, ahead=3, scratch=None):
        self.P = P
        self.n = nslots
        self.ahead = ahead
        self.scratch = scratch
        self.tiles = [es.enter_context(nc.sbuf_tensor(f"wslot{i}", [128, self.SLOT], BF16)) for i in range(nslots)]
        self.bufs = [Buf(f"wslot{i}") for i in range(nslots)]
        self.plan = []
        self.issued = 0
        self.cur = 0
        self.keys = {}

    def _view(self, i):
        src, npart, kc, ncol, qn, key = self.plan[i]
        t = self.tiles[i % self.n]
        return t[0:npart, 0:kc * ncol].rearrange("p (k n) -> p k n", k=kc)

    def _issue(self, i):
        src, npart, kc, ncol, qn, key = self.plan[i]
        sbuf_ = self.bufs[i % self.n]
        flat = self.tiles[i % self.n][0:npart, 0:kc * ncol]
        if key is None or self.scratch is None:
            self.P.dma(qn, self._view(i), src, writes=[sbuf_])
        elif key not in self.keys:
            k = len(self.keys)
            SB = Buf(f"scr{k}")
            self.keys[key] = (k, SB)
            self.P.dma(qn, self._view(i), src, writes=[sbuf_])
            self.P.dma("sp", self.scratch[k, 0:npart, 0:kc * ncol], flat, reads=[sbuf_], writes=[SB])
        else:
            k, SB = self.keys[key]
            self.P.dma("sp", flat, self.scratch[k, 0:npart, 0:kc * ncol], reads=[SB], writes=[sbuf_])

    def req(self, src, npart, kc, ncol, qn="pool", key=None):
        assert kc * ncol <= self.SLOT
        if self.P.planning:
            self.plan.append((src, npart, kc, ncol, qn, key))
            return None, None
        while self.issued < min(len(self.plan), self.cur + self.ahead + 1):
            self._issue(self.issued)
            self.issued += 1
        i = self.cur
        self.cur += 1
        assert self.plan[i][1:] == (npart, kc, ncol, qn, key), (i, self.plan[i][1:], (npart, kc, ncol, qn, key))
        return self._view(i), self.bufs[i % self.n]


def build_program(depth_run=DEPTH, dbg=False):
    nc = bass.Bass("TRN2", target_bir_lowering=False)
    es = ExitStack()
    with es:
        P = Prog(nc, es)

        def dram(name, shape, dt, kind):
            return nc.dram_tensor(name, shape, dt, kind=kind).ap()

        def sb(name, shape, dt=F32):
            return es.enter_context(nc.sbuf_tensor(name, shape, dt))

        xT = dram("xT", [D, T], F32, "ExternalInput")
        ctxT = dram("ctxT", [D, CT], F32, "ExternalInput")
        cvec_d = dram("cvec", [128, 16], F32, "ExternalInput")
        w_mod_d = dram("w_mod", [DEPTH, D, 6 * D], F32, "ExternalInput")
        bmod_d = dram("bmod", [128, DEPTH * 96], F32, "ExternalInput")
        nmix_d = dram("nmix", [128, DEPTH * 16], F32, "ExternalInput")
        nffn_d = dram("nffn", [128, DEPTH * 16], F32, "ExternalInput")
        nfin_d = dram("nfin", [128, 8], F32, "ExternalInput")
        w_in_d = dram("w_in2", [DEPTH, D, INW2], F32, "ExternalInput")
        sink_d = dram("sinkx", [DEPTH * 2, 512], F32, "ExternalInput")
        sel_d = dram("sel", [4, 256], F32, "ExternalInput")
        pwbd_d = dram("pwbd", [DEPTH * 2, 128, 128], F32, "ExternalInput")
        pscale_d = dram("pscale", [128, DEPTH * 2], F32, "ExternalInput")
        wba_d = dram("w_br_attn", [DEPTH, 512, D], F32, "ExternalInput")
        wbp_d = dram("w_br_pool", [DEPTH, 256, D], F32, "ExternalInput")
        wbf_d = dram("w_br_four", [DEPTH, 256, D], F32, "ExternalInput")
        wout_d = dram("w_out", [DEPTH, D, D], F32, "ExternalInput")
        wup_d = dram("w_up", [DEPTH, D, 2 * DFF], F32, "ExternalInput")
        convw_d = dram("convw", [128, DEPTH * 3 * 44], F32, "ExternalInput")
        wdn_d = dram("w_down", [DEPTH, DFF, D], F32, "ExternalInput")
        ropeC_d = dram("ropeC", [128, T], F32, "ExternalInput")
        ropeS_d = dram("ropeS", [128, T], F32, "ExternalInput")
        mask_d = dram("masks", [128, 1024], F32, "ExternalInput")
        ctab_d = dram("ctab", [T // TT, 4, 128, 4096], BF16, "ExternalInput")
        stab_d = dram("stab", [T // TT, 4, 128, 4096], BF16, "ExternalInput")
        ctabc_d = dram("ctabc", [CT, CT], BF16, "ExternalInput")
        stabc_d = dram("stabc", [CT, CT], BF16, "ExternalInput")
        cs64_d = dram("cs64", [128, 256], F32, "ExternalInput")
        rcnt_d = dram("rcnt", [128, 32], F32, "ExternalInput")
        invw_d = dram("invw", [128, 2], F32, "ExternalInput")
        outT = dram("outT", [D, T], F32, "ExternalOutput")
        XM = dram("xmid", [D, T], F32, "Internal")
        X1 = dram("x1s", [D, T], F32, "Internal")
        UL = dram("u_lat", [256, T + 16], F32, "Internal")
        UC = dram("u_ctx", [256, CT + 16], F32, "Internal")

        def fm(ap):
            return ap.rearrange("(k p) t -> p k t", p=128)

        NTL = T // TT
        XB = {"x0": [Buf(f"x0_{i}") for i in range(NTL)], "xm": [Buf(f"xm_{i}") for i in range(NTL)],
              "x1": [Buf(f"x1_{i}") for i in range(NTL)], "out": [Buf(f"o_{i}") for i in range(NTL)]}
        ULB = Buf("UL")
        UCB = Buf("UC")

        WSCR = dram("wscr", [96, 128, 4096], BF16, "Internal")
        W = WPool(P, nc, es, nslots=4, ahead=2, scratch=WSCR)
        ones = sb("ones", [128, 128], BF16)
        ONES = Buf("ones")
        kT = sb("kT", [128, T], BF16)
        KT = Buf("kT")
        V = sb("V", [128, T // 128, 128], BF16)
        VB = Buf("V")
        Ftok = sb("Ftok", [128, T // 128, 256], BF16)
        FB = Buf("Ftok")
        kcT = sb("kcT", [128, CT], BF16)
        KCT = Buf("kcT")
        Vc = sb("Vc", [128, 2, 128], BF16)
        VCB = Buf("Vc")
        Fc = sb("Fc", [128, 2, 256], BF16)
        FCB = Buf("Fc")
        xc = sb("xc", [128, 8, CT + 2], F32)
        XC = Buf("xc")
        cvec = sb("cvec_s", [128, 16], F32)
        csil = sb("csil", [128, 16], BF16)
        CV = Buf("cvec")
        MODS = [Buf("mod0"), Buf("mod1")]
        LCUR = [0]
        bmod = sb("bmod_s", [128, DEPTH * 96], F32)
        nmix = sb("nmix_s", [128, DEPTH * 16], F32)
        nffn = sb("nffn_s", [128, DEPTH * 16], F32)
        nfin = sb("nfin_s", [128, 8], F32)
        CONSTB = Buf("consts")
        esrow = sb("esrow", [4, 512], BF16)
        sel = sb("sel_s", [4, 256], BF16)
        pwbd = sb("pwbd_s", [128, DEPTH * 2, 128], BF16)
        pscale = sb("pscale_s", [128, DEPTH * 2], F32)
        convw = sb("convw_s", [128, DEPTH * 3 * 44], F32)
        masks = sb("masks_s", [128, 1024], BF16)
        cs64 = sb("cs64_s", [128, 256], BF16)
        rcnt = sb("rcnt_s", [128, 32], F32)
        invw = sb("invw_s", [128, 2], F32)
        class _Cur:
            pass
        cur = _Cur()
        _xts = [sb(f"xt{i}", [128, 8, TT + 2], F32) for i in range(2)]
        _XTs = [Buf(f"xt{i}") for i in range(2)]
        _hs = [sb(f"h{i}", [128, 8, TT + 2], BF16) for i in range(2)]
        _HBs = [Buf(f"h{i}") for i in range(2)]
        _flipc = [0]

        def flip():
            i = _flipc[0] % 2
            _flipc[0] += 1
            cur.xt, cur.XT, cur.h, cur.HB = _xts[i], _XTs[i], _hs[i], _HBs[i]
        def setcur(k):
            i = k % 2
            cur.xt, cur.XT, cur.h, cur.HB = _xts[i], _XTs[i], _hs[i], _HBs[i]
        flip()
        act = sb("act", [128, NFF, TT], BF16)
        ACTB = Buf("act")
        sq = [sb(f"sq{i}", [128, 512], BF16) for i in range(2)]
        SQ = [Buf(f"sq{i}") for i in range(2)]
        rstd = sb("rstd", [128, TT + 16], F32)
        RSTD = Buf("rstd")
        tmp = [sb(f"tmp{i}", [128, 512], F32) for i in range(3)]
        TMP = [Buf(f"tmp{i}") for i in range(3)]
        esrow_f = tmp[0][0:4, :]
        rope = sb("rope", [128, 2, 512], F32)
        ROPE = Buf("rope")
        qrot = act[:, 8:12, :]
        QR = Buf("qrot")
        oT = act[0:64, 12:20, :]
        OT = Buf("oT")
        PT = sb("PT", [128, 5, 512], BF16)
        PTB = [Buf(f"PT{i}") for i in range(5)]
        PTB2 = [Buf(f"PT2_{i}") for i in range(5)]
        ptc = [0]
        rden = tmp[2][0:64, :]
        RDEN = TMP[2]
        uloc = sb("uloc", [128, 2, TT + 16], F32)
        ULOC = Buf("uloc")
        s1 = sb("s1", [128, 2, TT + 16], F32)
        S1 = Buf("s1")
        s2 = sb("s2", [128, TT + 16], F32)
        S2 = Buf("s2")
        s4 = rstd
        S4 = RSTD
        win = sb("win", [128, 2, TT], F32)
        WIN = Buf("win")
        pooled = act[:, 20:22, :]
        PLD = Buf("pooled")
        ypool = sb("ypool", [128, 2, TT], BF16)
        YP = Buf("ypool")
        gT = sb("gT", [128, 4, TT], BF16)
        GT = Buf("gT")
        yfour = sb("yfour", [128, 2, TT], BF16)
        YF = Buf("yfour")
        sg = [sb(f"sg{i}", [128, 512], F32) for i in range(3)]
        SG = [Buf(f"sg{i}") for i in range(3)]
        y = act[:, 0:8, :]
        YB = Buf("y")
        sgB = [sb(f"sgB{i}", [128, 512], F32) for i in range(3)]
        SGB = [Buf(f"sgB{i}") for i in range(3)]
        cvsets = [(sg, SG), (sgB, SGB)]
        tvs = [[sb(f"tv{a}{b}", [128, 8], F32) for b in range(3)] for a in range(2)]
        TVS = [[Buf(f"tv{a}{b}") for b in range(3)] for a in range(2)]
        tvsets = [(tvs[0], TVS[0]), (tvs[1], TVS[1])]

        def fence():
            P.op("pool", lambda e: e.memset(cvec[:, 0:1], 0.0), [CV], [ACTB, QR, OT, YB, PLD])
        ps = [es.enter_context(nc.psum_tensor(f"ps{i}", [128, 512], F32)) for i in range(8)]
        PS = [Buf(f"ps{i}") for i in range(8)]
        pspos = [0]

        def next_ps():
            i = pspos[0]
            pspos[0] = (i + 1) % 8
            return ps[i], PS[i]

        dumps = {}

        def dump(name, src, shape, reads):
            if not dbg or P.planning or name in dumps:
                return
            dumps[name] = dram("dbg_" + name, list(shape), F32, "ExternalOutput")
            P.dma("pool", dumps[name], src, reads=reads, writes=[Buf("dbg_" + name)])

        def mm(out, lhsT, rhs, start, stop, reads, writes):
            P.op("pe", lambda e: e.matmul(out, lhsT=lhsT, rhs=rhs, start=start, stop=stop), reads, writes)

        def actf(out, in_, func, reads, writes, bias=None, scale=None, eng="act"):
            kw = {}
            if bias is not None:
                kw["bias"] = bias
            if scale is not None:
                kw["scale"] = scale
            P.op("act", lambda e: e.activation(out=out, in_=in_, func=func, **kw), reads, writes)

        def tt(eng, out, in0, in1, op, reads, writes):
            P.op(eng, lambda e: e.tensor_tensor(out=out, in0=in0, in1=in1, op=op), reads, writes)

        def stt(eng, out, in0, scalar, in1, op0, op1, reads, writes):
            P.op(eng, lambda e: e.scalar_tensor_tensor(out=out, in0=in0, scalar=scalar, in1=in1, op0=op0, op1=op1), reads, writes)

        def ts(eng, out, in0, s1_, s2_, op0, op1, reads, writes):
            if op1 is None:
                P.op(eng, lambda e: e.tensor_scalar(out=out, in0=in0, scalar1=s1_, scalar2=None, op0=op0), reads, writes)
            else:
                P.op(eng, lambda e: e.tensor_scalar(out=out, in0=in0, scalar1=s1_, scalar2=s2_, op0=op0, op1=op1), reads, writes)

        def cp(eng, out, in_, reads, writes):
            P.op(eng, lambda e: e.tensor_copy(out=out, in_=in_), reads, writes)

        def mset(eng, ap, val, writes):
            P.op(eng, lambda e: e.memset(ap, val), (), writes)

        def setup():
            mset("pool", ones[:, :], 1.0, [ONES])
            P.dma("sp", cvec[:, :], cvec_d[:, :], writes=[CV])
            P.dma("sp", bmod[:, :], bmod_d[:, :], writes=[CONSTB])
            P.dma("sp", nmix[:, :], nmix_d[:, :], writes=[CONSTB])
            P.dma("sp", nffn[:, :], nffn_d[:, :], writes=[CONSTB])
            P.dma("sp", nfin[:, :], nfin_d[:, :], writes=[CONSTB])
            P.dma("sp", pscale[:, :], pscale_d[:, :], writes=[CONSTB])
            P.dma("sp", convw[:, :], convw_d[:, :], writes=[CONSTB])
            P.dma("sp", rcnt[:, :], rcnt_d[:, :], writes=[CONSTB])
            P.dma("sp", invw[:, :], invw_d[:, :], writes=[CONSTB])
            P.dma("sp", esrow_f[:, :], sink_d[:, :], writes=[TMP[0]])
            P.dma("pool", sel[:, :], sel_d[:, :], writes=[CONSTB])
            P.dma("pool", pwbd[:, :, :], pwbd_d.rearrange("a p n -> p a n"), writes=[CONSTB])
            P.dma("pool", masks[:, :], mask_d[:, :], writes=[CONSTB])
            P.dma("pool", cs64[:, :], cs64_d[:, :], writes=[CONSTB])
            actf(esrow[:, :], esrow_f[:, :], AF.Exp, [CONSTB, TMP[0]], [CONSTB])
            actf(csil[:, :], cvec[:, :], AF.Silu, [CV], [CV])
            mset("pool", xc[:, :, :], 0.0, [XC])
            P.dma("sp", xc[:, :, 1:CT + 1], fm(ctxT), writes=[XC])
            mset("pool", s1[:, :, :], 0.0, [S1])
            for (U, UB, n) in ((UL, ULB, T), (UC, UCB, CT)):
                Uv = U.rearrange("(c p) t -> p c t", p=128)
                P.dma("sp", Uv[:, :, 0:8], s1[:, :, 0:8], reads=[S1], writes=[UB])
                P.dma("sp", Uv[:, :, n + 8:n + 16], s1[:, :, 0:8], reads=[S1], writes=[UB])

        def mod_phase(l):
            pm, PM = next_ps()
            for blk in range(12):
                wv, WB = W.req(w_mod_d[l].rearrange("(k p) n -> p k n", p=128)[:, :, blk * 512:(blk + 1) * 512], 128, 8, 512)
                for j in range(4):
                    oc = blk * 4 + j
                    for kc in range(8):
                        mm(pm[:, oc * 2:oc * 2 + 2], wv[:, kc, j * 128:(j + 1) * 128] if wv is not None else None,
                           csil[:, kc * 2:kc * 2 + 2], kc == 0, kc == 7, [WB, CV], [PM])
            MOD = MODS[l]
            mT = bmod[:, l * 96:(l + 1) * 96]
            tt("dve", mT, pm[:, 0:96], mT, ALU.add, [PM, CONSTB], [MOD])
            stt("dve", nmix[:, l * 16:(l + 1) * 16], mT[:, 16:32], 1.0, nmix[:, l * 16:(l + 1) * 16], ALU.add, ALU.mult, [MOD, CONSTB], [MOD])
            stt("dve", nffn[:, l * 16:(l + 1) * 16], mT[:, 64:80], 1.0, nffn[:, l * 16:(l + 1) * 16], ALU.add, ALU.mult, [MOD, CONSTB], [MOD])
            if l == 0:
                dump("modT", mT, [128, 96], [MOD])

        def modv(idx, kc, s):
            c = LCUR[0] * 96 + (idx * 8 + kc) * 2 + s
            return bmod[:, c:c + 1]

        def gmA():
            return nmix[:, LCUR[0] * 16:(LCUR[0] + 1) * 16]

        def gmB():
            return nffn[:, LCUR[0] * 16:(LCUR[0] + 1) * 16]

        def norm(xv, XBUF, ranges, gmt, shidx, s, hv, HBUF, zero_cols=()):
            for (c0, n) in ranges:
                pn, PN = next_ps()
                for kc in range(8):
                    k2 = kc % 2
                    tt("pool", sq[k2][:, 0:n], xv(kc, c0, n), xv(kc, c0, n), ALU.mult, [XBUF], [SQ[k2]])
                    mm(pn[:, 0:n], ones[:, :], sq[k2][:, 0:n], kc == 0, kc == 7, [ONES, SQ[k2]], [PN])
                actf(rstd[:, 0:n], pn[:, 0:n], AF.Sqrt, [PN], [RSTD], bias=EPS, scale=1.0 / D)
                P.op("dve", lambda e, n=n: e.reciprocal(out=rstd[:, 0:n], in_=rstd[:, 0:n]), [RSTD], [RSTD])
                for kc in range(8):
                    k3 = kc % 3
                    tt("dve", tmp[k3][:, 0:n], xv(kc, c0, n), rstd[:, 0:n], ALU.mult, [XBUF, RSTD], [TMP[k3]])
                    if shidx is None:
                        actf(hv(kc, c0, n), tmp[k3][:, 0:n], AF.Identity, [TMP[k3], CONSTB], [HBUF], scale=gmt[:, kc:kc + 1])
                    else:
                        actf(hv(kc, c0, n), tmp[k3][:, 0:n], AF.Identity, [TMP[k3], MODS[LCUR[0]]], [HBUF],
                             bias=modv(shidx, kc, s), scale=gmt[:, kc * 2 + s:kc * 2 + s + 1])
            for c in zero_cols:
                mset("pool", hv(0, c, 1).rearrange("p n -> p n") if False else cur.h[:, :, c:c + 1], 0.0, [HBUF])

        def p1_tile(l, S, n0, n, xv, XBUF, hoff, do_norm=True, hook=None):
            lat = S == "L"
            s = 0 if lat else 1
            hv = lambda kc, c0, m: cur.h[:, kc, hoff + c0:hoff + c0 + m]
            if do_norm:
                norm(xv, XBUF, [(0, n)], gmA(), 0, s, hv, cur.HB)
            if hook is not None:
                hook()
            if l == 0 and lat and n0 == 0:
                dump("h", cur.h[:, :, 0:n], [128, 8, n], [cur.HB])
            win_ = w_in_d[l].rearrange("(k p) n -> p k n", p=128)
            wa, WA = W.req(win_[:, :, 0:384], 128, 8, 384, key=("p1a", l))
            wb, WBb = W.req(win_[:, :, 384:896], 128, 8, 512, key=("p1b", l))
            hh = lambda kc: cur.h[:, kc, hoff:hoff + n]
            pk, PK = next_ps()
            for kc in range(8):
                mm(pk[:, 0:n], wa[:, kc, 0:128] if wa is not None else None, hh(kc), kc == 0, kc == 7, [WA, cur.HB], [PK])
            if lat:
                pkp, PKP = next_ps()
                for kc in range(8):
                    mm(pkp[:, 0:n], wa[:, kc, 128:256] if wa is not None else None, hh(kc), kc == 0, kc == 7, [WA, cur.HB], [PKP])
                P.dma("sp", rope[:, 0, 0:n], ropeC_d[:, n0:n0 + n], writes=[ROPE])
                P.dma("sp", rope[:, 1, 0:n], ropeS_d[:, n0:n0 + n], writes=[ROPE])
                tt("dve", tmp[0][:, 0:n], pk[:, 0:n], rope[:, 0, 0:n], ALU.mult, [PK, ROPE], [TMP[0]])
                tt("dve", tmp[1][:, 0:n], pkp[:, 0:n], rope[:, 1, 0:n], ALU.mult, [PKP, ROPE], [TMP[1]])
                tt("pool", kT[:, n0:n0 + n], tmp[0][:, 0:n], tmp[1][:, 0:n], ALU.add, [TMP[0], TMP[1]], [KT])
            else:
                actf(kcT[:, 0:n], pk[:, 0:n], AF.Copy, [PK], [KCT])
            pv, PV = next_ps()
            nb = n // 128
            for b in range(nb):
                for kc in range(8):
                    mm(pv[:, b * 128:(b + 1) * 128], cur.h[:, kc, hoff + b * 128:hoff + (b + 1) * 128],
                       wa[:, kc, 256:384] if wa is not None else None, kc == 0, kc == 7, [WA, cur.HB], [PV])
            if lat:
                actf(V[:, n0 // 128:n0 // 128 + nb, :], pv[:, 0:n].rearrange("p (b c) -> p b c", b=nb), AF.Copy, [PV], [VB])
            else:
                actf(Vc[:, 0:nb, :], pv[:, 0:n].rearrange("p (b c) -> p b c", b=nb), AF.Copy, [PV], [VCB])
            if l == DEPTH - 1 and not lat:
                return
            for c in range(2):
                pu, PU = next_ps()
                for kc in range(8):
                    mm(pu[:, 0:n], wb[:, kc, c * 128:(c + 1) * 128] if wb is not None else None, hh(kc), kc == 0, kc == 7, [WBb, cur.HB], [PU])
                actf(uloc[:, c, 0:n], pu[:, 0:n], AF.Copy, [PU], [ULOC])
            U, UB = (UL, ULB) if lat else (UC, UCB)
            P.dma("sp", U.rearrange("(c p) t -> p c t", p=128)[:, :, 8 + n0:8 + n0 + n], uloc[:, :, 0:n], reads=[ULOC], writes=[UB])
            for b0 in range(0, nb, 2):
                pf, PF = next_ps()
                nbb = min(2, nb - b0)
                for b in range(nbb):
                    for kc in range(8):
                        mm(pf[:, b * 256:(b + 1) * 256], cur.h[:, kc, hoff + (b0 + b) * 128:hoff + (b0 + b + 1) * 128],
                           wb[:, kc, 256:512] if wb is not None else None, kc == 0, kc == 7, [WBb, cur.HB], [PF])
                src = pf[:, 0:nbb * 256].rearrange("p (b c) -> p b c", b=nbb)
                if lat:
                    cp("dve", Ftok[:, n0 // 128 + b0:n0 // 128 + b0 + nbb, :], src, [PF], [FB])
                else:
                    cp("dve", Fc[:, b0:b0 + nbb, :], src, [PF], [FCB])

        def p2_tile(l, S, t0, n, xv, XBUF, hoff, do_norm=True, hook=None):
            lat = S == "L"
            s = 0 if lat else 1
            Tn = T if lat else CT
            dd = l == 0 and lat and t0 == 0
            hv = lambda kc, c0, m: cur.h[:, kc, hoff + c0:hoff + c0 + m]
            hh = lambda kc: cur.h[:, kc, hoff:hoff + n]
            if do_norm:
                norm(xv, XBUF, [(0, n)], gmA(), 0, s, hv, cur.HB)
            win_ = w_in_d[l].rearrange("(k p) n -> p k n", p=128)
            wq, WQ = W.req(win_[:, :, 896:1408], 128, 8, 512, key=("q", l))
            if lat:
                wqp, WQP = W.req(win_[:, :, 1408:1920], 128, 8, 512, key=("qp", l))
                P.dma("sp", rope[:, 0, 0:n], ropeC_d[:, t0:t0 + n], writes=[ROPE])
                P.dma("sp", rope[:, 1, 0:n], ropeS_d[:, t0:t0 + n], writes=[ROPE])
            for j in range(4):
                pq, PQ = next_ps()
                for kc in range(8):
                    mm(pq[:, 0:n], wq[:, kc, j * 128:(j + 1) * 128] if wq is not None else None, hh(kc), kc == 0, kc == 7, [WQ, cur.HB], [PQ])
                if lat:
                    pqp, PQP = next_ps()
                    for kc in range(8):
                        mm(pqp[:, 0:n], wqp[:, kc, j * 128:(j + 1) * 128] if wqp is not None else None, hh(kc), kc == 0, kc == 7, [WQP, cur.HB], [PQP])
                    tt("dve", tmp[0][:, 0:n], pq[:, 0:n], rope[:, 0, 0:n], ALU.mult, [PQ, ROPE], [TMP[0]])
                    tt("dve", tmp[1][:, 0:n], pqp[:, 0:n], rope[:, 1, 0:n], ALU.mult, [PQP, ROPE], [TMP[1]])
                    tt("pool", qrot[:, j, 0:n], tmp[0][:, 0:n], tmp[1][:, 0:n], ALU.add, [TMP[0], TMP[1]], [QR])
                else:
                    actf(qrot[:, j, 0:n], pq[:, 0:n], AF.Copy, [PQ], [QR])
            U, UB = (UL, ULB) if lat else (UC, UCB)
            L_ = n + 16
            P.dma("sp", uloc[:, :, 0:L_], U.rearrange("(c p) t -> p c t", p=128)[:, :, t0:t0 + L_], reads=[UB], writes=[ULOC])
            tt("pool", s1[:, :, 1:L_], uloc[:, :, 1:L_], uloc[:, :, 0:L_ - 1], ALU.add, [ULOC], [S1])
            tt("pool", s2[:, 3:L_], s1[:, 1, 3:L_], s1[:, 1, 1:L_ - 2], ALU.add, [S1], [S2])
            tt("pool", s4[64:128, 7:L_], s2[64:128, 7:L_], s2[64:128, 3:L_ - 4], ALU.add, [S2], [S4])
            cp("pool", win[0:64, 0, 0:n], s1[0:64, 0, 8:8 + n], [S1], [WIN])
            tt("pool", win[64:128, 0, 0:n], s1[64:128, 0, 7:7 + n], s1[64:128, 0, 9:9 + n], ALU.add, [S1], [WIN])
            tt("pool", win[0:64, 1, 0:n], s2[0:64, 7:7 + n], s2[0:64, 11:11 + n], ALU.add, [S2], [WIN])
            tt("pool", win[64:128, 1, 0:n], s4[64:128, 7:7 + n], s4[64:128, 15:15 + n], ALU.add, [S4], [WIN])
            for c in range(2):
                stt("dve", pooled[:, c, 0:n], win[:, c, 0:n], invw[:, c:c + 1], uloc[:, c, 8:8 + n], ALU.mult, ALU.subtract,
                    [WIN, CONSTB, ULOC], [PLD])
                if t0 == 0:
                    tt("pool", s2[:, 0:8], win[:, c, 0:8], rcnt[:, c * 16:c * 16 + 8], ALU.mult, [WIN, CONSTB, S2], [S2])
                    tt("pool", pooled[:, c, 0:8], s2[:, 0:8], uloc[:, c, 8:16], ALU.subtract, [S2, ULOC], [PLD])
                if t0 + n == Tn:
                    tt("pool", s2[:, 0:8], win[:, c, n - 8:n], rcnt[:, c * 16 + 8:c * 16 + 16], ALU.mult, [WIN, CONSTB, S2], [S2])
                    tt("pool", pooled[:, c, n - 8:n], s2[:, 0:8], uloc[:, c, n:n + 8], ALU.subtract, [S2, ULOC], [PLD])

            def pool_mm():
                for c in range(2):
                    pp, PP = next_ps()
                    mm(pp[:, 0:n], pwbd[:, l * 2 + c, :], pooled[:, c, 0:n], True, True, [CONSTB, PLD], [PP])
                    actf(ypool[:, c, 0:n], pp[:, 0:n], AF.Identity, [PP, CONSTB], [YP], scale=pscale[:, l * 2 + c:l * 2 + c + 1])
            ntt = Tn // 128
            ct_, st_ = (ctab_d, stab_d) if lat else (ctabc_d, stabc_d)
            Fsrc, FBUF = (Ftok, FB) if lat else (Fc, FCB)
            pg = [(ps[b_], PS[b_]) for b_ in range(4)]
            nblk = max(1, ntt // 8)
            per = ntt // nblk
            fops = []
            fpos = [0]
            for bi in range(nblk):
                for ti, tab in enumerate((ct_, st_)):
                    holder = {}
                    for fc in range(2):
                        for k in range(per):
                            def op_(bi=bi, ti=ti, tab=tab, fc=fc, k=k, holder=holder):
                                if "tv" not in holder:
                                    if lat:
                                        tsrc = tab[t0 // TT, bi].rearrange("p (k n) -> p k n", k=per)
                                    else:
                                        tsrc = tab.rearrange("(k p) t -> p k t", p=128)[:, bi * per:(bi + 1) * per, t0:t0 + n]
                                    holder["tv"], holder["TB"] = W.req(tsrc, 128, per, n, "sp")
                                tv, TB = holder["tv"], holder["TB"]
                                tti = bi * per + k
                                pgt, PGT = pg[ti * 2 + fc]
                                mm(pgt[:, 0:n], Fsrc[:, tti, fc * 128:(fc + 1) * 128], tv[:, k, :] if tv is not None else None,
                                   tti == 0, tti == ntt - 1, [FBUF, TB], [PGT])
                            fops.append(op_)
            FG = 4 if lat else 1

            def f_emit(cnt):
                for _ in range(cnt):
                    if fpos[0] < len(fops):
                        fops[fpos[0]]()
                        fpos[0] += 1

            sbank = [0]
            nqb = n // 128

            def att_front(qb, hk):
                gq = (t0 // 128) + qb
                pr = slice(hk * 64, (hk + 1) * 64)
                chunks = []
                if lat:
                    for d_, mi in ((-1, 0), (0, None), (1, 1)):
                        kb = gq + d_
                        if 0 <= kb < T // 128:
                            chunks.append((kT[pr, kb * 128:(kb + 1) * 128], V[:, kb, hk * 64:(hk + 1) * 64], mi, [KT, VB]))
                for cb in range(2):
                    chunks.append((kcT[pr, cb * 128:(cb + 1) * 128], Vc[:, cb, hk * 64:(hk + 1) * 64], None, [KCT, VCB]))
                rhs_q = qrot[pr, :, qb * 128:(qb + 1) * 128]
                if ptc[0] % 2 == 0:
                    PTv = [PT[:, ci_, :] for ci_ in range(5)]
                    PTb = PTB
                else:
                    PTv = [gT[:, 0, :], gT[:, 1, :], gT[:, 2, :], gT[:, 3, :], yfour[:, 0, :]]
                    PTb = PTB2
                ptc[0] += 1
                for ci, (kap, vap, mi, bb) in enumerate(chunks):
                    bsel = 4 + sbank[0] % 2
                    sbank[0] += 1
                    pS, PSb = ps[bsel], PS[bsel]
                    mm(pS[:, :].rearrange("p (g q) -> p g q", g=4), kap, rhs_q, True, True, [bb[0], QR], [PSb])
                    actf(PTv[ci], pS[:, :], AF.Exp, [PSb], [PTb[ci]], scale=0.125)
                    if mi is not None:
                        tt("pool", PTv[ci], PTv[ci], masks[:, mi * 512:(mi + 1) * 512], ALU.mult, [PTb[ci], CONSTB], [PTb[ci]])
                    if ci >= 1:
                        f_emit(FG)
                return chunks, PTv, PTb

            def att_back(qb, hk, st):
                chunks, PTv, PTb = st
                po, PO = ps[6], PS[6]
                pd, PD = ps[7], PS[7]
                nc_ = len(chunks)
                for ci, (kap, vap, mi, bb) in enumerate(chunks):
                    mm(po[0:64, :], vap, PTv[ci], ci == 0, ci == nc_ - 1, [bb[1], PTb[ci]], [PO])
                for ci in range(nc_):
                    mm(pd[0:64, :], ones[:, 0:64], PTv[ci], ci == 0, False, [ONES, PTb[ci]], [PD])
                er = l * 2 + hk
                mm(pd[0:64, :], sel[0:4, er * 64:(er + 1) * 64], esrow[0:4, :], False, True, [CONSTB], [PD])
                P.op("dve", lambda e, pd=pd: e.reciprocal(out=rden[:, :], in_=pd[0:64, :]), [PD], [RDEN])
                tt("dve", oT[:, hk * 4:(hk + 1) * 4, qb * 128:(qb + 1) * 128], po[0:64, :].rearrange("p (g q) -> p g q", g=4),
                   rden[:, :].rearrange("p (g q) -> p g q", g=4), ALU.mult, [PO, RDEN], [OT])

            prev_att = None
            for qb in range(nqb):
                for hk in range(2):
                    st = att_front(qb, hk)
                    if prev_att is not None:
                        att_back(*prev_att)
                    prev_att = (qb, hk, st)
            att_back(*prev_att)
            if dd:
                dump("kT", kT[:, :], [128, T], [KT])
                dump("V", V[:, :, :], [128, T // 128, 128], [VB])
                dump("Ftok", Ftok[:, :, :], [128, T // 128, 256], [FB])
                dump("qrot", qrot[:, :, :], [128, 4, TT], [QR])
                dump("oT", oT[:, :, :], [64, 8, TT], [OT])
            if dd:
                dump("uloc", uloc[:, :, :], [128, 2, TT + 16], [ULOC])
                dump("pooled", pooled[:, :, :], [128, 2, TT], [PLD])
                dump("ypool", ypool[:, :, :], [128, 2, TT], [YP])
            for op_ in fops[fpos[0]:]:
                op_()
            fpos[0] = len(fops)
            for i in range(4):
                pgt, PGT = pg[i]
                if i % 2 == 0:
                    actf(gT[:, i, 0:n], pgt[:, 0:n], AF.Copy, [PGT], [GT] + PTB2)
                else:
                    cp("dve", gT[:, i, 0:n], pgt[:, 0:n], [PGT], [GT] + PTB2)
            pool_mm()
            for fc in range(2):
                pf, PF = next_ps()
                mm(pf[:, 0:n], cs64[:, 0:128], gT[:, fc, 0:n], True, False, [CONSTB, GT], [PF])
                mm(pf[:, 0:n], cs64[:, 128:256], gT[:, 2 + fc, 0:n], False, True, [CONSTB, GT], [PF])
                actf(yfour[:, fc, 0:n], pf[:, 0:n], AF.Copy, [PF], [YF] + PTB2)
            if dd:
                dump("yfour", yfour[:, :, :], [128, 2, TT], [YF])
            if hook is not None:
                hook()
            for oc in range(8):
                wc, WC = W.req(("br", l, oc), 128, 1, 1536, key=("br", l, oc))
                wg, WG = W.req(win_[:, :, 1920 + oc * 384:1920 + (oc + 1) * 384], 128, 8, 384, key=("g", l, oc))
                if wc is not None:
                    wba = wc[0:64, 0, 0:1024].rearrange("p (h n) -> p h n", h=8)
                    wpf = wc[:, 0, 1024:1536].rearrange("p (c n) -> p c n", c=4)
                else:
                    wba = wpf = None
                pa, PA = next_ps()
                for hd in range(8):
                    mm(pa[:, 0:n], wba[:, hd, :] if wba is not None else None, oT[:, hd, 0:n], hd == 0, hd == 7, [WC, OT], [PA])
                pb, PB = next_ps()
                for c in range(2):
                    mm(pb[:, 0:n], wpf[:, c, :] if wpf is not None else None, ypool[:, c, 0:n], c == 0, c == 1, [WC, YP], [PB])
                pc, PC = next_ps()
                for c in range(2):
                    mm(pc[:, 0:n], wpf[:, 2 + c, :] if wpf is not None else None, yfour[:, c, 0:n], c == 0, c == 1, [WC, YF], [PC])
                brs = [(pa, PA), (pb, PB), (pc, PC)]
                for b in range(3):
                    pgt, PGT = next_ps()
                    for kc in range(8):
                        mm(pgt[:, 0:n], wg[:, kc, b * 128:(b + 1) * 128] if wg is not None else None, hh(kc), kc == 0, kc == 7, [WG, cur.HB], [PGT])
                    actf(sg[b][:, 0:n], pgt[:, 0:n], AF.Sigmoid, [PGT], [SG[b]])
                    tt("dve", sg[b][:, 0:n], sg[b][:, 0:n], brs[b][0][:, 0:n], ALU.mult, [SG[b], brs[b][1]], [SG[b]])
                tt("pool", sg[0][:, 0:n], sg[0][:, 0:n], sg[1][:, 0:n], ALU.add, [SG[0], SG[1]], [SG[0]])
                tt("pool", y[:, oc, 0:n], sg[0][:, 0:n], sg[2][:, 0:n], ALU.add, [SG[0], SG[2]], [YB])
            if dd:
                dump("y", y[:, :, :], [128, 8, TT], [YB])
            for half in range(2):
                wo, WO = W.req(wout_d[l].rearrange("(k p) n -> p k n", p=128)[:, :, half * 512:(half + 1) * 512], 128, 8, 512, key=("wo", l, half))
                for j in range(4):
                    oc = half * 4 + j
                    pz, PZ = next_ps()
                    for kc in range(8):
                        mm(pz[:, 0:n], wo[:, kc, j * 128:(j + 1) * 128] if wo is not None else None, y[:, kc, 0:n], kc == 0, kc == 7, [WO, YB], [PZ])
                    stt("dve", xv(oc, 0, n), pz[:, 0:n], modv(2, oc, s), xv(oc, 0, n), ALU.mult, ALU.add, [PZ, MODS[LCUR[0]], XBUF], [XBUF])

        def p3_tile(l, S, t0, n, xv, XBUF, zero_cols, do_norm=True, hook=None):
            lat = S == "L"
            s = 0 if lat else 1
            hv = lambda kc, c0, m: cur.h[:, kc, c0:c0 + m]
            if n == 512:
                ranges = [(0, 512), (510, 4)]
            else:
                ranges = [(0, n + 2)]
            if do_norm:
                norm(xv, XBUF, [(0, min(512, n + 2))] + ([(512, 2)] if n == 512 else []), gmB(), 3, s, hv, cur.HB, zero_cols=zero_cols)
            wup_ = wup_d[l].rearrange("(k p) n -> p k n", p=128)
            pend = [None]
            for jg in range(0, NFF, 4):
                nj = min(4, NFF - jg)
                wv_, WVb = W.req(wup_[:, :, jg * 128:(jg + nj) * 128], 128, 8, nj * 128, key=("wv", l, jg))
                wg_, WGb = W.req(wup_[:, :, DFF + jg * 128:DFF + (jg + nj) * 128], 128, 8, nj * 128, key=("wg", l, jg))
                for jj in range(nj):
                    j = jg + jj
                    for (c0, m) in ranges:
                        pvv, PVV = next_ps()
                        pgg, PGG = next_ps()
                        for kc in range(8):
                            mm(pvv[:, 0:m], wv_[:, kc, jj * 128:(jj + 1) * 128] if wv_ is not None else None, cur.h[:, kc, c0:c0 + m], kc == 0, kc == 7, [WVb, cur.HB], [PVV])
                        for kc in range(8):
                            mm(pgg[:, 0:m], wg_[:, kc, jj * 128:(jj + 1) * 128] if wg_ is not None else None, cur.h[:, kc, c0:c0 + m], kc == 0, kc == 7, [WGb, cur.HB], [PGG])
                        mo = m - 2
                        (cva, cvg, cvs), (CVA, CVG, CVS) = (cvsets if m > 8 else tvsets)[j % 2]
                        for (pp_, PPb, ch, dst, DST) in ((pvv, PVV, j, cva, CVA), (pgg, PGG, NFF + j, cvg, CVG)):
                            cw = lambda k, ch=ch: convw[:, (l * 3 + k) * 44 + ch:(l * 3 + k) * 44 + ch + 1]
                            actf(dst[:, 0:mo], pp_[:, 1:1 + mo], AF.Identity, [PPb, CONSTB], [DST], scale=cw(1))
                            stt("dve", dst[:, 0:mo], pp_[:, 0:mo], cw(0), dst[:, 0:mo], ALU.mult, ALU.add, [PPb, CONSTB, DST], [DST])
                            stt("dve", dst[:, 0:mo], pp_[:, 2:2 + mo], cw(2), dst[:, 0:mo], ALU.mult, ALU.add, [PPb, CONSTB, DST], [DST])
                        def fin_(cva=cva, cvg=cvg, cvs=cvs, CVA=CVA, CVG=CVG, CVS=CVS, j=j, c0=c0, mo=mo):
                            actf(cvs[:, 0:mo], cvg[:, 0:mo], AF.Silu, [CVG], [CVS])
                            tt("pool", act[:, j, c0:c0 + mo], cva[:, 0:mo], cvs[:, 0:mo], ALU.mult, [CVA, CVS], [ACTB])
                        if pend[0] is not None:
                            pend[0]()
                        pend[0] = fin_
            if pend[0] is not None:
                pend[0]()
                pend[0] = None
            if hook is not None:
                hook()
            wdn_ = wdn_d[l].rearrange("(k p) n -> p k n", p=128)
            for oc in range(8):
                wd, WD = W.req(wdn_[:, :, oc * 128:(oc + 1) * 128], 128, NFF, 128, key=("wd", l, oc))
                pz, PZ = next_ps()
                for j in range(NFF):
                    mm(pz[:, 0:n], wd[:, j, :] if wd is not None else None, act[:, j, 0:n], j == 0, j == NFF - 1, [WD, ACTB], [PZ])
                stt("dve", xv(oc, 1, n), pz[:, 0:n], modv(5, oc, s), xv(oc, 1, n), ALU.mult, ALU.add, [PZ, MODS[LCUR[0]], XBUF], [XBUF])

        _orig_req = W.req

        def req2(src, npart, kc, ncol, qn="pool", key=None):
            return _orig_req(src, npart, kc, ncol, qn, key)

        _orig_dma = P.dma

        def dma2(qn, out, in_, reads=(), writes=(), acc=False):
            if isinstance(in_, tuple):
                _, l_, oc_ = in_
                cs = slice(oc_ * 128, (oc_ + 1) * 128)
                _orig_dma(qn, out[0:64, 0, 0:1024].rearrange("p (h n) -> p h n", h=8),
                          wba_d[l_].rearrange("(h p) n -> p h n", p=64)[:, :, cs], reads, writes)
                _orig_dma(qn, out[:, 0, 1024:1280].rearrange("p (c n) -> p c n", c=2),
                          wbp_d[l_].rearrange("(c p) n -> p c n", p=128)[:, :, cs], reads, writes, acc=True)
                return _orig_dma(qn, out[:, 0, 1280:1536].rearrange("p (c n) -> p c n", c=2),
                                 wbf_d[l_].rearrange("(c p) n -> p c n", p=128)[:, :, cs], reads, writes, acc=True)
            return _orig_dma(qn, out, in_, reads, writes, acc)

        W.req = req2
        P.dma = dma2

        xcv = lambda kc, c0, m: xc[:, kc, 1 + c0:1 + c0 + m]
        xcv3 = lambda kc, c0, m: xc[:, kc, c0:c0 + m]
        xtv = lambda kc, c0, m: cur.xt[:, kc, c0:c0 + m]

        def program():
            last_out = []
            for l in range(depth_run):
                last = l == DEPTH - 1
                Xin, XIN = (fm(xT), XB["x0"]) if l == 0 else (fm(X1), XB["x1"])
                LCUR[0] = l
                if l == 0:
                    for l2 in range(depth_run):
                        mod_phase(l2)
                hv0 = lambda kc, c0, m: cur.h[:, kc, c0:c0 + m]

                def p12_prep(i):
                    setcur(i)
                    P.dma("sp", cur.xt[:, :, 0:TT], Xin[:, :, i * TT:(i + 1) * TT], reads=[XIN[i]], writes=[cur.XT])
                    norm(xtv, cur.XT, [(0, TT)], gmA(), 0, 0, hv0, cur.HB)

                def mk_hook(prep, i):
                    def hook():
                        if i + 1 < NTL:
                            prep(i + 1)
                            setcur(i)
                    return hook

                fuse_p1 = (not last) and (l + 1 < depth_run)
                if l == 0:
                    setcur(1)
                    p1_tile(l, "C", 0, CT, xcv, XC, 0)
                    p12_prep(0)
                    for i in range(NTL):
                        setcur(i)
                        p1_tile(l, "L", i * TT, TT, xtv, cur.XT, 0, do_norm=False, hook=mk_hook(p12_prep, i))
                fence()
                p12_prep(0)
                for i in range(NTL):
                    setcur(i)
                    p2_tile(l, "L", i * TT, TT, xtv, cur.XT, 0, do_norm=False, hook=mk_hook(p12_prep, i))
                    P.dma("sp", fm(XM)[:, :, i * TT:(i + 1) * TT], cur.xt[:, :, 0:TT], reads=[cur.XT], writes=[XB["xm"][i]])
                if not last:
                    setcur(NTL)
                    p2_tile(l, "C", 0, CT, xcv, XC, 0)
                    if l == 0:
                        dump("xcmid", xc[:, :, :], [128, 8, CT + 2], [XC])
                fence()

                def p3_prep(i):
                    setcur(i)
                    lo = i * TT - 1
                    zc = []
                    if i == 0:
                        P.dma("sp", cur.xt[:, :, 1:TT + 2], fm(XM)[:, :, 0:TT + 1], reads=[XB["xm"][0], XB["xm"][1]], writes=[cur.XT])
                        mset("pool", cur.xt[:, :, 0:1], 0.0, [cur.XT])
                        zc = [0]
                    elif i == NTL - 1:
                        P.dma("sp", cur.xt[:, :, 0:TT + 1], fm(XM)[:, :, lo:lo + TT + 1], reads=[XB["xm"][i - 1], XB["xm"][i]], writes=[cur.XT])
                        mset("pool", cur.xt[:, :, TT + 1:TT + 2], 0.0, [cur.XT])
                        zc = [TT + 1]
                    else:
                        P.dma("sp", cur.xt[:, :, 0:TT + 2], fm(XM)[:, :, lo:lo + TT + 2],
                              reads=[XB["xm"][i - 1], XB["xm"][i], XB["xm"][i + 1]], writes=[cur.XT])
                    norm(xtv, cur.XT, [(0, 512), (512, 2)], gmB(), 3, 0, hv0, cur.HB, zero_cols=zc)

                if not last:
                    setcur(NTL)
                    p3_tile(l, "C", 0, CT, xcv3, XC, [0, CT + 1])
                    if fuse_p1:
                        LCUR[0] = l + 1
                        setcur(NTL + 1)
                        p1_tile(l + 1, "C", 0, CT, xcv, XC, 0)
                        LCUR[0] = l
                p3_prep(0)
                for i in range(NTL):
                    setcur(i)
                    p3_tile(l, "L", i * TT, TT, xtv, cur.XT, [], do_norm=False, hook=mk_hook(p3_prep, i))
                    if last or depth_run == 1 and l == depth_run - 1:
                        if last:
                            xo = lambda kc, c0, m: cur.xt[:, kc, 1 + c0:1 + c0 + m]
                            norm(xo, cur.XT, [(0, TT)], nfin, None, 0, xo, cur.XT)
                            ev = P.dma("sp", fm(outT)[:, :, i * TT:(i + 1) * TT], cur.xt[:, :, 1:TT + 1], reads=[cur.XT], writes=[XB["out"][i]])
                        else:
                            ev = P.dma("sp", fm(outT)[:, :, i * TT:(i + 1) * TT], cur.xt[:, :, 1:TT + 1], reads=[cur.XT], writes=[XB["out"][i]])
                        last_out.append(ev)
                    else:
                        P.dma("sp", fm(X1)[:, :, i * TT:(i + 1) * TT], cur.xt[:, :, 1:TT + 1], reads=[cur.XT], writes=[XB["x1"][i]])
                        if fuse_p1:
                            LCUR[0] = l + 1
                            xo1 = lambda kc, c0, m: cur.xt[:, kc, 1 + c0:1 + c0 + m]
                            p1_tile(l + 1, "L", i * TT, TT, xo1, cur.XT, 0)
                            LCUR[0] = l
                if l == 0:
                    dump("xmid", fm(XM), [128, 8, T], XB["xm"])
                    dump("xc", xc[:, :, :], [128, 8, CT + 2], [XC])
            return last_out

        P.planning = True
        program()
        P.planning = False
        pspos[0] = 0
        setup()
        outs = program()
        for ev in outs:
            if ev is not None:
                P.final_wait("sp", ev)
        P.emit()
        print(f"[build] insts={P.n_inst} waits={P.n_wait} sems={P.nsem} wblocks={len(W.plan)}")
    if dbg:
        return nc, list(dumps.keys())
    return nc


def _host_consts():
    import ml_dtypes
    bf = ml_dtypes.bfloat16
    c = {}
    t = np.arange(T)
    row = (t // 64).astype(np.float32)
    col = (t % 64).astype(np.float32)
    inv_freq = (np.float32(10000.0) ** (-np.arange(16, dtype=np.float32) / np.float32(16))).astype(np.float32)
    C = np.zeros((128, T), np.float32)
    S = np.zeros((128, T), np.float32)
    for p in range(128):
        d = p % 64
        a, r, f = d // 32, (d % 32) // 16, d % 16
        pos = row if a == 0 else col
        ang = (pos * inv_freq[f]).astype(np.float32)
        C[p] = np.cos(ang)
        S[p] = np.sin(ang) * (-1.0 if r == 0 else 1.0)
    c["ropeC"], c["ropeS"] = C, S
    j = np.arange(128)[:, None]
    i = np.arange(128)[None, :]
    mp = (j >= i).astype(np.float32)
    mn = (j <= i).astype(np.float32)
    c["masks"] = np.concatenate([np.tile(mp, (1, 4)), np.tile(mn, (1, 4))], axis=1).astype(np.float32)
    for nm, N in (("", T), ("c", CT)):
        tt_ = np.arange(N, dtype=np.int64)
        k = (tt_[:, None] * tt_[None, :]) % N
        ang = 2.0 * np.pi * k.astype(np.float64) / N
        sc = 1.0 / np.sqrt(N * 64.0)
        ct = (np.cos(ang) * sc).astype(np.float32).astype(bf)
        st = (-np.sin(ang) * sc).astype(np.float32).astype(bf)
        if N == T:
            lay = lambda a: np.ascontiguousarray(a.reshape(4, 8, 128, T // TT, TT).transpose(3, 0, 2, 1, 4)).reshape(T // TT, 4, 128, 4096)
            ct, st = lay(ct), lay(st)
        c["ctab" + nm] = ct
        c["stab" + nm] = st
    cc = np.arange(64, dtype=np.int64)
    k = (cc[:, None] * cc[None, :]) % 64
    ang = 2.0 * np.pi * k.astype(np.float64) / 64
    C64, S64 = np.cos(ang), np.sin(ang)
    cs = np.zeros((128, 256), np.float32)
    for g in range(2):
        cs[g * 64:(g + 1) * 64, g * 64:(g + 1) * 64] = C64
        cs[g * 64:(g + 1) * 64, 128 + g * 64:128 + (g + 1) * 64] = S64
    c["cs64"] = cs
    wins = (2, 4, 8, 16)
    invw = np.zeros((128, 2), np.float32)
    rc = np.ones((128, 32), np.float32)
    for g, w in enumerate(wins):
        ch, half = g // 2, g % 2
        pr = slice(half * 64, (half + 1) * 64)
        invw[pr, ch] = 1.0 / w
        for e in range(8):
            cnt_first = min(e - w // 2 + w, 10 ** 9) - max(e - w // 2, 0)
            rc[pr, ch * 16 + e] = 1.0 / cnt_first
            cnt_last = min(8 - e + w // 2, w)
            rc[pr, ch * 16 + 8 + e] = 1.0 / cnt_last
    c["invw"], c["rcnt"] = invw, rc
    return c


_CONSTS = None
_NC = None
_DBG_HOOK = None


def _fmv(v):
    return np.ascontiguousarray(v.reshape(-1, 128).T)


def kernel(x, c, ctx, c_ctx, w_mod, b_mod, norm_mix, norm_ffn, w_in, attn_sink, pool_w, pool_scale,
           w_br_attn, w_br_pool, w_br_four, w_out, w_up, conv_w, w_down, norm_final):
    global _CONSTS, _NC
    f32 = np.float32
    A = lambda a: np.ascontiguousarray(np.asarray(a, dtype=f32))
    x, c, ctx, c_ctx = A(x), A(c), A(ctx), A(c_ctx)
    w_mod, b_mod, norm_mix, norm_ffn, w_in = A(w_mod), A(b_mod), A(norm_mix), A(norm_ffn), A(w_in)
    attn_sink, pool_w, pool_scale = A(attn_sink), A(pool_w), A(pool_scale)
    w_br_attn, w_br_pool, w_br_four, w_out = A(w_br_attn), A(w_br_pool), A(w_br_four), A(w_out)
    w_up, conv_w, w_down, norm_final = A(w_up), A(conv_w), A(w_down), A(norm_final)
    if _CONSTS is None:
        _CONSTS = _host_consts()
    if _NC is None:
        _NC = build_program()
    K = _CONSTS
    d = np.arange(64)
    swap = (d // 32) * 32 + (1 - (d % 32) // 16) * 16 + d % 16
    k_cols = np.arange(0, 128)
    kp_cols = np.concatenate([hh * 64 + swap for hh in range(2)])
    v_cols = np.arange(128, 256)
    u_cols = np.arange(768, 1024)
    f_cols = np.arange(1024, 1280)
    q_cols = np.concatenate([np.concatenate([256 + j * 64 + d, 256 + (4 + j) * 64 + d]) for j in range(4)])
    qp_cols = np.concatenate([np.concatenate([256 + j * 64 + swap, 256 + (4 + j) * 64 + swap]) for j in range(4)])
    g_cols = np.concatenate([np.concatenate([1280 + b * 1024 + oc * 128 + np.arange(128) for b in range(3)]) for oc in range(8)])
    cols = np.concatenate([k_cols, kp_cols, v_cols, u_cols, f_cols, q_cols, qp_cols, g_cols])
    assert cols.shape[0] == INW2
    w_in2 = np.ascontiguousarray(w_in[:, :, cols])
    bmod = np.concatenate([np.repeat(_fmv(b_mod[l])[:, :, None], 2, axis=2).reshape(128, 96) for l in range(DEPTH)], axis=1)
    nmix = np.concatenate([np.repeat(_fmv(norm_mix[l])[:, :, None], 2, axis=2).reshape(128, 16) for l in range(DEPTH)], axis=1)
    nffn = np.concatenate([np.repeat(_fmv(norm_ffn[l])[:, :, None], 2, axis=2).reshape(128, 16) for l in range(DEPTH)], axis=1)
    nfin = _fmv(norm_final)
    pscale = np.concatenate([_fmv(pool_scale[l]) for l in range(DEPTH)], axis=1)
    convw = np.concatenate([_fmv(conv_w[l, k]) for l in range(DEPTH) for k in range(3)], axis=1)
    sinkx = np.zeros((DEPTH * 2, 512), f32)
    for l in range(DEPTH):
        for hk in range(2):
            sinkx[l * 2 + hk] = np.repeat(attn_sink[l, hk * 4:(hk + 1) * 4], 128)
    pwbd = np.zeros((DEPTH * 2, 128, 128), f32)
    for l in range(DEPTH):
        for g in range(4):
            ch, half = g // 2, g % 2
            pwbd[l * 2 + ch, half * 64:(half + 1) * 64, half * 64:(half + 1) * 64] = pool_w[l, g]
    selm = np.zeros((4, 256), f32)
    for r in range(4):
        selm[r, r * 64:(r + 1) * 64] = 1.0
    shared = {
        "sel": selm,
        "w_mod": w_mod, "bmod": np.ascontiguousarray(bmod), "nmix": np.ascontiguousarray(nmix),
        "nffn": np.ascontiguousarray(nffn), "nfin": nfin, "w_in2": w_in2, "sinkx": sinkx, "pwbd": pwbd,
        "pscale": np.ascontiguousarray(pscale), "w_br_attn": w_br_attn, "w_br_pool": w_br_pool,
        "w_br_four": w_br_four, "w_out": w_out, "w_up": w_up, "convw": np.ascontiguousarray(convw),
        "w_down": w_down, "ropeC": K["ropeC"], "ropeS": K["ropeS"], "masks": K["masks"],
        "ctab": K["ctab"], "stab": K["stab"], "ctabc": K["ctabc"], "stabc": K["stabc"], "cs64": K["cs64"],
        "rcnt": K["rcnt"], "invw": K["invw"],
    }
    in_maps = []
    for b in range(NCORES):
        cv = np.stack([_fmv(c[b]), _fmv(c_ctx)], axis=2).reshape(128, 16)
        m = dict(shared)
        m["xT"] = np.ascontiguousarray(x[b].T)
        m["ctxT"] = np.ascontiguousarray(ctx[b].T)
        m["cvec"] = np.ascontiguousarray(cv)
        in_maps.append(m)
    if _DBG_HOOK is not None:
        return _DBG_HOOK(in_maps)
    res = run_bass_kernel_spmd(_NC, in_maps, core_ids=list(range(NCORES)))
    out = np.stack([np.ascontiguousarray(res.results[b]["outT"].T) for b in range(NCORES)], axis=0)
    return out.astype(np.float32)
```

```python
import numpy as np
from contextlib import ExitStack
import concourse.bass as bass
import concourse.mybir as mybir
from concourse.bass_utils import run_bass_kernel_spmd

F32 = mybir.dt.float32
BF16 = mybir.dt.bfloat16
AF = mybir.ActivationFunctionType
ALU = mybir.AluOpType

D = 1024
T = 4096
CT = 256
DEPTH = 2
DFF = 2816
NFF = 22
INW2 = 4992
TT = 512
NCORES = 4
EPS = 1e-6


class Buf:
    __slots__ = ("name", "w", "r")

    def __init__(self, name):
        self.name = name
        self.w = []
        self.r = []


class Prog:
    ENG = ("pe", "act", "dve", "pool", "sp")
    SEM_ROLL = 30000
    NDMASEM = 12

    def __init__(self, nc, es):
        self.nc = nc
        self.es = es
        self.planning = False
        self.q = {e: [] for e in self.ENG}
        self.esem = {}
        self.ecnt = {}
        self.waited = {e: {} for e in self.ENG}
        self.nsem = 0
        for e in self.ENG:
            self._new_esem(e)
        self.dsem = {}
        self.dpos = {}
        for qn in ("sp", "act", "pool"):
            self.dsem[qn] = [[self._sem(f"d_{qn}_{i}"), 0] for i in range(self.NDMASEM)]
            self.dpos[qn] = 0
        self.n_inst = 0
        self.n_wait = 0

    def _sem(self, name):
        self.nsem += 1
        return self.es.enter_context(self.nc.semaphore(f"{name}_{self.nsem}"))

    def _new_esem(self, e):
        self.esem[e] = self._sem(f"e_{e}")
        self.ecnt[e] = 0

    def _wait(self, eng, ev):
        if ev is None:
            return
        sem, val, src = ev
        if src == "pe" and eng == "pe":
            return
        k = id(sem)
        if self.waited[eng].get(k, 0) >= val:
            return
        self.waited[eng][k] = val
        self.q[eng].append(("w", sem, val))
        self.n_wait += 1

    def _deps(self, eng, reads, writes, acc=False):
        for b in reads:
            for ev in b.w:
                self._wait(eng, ev)
        for b in writes:
            if not acc:
                for ev in b.w:
                    self._wait(eng, ev)
            for ev in b.r:
                self._wait(eng, ev)

    def _commit(self, ev, reads, writes, acc=False):
        for b in reads:
            if len(b.r) > 48:
                d = {}
                for e2 in b.r:
                    k = id(e2[0])
                    if k not in d or d[k][1] < e2[1]:
                        d[k] = e2
                b.r = list(d.values())
            b.r.append(ev)
        for b in writes:
            if acc:
                b.w.append(ev)
            else:
                b.w = [ev]
            b.r = []

    def op(self, eng, fn, reads=(), writes=()):
        if self.planning:
            return None
        self._deps(eng, reads, writes)
        if self.ecnt[eng] >= self.SEM_ROLL:
            self._new_esem(eng)
        self.ecnt[eng] += 1
        sem = self.esem[eng]
        val = self.ecnt[eng]
        self.q[eng].append(("o", fn, sem))
        ev = (sem, val, eng)
        self._commit(ev, reads, writes)
        self.n_inst += 1
        return ev

    def dma(self, qn, out, in_, reads=(), writes=(), acc=False):
        if self.planning:
            return None
        self._deps(qn, reads, writes, acc)
        pos = self.dpos[qn]
        self.dpos[qn] = (pos + 1) % self.NDMASEM
        slot = self.dsem[qn][pos]
        sem, cnt = slot
        if cnt > 0:
            self._wait(qn, (sem, cnt, "dma"))
        if cnt >= self.SEM_ROLL:
            sem = self._sem(f"d_{qn}")
            cnt = 0
            slot[0] = sem
        cnt += 16
        slot[1] = cnt
        self.q[qn].append(("d", out, in_, sem))
        ev = (sem, cnt, "dma")
        self._commit(ev, reads, writes, acc)
        self.n_inst += 1
        return ev

    def final_wait(self, eng, ev):
        self.q[eng].append(("w", ev[0], ev[1]))

    def emit(self):
        nc = self.nc
        q = self.q

        def replay(e, lst):
            for it in lst:
                if it[0] == "w":
                    e.wait_ge(it[1], it[2])
                elif it[0] == "o":
                    it[1](e).then_inc(it[2], 1)
                else:
                    e.dma_start(out=it[1], in_=it[2]).then_inc(it[3], 16)

        with nc.Block() as block:
            @block.sync
            def _(e):
                replay(e, q["sp"])

            @block.tensor
            def _(e):
                replay(e, q["pe"])

            @block.scalar
            def _(e):
                replay(e, q["act"])

            @block.vector
            def _(e):
                replay(e, q["dve"])

            @block.gpsimd
            def _(e):
                replay(e, q["pool"])


class WPool:
    SLOT = 4096

    def __init__(self, P, nc, es, nslots=5, ahead=3, scratch=None):
        self.P = P
        self.n = nslots
        self.ahead = ahead
        self.scratch = scratch
        self.tiles = [es.enter_context(nc.sbuf_tensor(f"wslot{i}", [128, self.SLOT], BF16)) for i in range(nslots)]
        self.bufs = [Buf(f"wslot{i}") for i in range(nslots)]
        self.plan = []
        self.issued = 0
        self.cur = 0
        self.keys = {}

    def _view(self, i):
        src, npart, kc, ncol, qn, key = self.plan[i]
        t = self.tiles[i % self.n]
        return t[0:npart, 0:kc * ncol].rearrange("p (k n) -> p k n", k=kc)

    def _issue(self, i):
        src, npart, kc, ncol, qn, key = self.plan[i]
        sbuf_ = self.bufs[i % self.n]
        flat = self.tiles[i % self.n][0:npart, 0:kc * ncol]
        if key is None or self.scratch is None:
            self.P.dma(qn, self._view(i), src, writes=[sbuf_])
        elif key not in self.keys:
            k = len(self.keys)
            SB = Buf(f"scr{k}")
            self.keys[key] = (k, SB)
            self.P.dma(qn, self._view(i), src, writes=[sbuf_])
            self.P.dma("sp", self.scratch[k, 0:npart, 0:kc * ncol], flat, reads=[sbuf_], writes=[SB])
        else:
            k, SB = self.keys[key]
            self.P.dma("sp", flat, self.scratch[k, 0:npart, 0:kc * ncol], reads=[SB], writes=[sbuf_])

    def req(self, src, npart, kc, ncol, qn="pool", key=None):
        assert kc * ncol <= self.SLOT
        if self.P.planning:
            self.plan.append((src, npart, kc, ncol, qn, key))
            return None, None
        while self.issued < min(len(self.plan), self.cur + self.ahead + 1):
            self._issue(self.issued)
            self.issued += 1
        i = self.cur
        self.cur += 1
        assert self.plan[i][1:] == (npart, kc, ncol, qn, key), (i, self.plan[i][1:], (npart, kc, ncol, qn, key))
        return self._view(i), self.bufs[i % self.n]


def build_program(depth_run=DEPTH, dbg=False):
    nc = bass.Bass("TRN2", target_bir_lowering=False)
    es = ExitStack()
    with es:
        P = Prog(nc, es)

        def dram(name, shape, dt, kind):
            return nc.dram_tensor(name, shape, dt, kind=kind).ap()

        def sb(name, shape, dt=F32):
            return es.enter_context(nc.sbuf_tensor(name, shape, dt))

        xT = dram("xT", [D, T], F32, "ExternalInput")
        ctxT = dram("ctxT", [D, CT], F32, "ExternalInput")
        cvec_d = dram("cvec", [128, 16], F32, "ExternalInput")
        w_mod_d = dram("w_mod", [DEPTH, D, 6 * D], F32, "ExternalInput")
        bmod_d = dram("bmod", [128, DEPTH * 96], F32, "ExternalInput")
        nmix_d = dram("nmix", [128, DEPTH * 16], F32, "ExternalInput")
        nffn_d = dram("nffn", [128, DEPTH * 16], F32, "ExternalInput")
        nfin_d = dram("nfin", [128, 8], F32, "ExternalInput")
        w_in_d = dram("w_in2", [DEPTH, D, INW2], F32, "ExternalInput")
        sink_d = dram("sinkx", [DEPTH * 2, 512], F32, "ExternalInput")
        sel_d = dram("sel", [4, 256], F32, "ExternalInput")
        pwbd_d = dram("pwbd", [DEPTH * 2, 128, 128], F32, "ExternalInput")
        pscale_d = dram("pscale", [128, DEPTH * 2], F32, "ExternalInput")
        wba_d = dram("w_br_attn", [DEPTH, 512, D], F32, "ExternalInput")
        wbp_d = dram("w_br_pool", [DEPTH, 256, D], F32, "ExternalInput")
        wbf_d = dram("w_br_four", [DEPTH, 256, D], F32, "ExternalInput")
        wout_d = dram("w_out", [DEPTH, D, D], F32, "ExternalInput")
        wup_d = dram("w_up", [DEPTH, D, 2 * DFF], F32, "ExternalInput")
        convw_d = dram("convw", [128, DEPTH * 3 * 44], F32, "ExternalInput")
        wdn_d = dram("w_down", [DEPTH, DFF, D], F32, "ExternalInput")
        ropeC_d = dram("ropeC", [128, T], F32, "ExternalInput")
        ropeS_d = dram("ropeS", [128, T], F32, "ExternalInput")
        mask_d = dram("masks", [128, 1024], F32, "ExternalInput")
        ctab_d = dram("ctab", [T // TT, 4, 128, 4096], BF16, "ExternalInput")
        stab_d = dram("stab", [T // TT, 4, 128, 4096], BF16, "ExternalInput")
        ctabc_d = dram("ctabc", [CT, CT], BF16, "ExternalInput")
        stabc_d = dram("stabc", [CT, CT], BF16, "ExternalInput")
        cs64_d = dram("cs64", [128, 256], F32, "ExternalInput")
        rcnt_d = dram("rcnt", [128, 32], F32, "ExternalInput")
        invw_d = dram("invw", [128, 2], F32, "ExternalInput")
        outT = dram("outT", [D, T], F32, "ExternalOutput")
        XM = dram("xmid", [D, T], F32, "Internal")
        X1 = dram("x1s", [D, T], F32, "Internal")
        UL = dram("u_lat", [256, T + 16], F32, "Internal")
        UC = dram("u_ctx", [256, CT + 16], F32, "Internal")

        def fm(ap):
            return ap.rearrange("(k p) t -> p k t", p=128)

        NTL = T // TT
        XB = {"x0": [Buf(f"x0_{i}") for i in range(NTL)], "xm": [Buf(f"xm_{i}") for i in range(NTL)],
              "x1": [Buf(f"x1_{i}") for i in range(NTL)], "out": [Buf(f"o_{i}") for i in range(NTL)]}
        ULB = Buf("UL")
        UCB = Buf("UC")

        WSCR = dram("wscr", [96, 128, 4096], BF16, "Internal")
        W = WPool(P, nc, es, nslots=4, ahead=2, scratch=WSCR)
        ones = sb("ones", [128, 128], BF16)
        ONES = Buf("ones")
        kT = sb("kT", [128, T], BF16)
        KT = Buf("kT")
        V = sb("V", [128, T // 128, 128], BF16)
        VB = Buf("V")
        Ftok = sb("Ftok", [128, T // 128, 256], BF16)
        FB = Buf("Ftok")
        kcT = sb("kcT", [128, CT], BF16)
        KCT = Buf("kcT")
        Vc = sb("Vc", [128, 2, 128], BF16)
        VCB = Buf("Vc")
        Fc = sb("Fc", [128, 2, 256], BF16)
        FCB = Buf("Fc")
        xc = sb("xc", [128, 8, CT + 2], F32)
        XC = Buf("xc")
        cvec = sb("cvec_s", [128, 16], F32)
        csil = sb("csil", [128, 16], BF16)
        CV = Buf("cvec")
        MODS = [Buf("mod0"), Buf("mod1")]
        LCUR = [0]
        bmod = sb("bmod_s", [128, DEPTH * 96], F32)
        nmix = sb("nmix_s", [128, DEPTH * 16], F32)
        nffn = sb("nffn_s", [128, DEPTH * 16], F32)
        nfin = sb("nfin_s", [128, 8], F32)
        CONSTB = Buf("consts")
        esrow = sb("esrow", [4, 512], BF16)
        sel = sb("sel_s", [4, 256], BF16)
        pwbd = sb("pwbd_s", [128, DEPTH * 2, 128], BF16)
        pscale = sb("pscale_s", [128, DEPTH * 2], F32)
        convw = sb("convw_s", [128, DEPTH * 3 * 44], F32)
        masks = sb("masks_s", [128, 1024], BF16)
        cs64 = sb("cs64_s", [128, 256], BF16)
        rcnt = sb("rcnt_s", [128, 32], F32)
        invw = sb("invw_s", [128, 2], F32)
        class _Cur:
            pass
        cur = _Cur()
        _xts = [sb(f"xt{i}", [128, 8, TT + 2], F32) for i in range(2)]
        _XTs = [Buf(f"xt{i}") for i in range(2)]
        _hs = [sb(f"h{i}", [128, 8, TT + 2], BF16) for i in range(2)]
        _HBs = [Buf(f"h{i}") for i in range(2)]
        _flipc = [0]

        def flip():
            i = _flipc[0] % 2
            _flipc[0] += 1
            cur.xt, cur.XT, cur.h, cur.HB = _xts[i], _XTs[i], _hs[i], _HBs[i]
        def setcur(k):
            i = k % 2
            cur.xt, cur.XT, cur.h, cur.HB = _xts[i], _XTs[i], _hs[i], _HBs[i]
        flip()
        act = sb("act", [128, NFF, TT], BF16)
        ACTB = Buf("act")
        sq = [sb(f"sq{i}", [128, 512], BF16) for i in range(2)]
        SQ = [Buf(f"sq{i}") for i in range(2)]
        rstd = sb("rstd", [128, TT + 16], F32)
        RSTD = Buf("rstd")
        tmp = [sb(f"tmp{i}", [128, 512], F32) for i in range(3)]
        TMP = [Buf(f"tmp{i}") for i in range(3)]
        esrow_f = tmp[0][0:4, :]
        rope = sb("rope", [128, 2, 512], F32)
        ROPE = Buf("rope")
        qrot = act[:, 8:12, :]
        QR = Buf("qrot")
        oT = act[0:64, 12:20, :]
        OT = Buf("oT")
        PT = sb("PT", [128, 5, 512], BF16)
        PTB = [Buf(f"PT{i}") for i in range(5)]
        PTB2 = [Buf(f"PT2_{i}") for i in range(5)]
        ptc = [0]
        rden = tmp[2][0:64, :]
        RDEN = TMP[2]
        uloc = sb("uloc", [128, 2, TT + 16], F32)
        ULOC = Buf("uloc")
        s1 = sb("s1", [128, 2, TT + 16], F32)
        S1 = Buf("s1")
        s2 = sb("s2", [128, TT + 16], F32)
        S2 = Buf("s2")
        s4 = rstd
        S4 = RSTD
        win = sb("win", [128, 2, TT], F32)
        WIN = Buf("win")
        pooled = act[:, 20:22, :]
        PLD = Buf("pooled")
        ypool = sb("ypool", [128, 2, TT], BF16)
        YP = Buf("ypool")
        gT = sb("gT", [128, 4, TT], BF16)
        GT = Buf("gT")
        yfour = sb("yfour", [128, 2, TT], BF16)
        YF = Buf("yfour")
        sg = [sb(f"sg{i}", [128, 512], F32) for i in range(3)]
        SG = [Buf(f"sg{i}") for i in range(3)]
        y = act[:, 0:8, :]
        YB = Buf("y")
        sgB = [sb(f"sgB{i}", [128, 512], F32) for i in range(3)]
        SGB = [Buf(f"sgB{i}") for i in range(3)]
        cvsets = [(sg, SG), (sgB, SGB)]
        tvs = [[sb(f"tv{a}{b}", [128, 8], F32) for b in range(3)] for a in range(2)]
        TVS = [[Buf(f"tv{a}{b}") for b in range(3)] for a in range(2)]
        tvsets = [(tvs[0], TVS[0]), (tvs[1], TVS[1])]

        def fence():
            P.op("pool", lambda e: e.memset(cvec[:, 0:1], 0.0), [CV], [ACTB, QR, OT, YB, PLD])
        ps = [es.enter_context(nc.psum_tensor(f"ps{i}", [128, 512], F32)) for i in range(8)]
        PS = [Buf(f"ps{i}") for i in range(8)]
        pspos = [0]

        def next_ps():
            i = pspos[0]
            pspos[0] = (i + 1) % 8
            return ps[i], PS[i]

        dumps = {}

        def dump(name, src, shape, reads):
            if not dbg or P.planning or name in dumps:
                return
            dumps[name] = dram("dbg_" + name, list(shape), F32, "ExternalOutput")
            P.dma("pool", dumps[name], src, reads=reads, writes=[Buf("dbg_" + name)])

        def mm(out, lhsT, rhs, start, stop, reads, writes):
            P.op("pe", lambda e: e.matmul(out, lhsT=lhsT, rhs=rhs, start=start, stop=stop), reads, writes)

        def actf(out, in_, func, reads, writes, bias=None, scale=None, eng="act"):
            kw = {}
            if bias is not None:
                kw["bias"] = bias
            if scale is not None:
                kw["scale"] = scale
            P.op("act", lambda e: e.activation(out=out, in_=in_, func=func, **kw), reads, writes)

        def tt(eng, out, in0, in1, op, reads, writes):
            P.op(eng, lambda e: e.tensor_tensor(out=out, in0=in0, in1=in1, op=op), reads, writes)

        def stt(eng, out, in0, scalar, in1, op0, op1, reads, writes):
            P.op(eng, lambda e: e.scalar_tensor_tensor(out=out, in0=in0, scalar=scalar, in1=in1, op0=op0, op1=op1), reads, writes)

        def ts(eng, out, in0, s1_, s2_, op0, op1, reads, writes):
            if op1 is None:
                P.op(eng, lambda e: e.tensor_scalar(out=out, in0=in0, scalar1=s1_, scalar2=None, op0=op0), reads, writes)
            else:
                P.op(eng, lambda e: e.tensor_scalar(out=out, in0=in0, scalar1=s1_, scalar2=s2_, op0=op0, op1=op1), reads, writes)

        def cp(eng, out, in_, reads, writes):
            P.op(eng, lambda e: e.tensor_copy(out=out, in_=in_), reads, writes)

        def mset(eng, ap, val, writes):
            P.op(eng, lambda e: e.memset(ap, val), (), writes)

        def setup():
            mset("pool", ones[:, :], 1.0, [ONES])
            P.dma("sp", cvec[:, :], cvec_d[:, :], writes=[CV])
            P.dma("sp", bmod[:, :], bmod_d[:, :], writes=[CONSTB])
            P.dma("sp", nmix[:, :], nmix_d[:, :], writes=[CONSTB])
            P.dma("sp", nffn[:, :], nffn_d[:, :], writes=[CONSTB])
            P.dma("sp", nfin[:, :], nfin_d[:, :], writes=[CONSTB])
            P.dma("sp", pscale[:, :], pscale_d[:, :], writes=[CONSTB])
            P.dma("sp", convw[:, :], convw_d[:, :], writes=[CONSTB])
            P.dma("sp", rcnt[:, :], rcnt_d[:, :], writes=[CONSTB])
            P.dma("sp", invw[:, :], invw_d[:, :], writes=[CONSTB])
            P.dma("sp", esrow_f[:, :], sink_d[:, :], writes=[TMP[0]])
            P.dma("pool", sel[:, :], sel_d[:, :], writes=[CONSTB])
            P.dma("pool", pwbd[:, :, :], pwbd_d.rearrange("a p n -> p a n"), writes=[CONSTB])
            P.dma("pool", masks[:, :], mask_d[:, :], writes=[CONSTB])
            P.dma("pool", cs64[:, :], cs64_d[:, :], writes=[CONSTB])
            actf(esrow[:, :], esrow_f[:, :], AF.Exp, [CONSTB, TMP[0]], [CONSTB])
            actf(csil[:, :], cvec[:, :], AF.Silu, [CV], [CV])
            mset("pool", xc[:, :, :], 0.0, [XC])
            P.dma("sp", xc[:, :, 1:CT + 1], fm(ctxT), writes=[XC])
            mset("pool", s1[:, :, :], 0.0, [S1])
            for (U, UB, n) in ((UL, ULB, T), (UC, UCB, CT)):
                Uv = U.rearrange("(c p) t -> p c t", p=128)
                P.dma("sp", Uv[:, :, 0:8], s1[:, :, 0:8], reads=[S1], writes=[UB])
                P.dma("sp", Uv[:, :, n + 8:n + 16], s1[:, :, 0:8], reads=[S1], writes=[UB])

        def mod_phase(l):
            pm, PM = next_ps()
            for blk in range(12):
                wv, WB = W.req(w_mod_d[l].rearrange("(k p) n -> p k n", p=128)[:, :, blk * 512:(blk + 1) * 512], 128, 8, 512)
                for j in range(4):
                    oc = blk * 4 + j
                    for kc in range(8):
                        mm(pm[:, oc * 2:oc * 2 + 2], wv[:, kc, j * 128:(j + 1) * 128] if wv is not None else None,
                           csil[:, kc * 2:kc * 2 + 2], kc == 0, kc == 7, [WB, CV], [PM])
            MOD = MODS[l]
            mT = bmod[:, l * 96:(l + 1) * 96]
            tt("dve", mT, pm[:, 0:96], mT, ALU.add, [PM, CONSTB], [MOD])
            stt("dve", nmix[:, l * 16:(l + 1) * 16], mT[:, 16:32], 1.0, nmix[:, l * 16:(l + 1) * 16], ALU.add, ALU.mult, [MOD, CONSTB], [MOD])
            stt("dve", nffn[:, l * 16:(l + 1) * 16], mT[:, 64:80], 1.0, nffn[:, l * 16:(l + 1) * 16], ALU.add, ALU.mult, [MOD, CONSTB], [MOD])
            if l == 0:
                dump("modT", mT, [128, 96], [MOD])

        def modv(idx, kc, s):
            c = LCUR[0] * 96 + (idx * 8 + kc) * 2 + s
            return bmod[:, c:c + 1]

        def gmA():
            return nmix[:, LCUR[0] * 16:(LCUR[0] + 1) * 16]

        def gmB():
            return nffn[:, LCUR[0] * 16:(LCUR[0] + 1) * 16]

        def norm(xv, XBUF, ranges, gmt, shidx, s, hv, HBUF, zero_cols=()):
            for (c0, n) in ranges:
                pn, PN = next_ps()
                for kc in range(8):
                    k2 = kc % 2
                    tt("pool", sq[k2][:, 0:n], xv(kc, c0, n), xv(kc, c0, n), ALU.mult, [XBUF], [SQ[k2]])
                    mm(pn[:, 0:n], ones[:, :], sq[k2][:, 0:n], kc == 0, kc == 7, [ONES, SQ[k2]], [PN])
                actf(rstd[:, 0:n], pn[:, 0:n], AF.Sqrt, [PN], [RSTD], bias=EPS, scale=1.0 / D)
                P.op("dve", lambda e, n=n: e.reciprocal(out=rstd[:, 0:n], in_=rstd[:, 0:n]), [RSTD], [RSTD])
                for kc in range(8):
                    k3 = kc % 3
                    tt("dve", tmp[k3][:, 0:n], xv(kc, c0, n), rstd[:, 0:n], ALU.mult, [XBUF, RSTD], [TMP[k3]])
                    if shidx is None:
                        actf(hv(kc, c0, n), tmp[k3][:, 0:n], AF.Identity, [TMP[k3], CONSTB], [HBUF], scale=gmt[:, kc:kc + 1])
                    else:
                        actf(hv(kc, c0, n), tmp[k3][:, 0:n], AF.Identity, [TMP[k3], MODS[LCUR[0]]], [HBUF],
                             bias=modv(shidx, kc, s), scale=gmt[:, kc * 2 + s:kc * 2 + s + 1])
            for c in zero_cols:
                mset("pool", hv(0, c, 1).rearrange("p n -> p n") if False else cur.h[:, :, c:c + 1], 0.0, [HBUF])

        def p1_tile(l, S, n0, n, xv, XBUF, hoff, do_norm=True, hook=None):
            lat = S == "L"
            s = 0 if lat else 1
            hv = lambda kc, c0, m: cur.h[:, kc, hoff + c0:hoff + c0 + m]
            if do_norm:
                norm(xv, XBUF, [(0, n)], gmA(), 0, s, hv, cur.HB)
            if hook is not None:
                hook()
            if l == 0 and lat and n0 == 0:
                dump("h", cur.h[:, :, 0:n], [128, 8, n], [cur.HB])
            win_ = w_in_d[l].rearrange("(k p) n -> p k n", p=128)
            wa, WA = W.req(win_[:, :, 0:384], 128, 8, 384, key=("p1a", l))
            wb, WBb = W.req(win_[:, :, 384:896], 128, 8, 512, key=("p1b", l))
            hh = lambda kc: cur.h[:, kc, hoff:hoff + n]
            pk, PK = next_ps()
            for kc in range(8):
                mm(pk[:, 0:n], wa[:, kc, 0:128] if wa is not None else None, hh(kc), kc == 0, kc == 7, [WA, cur.HB], [PK])
            if lat:
                pkp, PKP = next_ps()
                for kc in range(8):
                    mm(pkp[:, 0:n], wa[:, kc, 128:256] if wa is not None else None, hh(kc), kc == 0, kc == 7, [WA, cur.HB], [PKP])
                P.dma("sp", rope[:, 0, 0:n], ropeC_d[:, n0:n0 + n], writes=[ROPE])
                P.dma("sp", rope[:, 1, 0:n], ropeS_d[:, n0:n0 + n], writes=[ROPE])
                tt("dve", tmp[0][:, 0:n], pk[:, 0:n], rope[:, 0, 0:n], ALU.mult, [PK, ROPE], [TMP[0]])
                tt("dve", tmp[1][:, 0:n], pkp[:, 0:n], rope[:, 1, 0:n], ALU.mult, [PKP, ROPE], [TMP[1]])
                tt("pool", kT[:, n0:n0 + n], tmp[0][:, 0:n], tmp[1][:, 0:n], ALU.add, [TMP[0], TMP[1]], [KT])
            else:
                actf(kcT[:, 0:n], pk[:, 0:n], AF.Copy, [PK], [KCT])
            pv, PV = next_ps()
            nb = n // 128
            for b in range(nb):
                for kc in range(8):
                    mm(pv[:, b * 128:(b + 1) * 128], cur.h[:, kc, hoff + b * 128:hoff + (b + 1) * 128],
                       wa[:, kc, 256:384] if wa is not None else None, kc == 0, kc == 7, [WA, cur.HB], [PV])
            if lat:
                actf(V[:, n0 // 128:n0 // 128 + nb, :], pv[:, 0:n].rearrange("p (b c) -> p b c", b=nb), AF.Copy, [PV], [VB])
            else:
                actf(Vc[:, 0:nb, :], pv[:, 0:n].rearrange("p (b c) -> p b c", b=nb), AF.Copy, [PV], [VCB])
            if l == DEPTH - 1 and not lat:
                return
            for c in range(2):
                pu, PU = next_ps()
                for kc in range(8):
                    mm(pu[:, 0:n], wb[:, kc, c * 128:(c + 1) * 128] if wb is not None else None, hh(kc), kc == 0, kc == 7, [WBb, cur.HB], [PU])
                actf(uloc[:, c, 0:n], pu[:, 0:n], AF.Copy, [PU], [ULOC])
            U, UB = (UL, ULB) if lat else (UC, UCB)
            P.dma("sp", U.rearrange("(c p) t -> p c t", p=128)[:, :, 8 + n0:8 + n0 + n], uloc[:, :, 0:n], reads=[ULOC], writes=[UB])
            for b0 in range(0, nb, 2):
                pf, PF = next_ps()
                nbb = min(2, nb - b0)
                for b in range(nbb):
                    for kc in range(8):
                        mm(pf[:, b * 256:(b + 1) * 256], cur.h[:, kc, hoff + (b0 + b) * 128:hoff + (b0 + b + 1) * 128],
                           wb[:, kc, 256:512] if wb is not None else None, kc == 0, kc == 7, [WBb, cur.HB], [PF])
                src = pf[:, 0:nbb * 256].rearrange("p (b c) -> p b c", b=nbb)
                if lat:
                    cp("dve", Ftok[:, n0 // 128 + b0:n0 // 128 + b0 + nbb, :], src, [PF], [FB])
                else:
                    cp("dve", Fc[:, b0:b0 + nbb, :], src, [PF], [FCB])

        def p2_tile(l, S, t0, n, xv, XBUF, hoff, do_norm=True, hook=None, rope_loaded=False):
            lat = S == "L"
            s = 0 if lat else 1
            Tn = T if lat else CT
            dd = l == 0 and lat and t0 == 0
            hv = lambda kc, c0, m: cur.h[:, kc, hoff + c0:hoff + c0 + m]
            hh = lambda kc: cur.h[:, kc, hoff:hoff + n]
            if do_norm:
                norm(xv, XBUF, [(0, n)], gmA(), 0, s, hv, cur.HB)
            win_ = w_in_d[l].rearrange("(k p) n -> p k n", p=128)
            wq, WQ = W.req(win_[:, :, 896:1408], 128, 8, 512, key=("q", l))
            if lat:
                wqp, WQP = W.req(win_[:, :, 1408:1920], 128, 8, 512, key=("qp", l))
                if not rope_loaded:
                    P.dma("sp", rope[:, 0, 0:n], ropeC_d[:, t0:t0 + n], writes=[ROPE])
                    P.dma("sp", rope[:, 1, 0:n], ropeS_d[:, t0:t0 + n], writes=[ROPE])
            for j in range(4):
                pq, PQ = next_ps()
                for kc in range(8):
                    mm(pq[:, 0:n], wq[:, kc, j * 128:(j + 1) * 128] if wq is not None else None, hh(kc), kc == 0, kc == 7, [WQ, cur.HB], [PQ])
                if lat:
                    pqp, PQP = next_ps()
                    for kc in range(8):
                        mm(pqp[:, 0:n], wqp[:, kc, j * 128:(j + 1) * 128] if wqp is not None else None, hh(kc), kc == 0, kc == 7, [WQP, cur.HB], [PQP])
                    if j % 2 == 0:
                        ta, TA, tb, TBb = tmp[0], TMP[0], tmp[1], TMP[1]
                    else:
                        ta, TA, tb, TBb = sgB[0], SGB[0], sgB[1], SGB[1]
                    tt("dve", ta[:, 0:n], pq[:, 0:n], rope[:, 0, 0:n], ALU.mult, [PQ, ROPE], [TA])
                    tt("dve", tb[:, 0:n], pqp[:, 0:n], rope[:, 1, 0:n], ALU.mult, [PQP, ROPE], [TBb])
                    tt("pool", qrot[:, j, 0:n], ta[:, 0:n], tb[:, 0:n], ALU.add, [TA, TBb], [QR])
                else:
                    actf(qrot[:, j, 0:n], pq[:, 0:n], AF.Copy, [PQ], [QR])
            U, UB = (UL, ULB) if lat else (UC, UCB)
            L_ = n + 16
            P.dma("sp", uloc[:, :, 0:L_], U.rearrange("(c p) t -> p c t", p=128)[:, :, t0:t0 + L_], reads=[UB], writes=[ULOC])
            tt("pool", s1[:, :, 1:L_], uloc[:, :, 1:L_], uloc[:, :, 0:L_ - 1], ALU.add, [ULOC], [S1])
            tt("pool", s2[:, 3:L_], s1[:, 1, 3:L_], s1[:, 1, 1:L_ - 2], ALU.add, [S1], [S2])
            tt("pool", s4[64:128, 7:L_], s2[64:128, 7:L_], s2[64:128, 3:L_ - 4], ALU.add, [S2], [S4])
            cp("pool", win[0:64, 0, 0:n], s1[0:64, 0, 8:8 + n], [S1], [WIN])
            tt("pool", win[64:128, 0, 0:n], s1[64:128, 0, 7:7 + n], s1[64:128, 0, 9:9 + n], ALU.add, [S1], [WIN])
            tt("pool", win[0:64, 1, 0:n], s2[0:64, 7:7 + n], s2[0:64, 11:11 + n], ALU.add, [S2], [WIN])
            tt("pool", win[64:128, 1, 0:n], s4[64:128, 7:7 + n], s4[64:128, 15:15 + n], ALU.add, [S4], [WIN])
            for c in range(2):
                stt("dve", pooled[:, c, 0:n], win[:, c, 0:n], invw[:, c:c + 1], uloc[:, c, 8:8 + n], ALU.mult, ALU.subtract,
                    [WIN, CONSTB, ULOC], [PLD])
                if t0 == 0:
                    tt("pool", s2[:, 0:8], win[:, c, 0:8], rcnt[:, c * 16:c * 16 + 8], ALU.mult, [WIN, CONSTB, S2], [S2])
                    tt("pool", pooled[:, c, 0:8], s2[:, 0:8], uloc[:, c, 8:16], ALU.subtract, [S2, ULOC], [PLD])
                if t0 + n == Tn:
                    tt("pool", s2[:, 0:8], win[:, c, n - 8:n], rcnt[:, c * 16 + 8:c * 16 + 16], ALU.mult, [WIN, CONSTB, S2], [S2])
                    tt("pool", pooled[:, c, n - 8:n], s2[:, 0:8], uloc[:, c, n:n + 8], ALU.subtract, [S2, ULOC], [PLD])

            def pool_mm():
                for c in range(2):
                    pp, PP = next_ps()
                    mm(pp[:, 0:n], pwbd[:, l * 2 + c, :], pooled[:, c, 0:n], True, True, [CONSTB, PLD], [PP])
                    actf(ypool[:, c, 0:n], pp[:, 0:n], AF.Identity, [PP, CONSTB], [YP], scale=pscale[:, l * 2 + c:l * 2 + c + 1])
            ntt = Tn // 128
            ct_, st_ = (ctab_d, stab_d) if lat else (ctabc_d, stabc_d)
            Fsrc, FBUF = (Ftok, FB) if lat else (Fc, FCB)
            pg = [(ps[b_], PS[b_]) for b_ in range(4)]
            nblk = max(1, ntt // 8)
            per = ntt // nblk
            fops = []
            fpos = [0]
            for bi in range(nblk):
                for ti, tab in enumerate((ct_, st_)):
                    holder = {}
                    for fc in range(2):
                        for k in range(per):
                            def op_(bi=bi, ti=ti, tab=tab, fc=fc, k=k, holder=holder):
                                if "tv" not in holder:
                                    if lat:
                                        tsrc = tab[t0 // TT, bi].rearrange("p (k n) -> p k n", k=per)
                                    else:
                                        tsrc = tab.rearrange("(k p) t -> p k t", p=128)[:, bi * per:(bi + 1) * per, t0:t0 + n]
                                    holder["tv"], holder["TB"] = W.req(tsrc, 128, per, n, "sp")
                                tv, TB = holder["tv"], holder["TB"]
                                tti = bi * per + k
                                pgt, PGT = pg[ti * 2 + fc]
                                mm(pgt[:, 0:n], Fsrc[:, tti, fc * 128:(fc + 1) * 128], tv[:, k, :] if tv is not None else None,
                                   tti == 0, tti == ntt - 1, [FBUF, TB], [PGT])
                            fops.append(op_)
            FG = 4 if lat else 1

            def f_emit(cnt):
                for _ in range(cnt):
                    if fpos[0] < len(fops):
                        fops[fpos[0]]()
                        fpos[0] += 1

            sbank = [0]
            nqb = n // 128

            def att_front(qb, hk):
                gq = (t0 // 128) + qb
                pr = slice(hk * 64, (hk + 1) * 64)
                chunks = []
                if lat:
                    for d_, mi in ((-1, 0), (0, None), (1, 1)):
                        kb = gq + d_
                        if 0 <= kb < T // 128:
                            chunks.append((kT[pr, kb * 128:(kb + 1) * 128], V[:, kb, hk * 64:(hk + 1) * 64], mi, [KT, VB]))
                for cb in range(2):
                    chunks.append((kcT[pr, cb * 128:(cb + 1) * 128], Vc[:, cb, hk * 64:(hk + 1) * 64], None, [KCT, VCB]))
                rhs_q = qrot[pr, :, qb * 128:(qb + 1) * 128]
                if ptc[0] % 2 == 0:
                    PTv = [PT[:, ci_, :] for ci_ in range(5)]
                    PTb = PTB
                else:
                    PTv = [gT[:, 0, :], gT[:, 1, :], gT[:, 2, :], gT[:, 3, :], yfour[:, 0, :]]
                    PTb = PTB2
                ptc[0] += 1
                for ci, (kap, vap, mi, bb) in enumerate(chunks):
                    bsel = 4 + sbank[0] % 2
                    sbank[0] += 1
                    pS, PSb = ps[bsel], PS[bsel]
                    mm(pS[:, :].rearrange("p (g q) -> p g q", g=4), kap, rhs_q, True, True, [bb[0], QR], [PSb])
                    actf(PTv[ci], pS[:, :], AF.Exp, [PSb], [PTb[ci]], scale=0.125)
                    if mi is not None:
                        tt("pool", PTv[ci], PTv[ci], masks[:, mi * 512:(mi + 1) * 512], ALU.mult, [PTb[ci], CONSTB], [PTb[ci]])
                    if ci >= 1:
                        f_emit(FG)
                return chunks, PTv, PTb

            def att_back(qb, hk, st):
                chunks, PTv, PTb = st
                po, PO = ps[6], PS[6]
                pd, PD = ps[7], PS[7]
                nc_ = len(chunks)
                for ci, (kap, vap, mi, bb) in enumerate(chunks):
                    mm(po[0:64, :], vap, PTv[ci], ci == 0, ci == nc_ - 1, [bb[1], PTb[ci]], [PO])
                for ci in range(nc_):
                    mm(pd[0:64, :], ones[:, 0:64], PTv[ci], ci == 0, False, [ONES, PTb[ci]], [PD])
                er = l * 2 + hk
                mm(pd[0:64, :], sel[0:4, er * 64:(er + 1) * 64], esrow[0:4, :], False, True, [CONSTB], [PD])
                P.op("dve", lambda e, pd=pd: e.reciprocal(out=rden[:, :], in_=pd[0:64, :]), [PD], [RDEN])
                tt("dve", oT[:, hk * 4:(hk + 1) * 4, qb * 128:(qb + 1) * 128], po[0:64, :].rearrange("p (g q) -> p g q", g=4),
                   rden[:, :].rearrange("p (g q) -> p g q", g=4), ALU.mult, [PO, RDEN], [OT])

            prev_att = None
            for qb in range(nqb):
                for hk in range(2):
                    st = att_front(qb, hk)
                    if prev_att is not None:
                        att_back(*prev_att)
                    prev_att = (qb, hk, st)
            att_back(*prev_att)
            if dd:
                dump("kT", kT[:, :], [128, T], [KT])
                dump("V", V[:, :, :], [128, T // 128, 128], [VB])
                dump("Ftok", Ftok[:, :, :], [128, T // 128, 256], [FB])
                dump("qrot", qrot[:, :, :], [128, 4, TT], [QR])
                dump("oT", oT[:, :, :], [64, 8, TT], [OT])
            if dd:
                dump("uloc", uloc[:, :, :], [128, 2, TT + 16], [ULOC])
                dump("pooled", pooled[:, :, :], [128, 2, TT], [PLD])
                dump("ypool", ypool[:, :, :], [128, 2, TT], [YP])
            for op_ in fops[fpos[0]:]:
                op_()
            fpos[0] = len(fops)
            for i in range(4):
                pgt, PGT = pg[i]
                if i % 2 == 0:
                    actf(gT[:, i, 0:n], pgt[:, 0:n], AF.Copy, [PGT], [GT] + PTB2)
                else:
                    cp("dve", gT[:, i, 0:n], pgt[:, 0:n], [PGT], [GT] + PTB2)
            pool_mm()
            for fc in range(2):
                pf, PF = next_ps()
                mm(pf[:, 0:n], cs64[:, 0:128], gT[:, fc, 0:n], True, False, [CONSTB, GT], [PF])
                mm(pf[:, 0:n], cs64[:, 128:256], gT[:, 2 + fc, 0:n], False, True, [CONSTB, GT], [PF])
                actf(yfour[:, fc, 0:n], pf[:, 0:n], AF.Copy, [PF], [YF] + PTB2)
            if dd:
                dump("yfour", yfour[:, :, :], [128, 2, TT], [YF])
            if hook is not None:
                hook()
            for oc in range(8):
                wc, WC = W.req(("br", l, oc), 128, 1, 1536, key=("br", l, oc))
                wg, WG = W.req(win_[:, :, 1920 + oc * 384:1920 + (oc + 1) * 384], 128, 8, 384, key=("g", l, oc))
                if wc is not None:
                    wba = wc[0:64, 0, 0:1024].rearrange("p (h n) -> p h n", h=8)
                    wpf = wc[:, 0, 1024:1536].rearrange("p (c n) -> p c n", c=4)
                else:
                    wba = wpf = None
                pa, PA = next_ps()
                for hd in range(8):
                    mm(pa[:, 0:n], wba[:, hd, :] if wba is not None else None, oT[:, hd, 0:n], hd == 0, hd == 7, [WC, OT], [PA])
                pb, PB = next_ps()
                for c in range(2):
                    mm(pb[:, 0:n], wpf[:, c, :] if wpf is not None else None, ypool[:, c, 0:n], c == 0, c == 1, [WC, YP], [PB])
                pc, PC = next_ps()
                for c in range(2):
                    mm(pc[:, 0:n], wpf[:, 2 + c, :] if wpf is not None else None, yfour[:, c, 0:n], c == 0, c == 1, [WC, YF], [PC])
                brs = [(pa, PA), (pb, PB), (pc, PC)]
                for b in range(3):
                    pgt, PGT = next_ps()
                    for kc in range(8):
                        mm(pgt[:, 0:n], wg[:, kc, b * 128:(b + 1) * 128] if wg is not None else None, hh(kc), kc == 0, kc == 7, [WG, cur.HB], [PGT])
                    actf(sg[b][:, 0:n], pgt[:, 0:n], AF.Sigmoid, [PGT], [SG[b]])
                    tt("dve", sg[b][:, 0:n], sg[b][:, 0:n], brs[b][0][:, 0:n], ALU.mult, [SG[b], brs[b][1]], [SG[b]])
                tt("pool", sg[0][:, 0:n], sg[0][:, 0:n], sg[1][:, 0:n], ALU.add, [SG[0], SG[1]], [SG[0]])
                tt("pool", y[:, oc, 0:n], sg[0][:, 0:n], sg[2][:, 0:n], ALU.add, [SG[0], SG[2]], [YB])
            if dd:
                dump("y", y[:, :, :], [128, 8, TT], [YB])
            for half in range(2):
                wo, WO = W.req(wout_d[l].rearrange("(k p) n -> p k n", p=128)[:, :, half * 512:(half + 1) * 512], 128, 8, 512, key=("wo", l, half))
                for j in range(4):
                    oc = half * 4 + j
                    pz, PZ = next_ps()
                    for kc in range(8):
                        mm(pz[:, 0:n], wo[:, kc, j * 128:(j + 1) * 128] if wo is not None else None, y[:, kc, 0:n], kc == 0, kc == 7, [WO, YB], [PZ])
                    stt("dve", xv(oc, 0, n), pz[:, 0:n], modv(2, oc, s), xv(oc, 0, n), ALU.mult, ALU.add, [PZ, MODS[LCUR[0]], XBUF], [XBUF])

        def p3_tile(l, S, t0, n, xv, XBUF, zero_cols, do_norm=True, hook=None):
            lat = S == "L"
            s = 0 if lat else 1
            hv = lambda kc, c0, m: cur.h[:, kc, c0:c0 + m]
            if n == 512:
                ranges = [(0, 512), (510, 4)]
            else:
                ranges = [(0, n + 2)]
            if do_norm:
                norm(xv, XBUF, [(0, min(512, n + 2))] + ([(512, 2)] if n == 512 else []), gmB(), 3, s, hv, cur.HB, zero_cols=zero_cols)
            wup_ = wup_d[l].rearrange("(k p) n -> p k n", p=128)
            pend = [None]
            for jg in range(0, NFF, 4):
                nj = min(4, NFF - jg)
                wv_, WVb = W.req(wup_[:, :, jg * 128:(jg + nj) * 128], 128, 8, nj * 128, key=("wv", l, jg))
                wg_, WGb = W.req(wup_[:, :, DFF + jg * 128:DFF + (jg + nj) * 128], 128, 8, nj * 128, key=("wg", l, jg))
                for jj in range(nj):
                    j = jg + jj
                    for (c0, m) in ranges:
                        pvv, PVV = next_ps()
                        pgg, PGG = next_ps()
                        for kc in range(8):
                            mm(pvv[:, 0:m], wv_[:, kc, jj * 128:(jj + 1) * 128] if wv_ is not None else None, cur.h[:, kc, c0:c0 + m], kc == 0, kc == 7, [WVb, cur.HB], [PVV])
                        for kc in range(8):
                            mm(pgg[:, 0:m], wg_[:, kc, jj * 128:(jj + 1) * 128] if wg_ is not None else None, cur.h[:, kc, c0:c0 + m], kc == 0, kc == 7, [WGb, cur.HB], [PGG])
                        mo = m - 2
                        (cva, cvg, cvs), (CVA, CVG, CVS) = (cvsets if m > 8 else tvsets)[j % 2]
                        for (pp_, PPb, ch, dst, DST) in ((pvv, PVV, j, cva, CVA), (pgg, PGG, NFF + j, cvg, CVG)):
                            cw = lambda k, ch=ch: convw[:, (l * 3 + k) * 44 + ch:(l * 3 + k) * 44 + ch + 1]
                            actf(dst[:, 0:mo], pp_[:, 1:1 + mo], AF.Identity, [PPb, CONSTB], [DST], scale=cw(1))
                            stt("dve", dst[:, 0:mo], pp_[:, 0:mo], cw(0), dst[:, 0:mo], ALU.mult, ALU.add, [PPb, CONSTB, DST], [DST])
                            stt("dve", dst[:, 0:mo], pp_[:, 2:2 + mo], cw(2), dst[:, 0:mo], ALU.mult, ALU.add, [PPb, CONSTB, DST], [DST])
                        def fin_(cva=cva, cvg=cvg, cvs=cvs, CVA=CVA, CVG=CVG, CVS=CVS, j=j, c0=c0, mo=mo):
                            actf(cvs[:, 0:mo], cvg[:, 0:mo], AF.Silu, [CVG], [CVS])
                            tt("pool", act[:, j, c0:c0 + mo], cva[:, 0:mo], cvs[:, 0:mo], ALU.mult, [CVA, CVS], [ACTB])
                        if pend[0] is not None:
                            pend[0]()
                        pend[0] = fin_
            if pend[0] is not None:
                pend[0]()
                pend[0] = None
            if hook is not None:
                hook()
            wdn_ = wdn_d[l].rearrange("(k p) n -> p k n", p=128)
            for oc in range(8):
                wd, WD = W.req(wdn_[:, :, oc * 128:(oc + 1) * 128], 128, NFF, 128, key=("wd", l, oc))
                pz, PZ = next_ps()
                for j in range(NFF):
                    mm(pz[:, 0:n], wd[:, j, :] if wd is not None else None, act[:, j, 0:n], j == 0, j == NFF - 1, [WD, ACTB], [PZ])
                stt("dve", xv(oc, 1, n), pz[:, 0:n], modv(5, oc, s), xv(oc, 1, n), ALU.mult, ALU.add, [PZ, MODS[LCUR[0]], XBUF], [XBUF])

        _orig_req = W.req

        def req2(src, npart, kc, ncol, qn="pool", key=None):
            return _orig_req(src, npart, kc, ncol, qn, key)

        _orig_dma = P.dma

        def dma2(qn, out, in_, reads=(), writes=(), acc=False):
            if isinstance(in_, tuple):
                _, l_, oc_ = in_
                cs = slice(oc_ * 128, (oc_ + 1) * 128)
                _orig_dma(qn, out[0:64, 0, 0:1024].rearrange("p (h n) -> p h n", h=8),
                          wba_d[l_].rearrange("(h p) n -> p h n", p=64)[:, :, cs], reads, writes)
                _orig_dma(qn, out[:, 0, 1024:1280].rearrange("p (c n) -> p c n", c=2),
                          wbp_d[l_].rearrange("(c p) n -> p c n", p=128)[:, :, cs], reads, writes, acc=True)
                return _orig_dma(qn, out[:, 0, 1280:1536].rearrange("p (c n) -> p c n", c=2),
                                 wbf_d[l_].rearrange("(c p) n -> p c n", p=128)[:, :, cs], reads, writes, acc=True)
            return _orig_dma(qn, out, in_, reads, writes, acc)

        W.req = req2
        P.dma = dma2

        xcv = lambda kc, c0, m: xc[:, kc, 1 + c0:1 + c0 + m]
        xcv3 = lambda kc, c0, m: xc[:, kc, c0:c0 + m]
        xtv = lambda kc, c0, m: cur.xt[:, kc, c0:c0 + m]

        def program():
            last_out = []
            for l in range(depth_run):
                last = l == DEPTH - 1
                Xin, XIN = (fm(xT), XB["x0"]) if l == 0 else (fm(X1), XB["x1"])
                LCUR[0] = l
                if l == 0:
                    for l2 in range(depth_run):
                        mod_phase(l2)
                hv0 = lambda kc, c0, m: cur.h[:, kc, c0:c0 + m]

                def p12_prep(i):
                    setcur(i)
                    P.dma("sp", cur.xt[:, :, 0:TT], Xin[:, :, i * TT:(i + 1) * TT], reads=[XIN[i]], writes=[cur.XT])
                    norm(xtv, cur.XT, [(0, TT)], gmA(), 0, 0, hv0, cur.HB)

                def mk_hook(prep, i):
                    def hook():
                        if i + 1 < NTL:
                            prep(i + 1)
                            setcur(i)
                    return hook

                fuse_p1 = (not last) and (l + 1 < depth_run)
                if l == 0:
                    setcur(1)
                    p1_tile(l, "C", 0, CT, xcv, XC, 0)
                    p12_prep(0)
                    for i in range(NTL):
                        setcur(i)
                        p1_tile(l, "L", i * TT, TT, xtv, cur.XT, 0, do_norm=False, hook=mk_hook(p12_prep, i))
                fence()

                def p2_prep(i):
                    p12_prep(i)
                    P.dma("pool", rope[:, 0, 0:TT], ropeC_d[:, i * TT:(i + 1) * TT], writes=[ROPE])
                    P.dma("pool", rope[:, 1, 0:TT], ropeS_d[:, i * TT:(i + 1) * TT], writes=[ROPE])

                p2_prep(0)
                for i in range(NTL):
                    setcur(i)
                    p2_tile(l, "L", i * TT, TT, xtv, cur.XT, 0, do_norm=False, hook=mk_hook(p2_prep, i), rope_loaded=True)
                    P.dma("sp", fm(XM)[:, :, i * TT:(i + 1) * TT], cur.xt[:, :, 0:TT], reads=[cur.XT], writes=[XB["xm"][i]])
                if not last:
                    setcur(NTL)
                    p2_tile(l, "C", 0, CT, xcv, XC, 0)
                    if l == 0:
                        dump("xcmid", xc[:, :, :], [128, 8, CT + 2], [XC])
                fence()

                def p3_prep(i):
                    setcur(i)
                    lo = i * TT - 1
                    zc = []
                    if i == 0:
                        P.dma("sp", cur.xt[:, :, 1:TT + 2], fm(XM)[:, :, 0:TT + 1], reads=[XB["xm"][0], XB["xm"][1]], writes=[cur.XT])
                        mset("pool", cur.xt[:, :, 0:1], 0.0, [cur.XT])
                        zc = [0]
                    elif i == NTL - 1:
                        P.dma("sp", cur.xt[:, :, 0:TT + 1], fm(XM)[:, :, lo:lo + TT + 1], reads=[XB["xm"][i - 1], XB["xm"][i]], writes=[cur.XT])
                        mset("pool", cur.xt[:, :, TT + 1:TT + 2], 0.0, [cur.XT])
                        zc = [TT + 1]
                    else:
                        P.dma("sp", cur.xt[:, :, 0:TT + 2], fm(XM)[:, :, lo:lo + TT + 2],
                              reads=[XB["xm"][i - 1], XB["xm"][i], XB["xm"][i + 1]], writes=[cur.XT])
                    norm(xtv, cur.XT, [(0, 512), (512, 2)], gmB(), 3, 0, hv0, cur.HB, zero_cols=zc)

                if not last:
                    setcur(NTL)
                    p3_tile(l, "C", 0, CT, xcv3, XC, [0, CT + 1])
                    if fuse_p1:
                        LCUR[0] = l + 1
                        setcur(NTL + 1)
                        p1_tile(l + 1, "C", 0, CT, xcv, XC, 0)
                        LCUR[0] = l
                p3_prep(0)
                for i in range(NTL):
                    setcur(i)
                    p3_tile(l, "L", i * TT, TT, xtv, cur.XT, [], do_norm=False, hook=mk_hook(p3_prep, i))
                    if last or depth_run == 1 and l == depth_run - 1:
                        if last:
                            xo = lambda kc, c0, m: cur.xt[:, kc, 1 + c0:1 + c0 + m]
                            norm(xo, cur.XT, [(0, TT)], nfin, None, 0, xo, cur.XT)
                            ev = P.dma("sp", fm(outT)[:, :, i * TT:(i + 1) * TT], cur.xt[:, :, 1:TT + 1], reads=[cur.XT], writes=[XB["out"][i]])
                        else:
                            ev = P.dma("sp", fm(outT)[:, :, i * TT:(i + 1) * TT], cur.xt[:, :, 1:TT + 1], reads=[cur.XT], writes=[XB["out"][i]])
                        last_out.append(ev)
                    else:
                        P.dma("sp", fm(X1)[:, :, i * TT:(i + 1) * TT], cur.xt[:, :, 1:TT + 1], reads=[cur.XT], writes=[XB["x1"][i]])
                        if fuse_p1:
                            LCUR[0] = l + 1
                            xo1 = lambda kc, c0, m: cur.xt[:, kc, 1 + c0:1 + c0 + m]
                            p1_tile(l + 1, "L", i * TT, TT, xo1, cur.XT, 0)
                            LCUR[0] = l
                if l == 0:
                    dump("xmid", fm(XM), [128, 8, T], XB["xm"])
                    dump("xc", xc[:, :, :], [128, 8, CT + 2], [XC])
            return last_out

        P.planning = True
        program()
        P.planning = False
        pspos[0] = 0
        setup()
        outs = program()
        for ev in outs:
            if ev is not None:
                P.final_wait("sp", ev)
        P.emit()
        print(f"[build] insts={P.n_inst} waits={P.n_wait} sems={P.nsem} wblocks={len(W.plan)}")
    if dbg:
        return nc, list(dumps.keys())
    return nc


def _host_consts():
    import ml_dtypes
    bf = ml_dtypes.bfloat16
    c = {}
    t = np.arange(T)
    row = (t // 64).astype(np.float32)
    col = (t % 64).astype(np.float32)
    inv_freq = (np.float32(10000.0) ** (-np.arange(16, dtype=np.float32) / np.float32(16))).astype(np.float32)
    C = np.zeros((128, T), np.float32)
    S = np.zeros((128, T), np.float32)
    for p in range(128):
        d = p % 64
        a, r, f = d // 32, (d % 32) // 16, d % 16
        pos = row if a == 0 else col
        ang = (pos * inv_freq[f]).astype(np.float32)
        C[p] = np.cos(ang)
        S[p] = np.sin(ang) * (-1.0 if r == 0 else 1.0)
    c["ropeC"], c["ropeS"] = C, S
    j = np.arange(128)[:, None]
    i = np.arange(128)[None, :]
    mp = (j >= i).astype(np.float32)
    mn = (j <= i).astype(np.float32)
    c["masks"] = np.concatenate([np.tile(mp, (1, 4)), np.tile(mn, (1, 4))], axis=1).astype(np.float32)
    for nm, N in (("", T), ("c", CT)):
        tt_ = np.arange(N, dtype=np.int64)
        k = (tt_[:, None] * tt_[None, :]) % N
        ang = 2.0 * np.pi * k.astype(np.float64) / N
        sc = 1.0 / np.sqrt(N * 64.0)
        ct = (np.cos(ang) * sc).astype(np.float32).astype(bf)
        st = (-np.sin(ang) * sc).astype(np.float32).astype(bf)
        if N == T:
            lay = lambda a: np.ascontiguousarray(a.reshape(4, 8, 128, T // TT, TT).transpose(3, 0, 2, 1, 4)).reshape(T // TT, 4, 128, 4096)
            ct, st = lay(ct), lay(st)
        c["ctab" + nm] = ct
        c["stab" + nm] = st
    cc = np.arange(64, dtype=np.int64)
    k = (cc[:, None] * cc[None, :]) % 64
    ang = 2.0 * np.pi * k.astype(np.float64) / 64
    C64, S64 = np.cos(ang), np.sin(ang)
    cs = np.zeros((128, 256), np.float32)
    for g in range(2):
        cs[g * 64:(g + 1) * 64, g * 64:(g + 1) * 64] = C64
        cs[g * 64:(g + 1) * 64, 128 + g * 64:128 + (g + 1) * 64] = S64
    c["cs64"] = cs
    wins = (2, 4, 8, 16)
    invw = np.zeros((128, 2), np.float32)
    rc = np.ones((128, 32), np.float32)
    for g, w in enumerate(wins):
        ch, half = g // 2, g % 2
        pr = slice(half * 64, (half + 1) * 64)
        invw[pr, ch] = 1.0 / w
        for e in range(8):
            cnt_first = min(e - w // 2 + w, 10 ** 9) - max(e - w // 2, 0)
            rc[pr, ch * 16 + e] = 1.0 / cnt_first
            cnt_last = min(8 - e + w // 2, w)
            rc[pr, ch * 16 + 8 + e] = 1.0 / cnt_last
    c["invw"], c["rcnt"] = invw, rc
    return c


_CONSTS = None
_NC = None
_DBG_HOOK = None


def _fmv(v):
    return np.ascontiguousarray(v.reshape(-1, 128).T)


def kernel(x, c, ctx, c_ctx, w_mod, b_mod, norm_mix, norm_ffn, w_in, attn_sink, pool_w, pool_scale,
           w_br_attn, w_br_pool, w_br_four, w_out, w_up, conv_w, w_down, norm_final):
    global _CONSTS, _NC
    f32 = np.float32
    A = lambda a: np.ascontiguousarray(np.asarray(a, dtype=f32))
    x, c, ctx, c_ctx = A(x), A(c), A(ctx), A(c_ctx)
    w_mod, b_mod, norm_mix, norm_ffn, w_in = A(w_mod), A(b_mod), A(norm_mix), A(norm_ffn), A(w_in)
    attn_sink, pool_w, pool_scale = A(attn_sink), A(pool_w), A(pool_scale)
    w_br_attn, w_br_pool, w_br_four, w_out = A(w_br_attn), A(w_br_pool), A(w_br_four), A(w_out)
    w_up, conv_w, w_down, norm_final = A(w_up), A(conv_w), A(w_down), A(norm_final)
    if _CONSTS is None:
        _CONSTS = _host_consts()
    if _NC is None:
        _NC = build_program()
    K = _CONSTS
    d = np.arange(64)
    swap = (d // 32) * 32 + (1 - (d % 32) // 16) * 16 + d % 16
    k_cols = np.arange(0, 128)
    kp_cols = np.concatenate([hh * 64 + swap for hh in range(2)])
    v_cols = np.arange(128, 256)
    u_cols = np.arange(768, 1024)
    f_cols = np.arange(1024, 1280)
    q_cols = np.concatenate([np.concatenate([256 + j * 64 + d, 256 + (4 + j) * 64 + d]) for j in range(4)])
    qp_cols = np.concatenate([np.concatenate([256 + j * 64 + swap, 256 + (4 + j) * 64 + swap]) for j in range(4)])
    g_cols = np.concatenate([np.concatenate([1280 + b * 1024 + oc * 128 + np.arange(128) for b in range(3)]) for oc in range(8)])
    cols = np.concatenate([k_cols, kp_cols, v_cols, u_cols, f_cols, q_cols, qp_cols, g_cols])
    assert cols.shape[0] == INW2
    w_in2 = np.ascontiguousarray(w_in[:, :, cols])
    bmod = np.concatenate([np.repeat(_fmv(b_mod[l])[:, :, None], 2, axis=2).reshape(128, 96) for l in range(DEPTH)], axis=1)
    nmix = np.concatenate([np.repeat(_fmv(norm_mix[l])[:, :, None], 2, axis=2).reshape(128, 16) for l in range(DEPTH)], axis=1)
    nffn = np.concatenate([np.repeat(_fmv(norm_ffn[l])[:, :, None], 2, axis=2).reshape(128, 16) for l in range(DEPTH)], axis=1)
    nfin = _fmv(norm_final)
    pscale = np.concatenate([_fmv(pool_scale[l]) for l in range(DEPTH)], axis=1)
    convw = np.concatenate([_fmv(conv_w[l, k]) for l in range(DEPTH) for k in range(3)], axis=1)
    sinkx = np.zeros((DEPTH * 2, 512), f32)
    for l in range(DEPTH):
        for hk in range(2):
            sinkx[l * 2 + hk] = np.repeat(attn_sink[l, hk * 4:(hk + 1) * 4], 128)
    pwbd = np.zeros((DEPTH * 2, 128, 128), f32)
    for l in range(DEPTH):
        for g in range(4):
            ch, half = g // 2, g % 2
            pwbd[l * 2 + ch, half * 64:(half + 1) * 64, half * 64:(half + 1) * 64] = pool_w[l, g]
    selm = np.zeros((4, 256), f32)
    for r in range(4):
        selm[r, r * 64:(r + 1) * 64] = 1.0
    shared = {
        "sel": selm,
        "w_mod": w_mod, "bmod": np.ascontiguousarray(bmod), "nmix": np.ascontiguousarray(nmix),
        "nffn": np.ascontiguousarray(nffn), "nfin": nfin, "w_in2": w_in2, "sinkx": sinkx, "pwbd": pwbd,
        "pscale": np.ascontiguousarray(pscale), "w_br_attn": w_br_attn, "w_br_pool": w_br_pool,
        "w_br_four": w_br_four, "w_out": w_out, "w_up": w_up, "convw": np.ascontiguousarray(convw),
        "w_down": w_down, "ropeC": K["ropeC"], "ropeS": K["ropeS"], "masks": K["masks"],
        "ctab": K["ctab"], "stab": K["stab"], "ctabc": K["ctabc"], "stabc": K["stabc"], "cs64": K["cs64"],
        "rcnt": K["rcnt"], "invw": K["invw"],
    }
    in_maps = []
    for b in range(NCORES):
        cv = np.stack([_fmv(c[b]), _fmv(c_ctx)], axis=2).reshape(128, 16)
        m = dict(shared)
        m["xT"] = np.ascontiguousarray(x[b].T)
        m["ctxT"] = np.ascontiguousarray(ctx[b].T)
        m["cvec"] = np.ascontiguousarray(cv)
        in_maps.append(m)
    if _DBG_HOOK is not None:
        return _DBG_HOOK(in_maps)
    res = run_bass_kernel_spmd(_NC, in_maps, core_ids=list(range(NCORES)))
    out = np.stack([np.ascontiguousarray(res.results[b]["outT"].T) for b in range(NCORES)], axis=0)
    return out.astype(np.float32)
```
